# Optimizing a Trainium2 kernel written in Bass

```python
import jax, jax.numpy as jnp
from jax import lax
import numpy as np

D_MODEL = 1024
BATCH = 8
SEQ = 2048
DEPTH = 4
DEC_BATCH = 128
DEC_SEQ = 8
PAST_LEN = 8192
PAGE_SIZE = 128

GDN_DK = 128
GDN_DV = 128
GDN_HEADS = D_MODEL // GDN_DV
GDN_KEY_DIM = GDN_HEADS * GDN_DK
GDN_VAL_DIM = GDN_HEADS * GDN_DV
CONV_WIDTH = 4
CONV_DIM = 2 * GDN_KEY_DIM + GDN_VAL_DIM
GDN_CHUNK = 64
SWA_HD = 64
SWA_HEADS = D_MODEL // SWA_HD
SWA_KV_HEADS = SWA_HEADS // 4
SWA_GROUP = SWA_HEADS // SWA_KV_HEADS
SWA_Q_DIM = SWA_HEADS * SWA_HD
SWA_KV_DIM = SWA_KV_HEADS * SWA_HD
WINDOW = 128
D_FF = 4 * D_MODEL
PLE_DIM = 256
EPS = 1e-6
IN_DIM = CONV_DIM + GDN_VAL_DIM + 2 * GDN_HEADS + SWA_Q_DIM + 2 * SWA_KV_DIM + 2 * D_MODEL

kernel_name = 'hybrid_gdn_swa_decoder_step'


def rmsnorm(x, w):
    xf = x.astype(jnp.float32)
    xf = xf * lax.rsqrt(jnp.mean(xf * xf, axis=-1, keepdims=True) + EPS)
    return xf.astype(x.dtype) * w


def l2norm(x):
    xf = x.astype(jnp.float32)
    return (xf * lax.rsqrt(jnp.sum(xf * xf, axis=-1, keepdims=True) + EPS)).astype(x.dtype)


def causal_conv(u, buf, w):
    t = u.shape[1]
    full = jnp.concatenate([buf.astype(u.dtype), u], axis=1)
    out = full[:, 0:t] * w[0]
    for j in range(1, CONV_WIDTH):
        out = out + full[:, j:j + t] * w[j]
    return jax.nn.silu(out), full[:, t:]


def gated_delta_rule(q, k, v, g, beta, s0):
    b, t, h, dk = q.shape
    dv = v.shape[-1]
    c = min(GDN_CHUNK, t)
    n = -(-t // c)
    pad = n * c - t
    f32 = jnp.float32

    def blocks(a):
        a = a.astype(f32)
        a = jnp.pad(a, [(0, 0), (0, pad)] + [(0, 0)] * (a.ndim - 2))
        a = a.reshape((b, n, c) + a.shape[2:])
        return jnp.moveaxis(a, 2, 3)

    q, k, v, g, beta = blocks(q), blocks(k), blocks(v), blocks(g), blocks(beta)
    gc = jnp.cumsum(g, axis=-1)
    incl = jnp.tril(jnp.ones((c, c), dtype=bool))
    strict = jnp.tril(jnp.ones((c, c), dtype=bool), -1)
    decay = jnp.exp(jnp.where(incl, gc[..., :, None] - gc[..., None, :], -jnp.inf))
    kb = k * beta[..., None]
    a_mat = jnp.where(strict, jnp.einsum('bnhid,bnhjd->bnhij', kb, k) * decay, 0.0)
    rhs = jnp.concatenate([v * beta[..., None], kb * jnp.exp(gc)[..., None]], axis=-1)
    sol = lax.linalg.triangular_solve(a_mat, rhs, left_side=True, lower=True, unit_diagonal=True)
    u, w = sol[..., :dv], sol[..., dv:]
    qk = jnp.einsum('bnhid,bnhjd->bnhij', q, k) * decay

    def step(s, xs):
        q_c, k_c, u_c, w_c, qk_c, g_c = xs
        v_new = u_c - jnp.einsum('bhck,bhkv->bhcv', w_c, s)
        o_c = (jnp.einsum('bhck,bhkv->bhcv', q_c * jnp.exp(g_c)[..., None], s)
               + jnp.einsum('bhij,bhjv->bhiv', qk_c, v_new))
        g_last = g_c[..., -1:]
        s = s * jnp.exp(g_last)[..., None] + jnp.einsum(
            'bhck,bhcv->bhkv', k_c * jnp.exp(g_last - g_c)[..., None], v_new)
        return s, o_c

    xs = (jnp.moveaxis(q, 1, 0), jnp.moveaxis(k, 1, 0), jnp.moveaxis(u, 1, 0),
          jnp.moveaxis(w, 1, 0), jnp.moveaxis(qk, 1, 0), jnp.moveaxis(gc, 1, 0))
    s_final, o = lax.scan(step, s0.astype(f32), xs)
    o = jnp.moveaxis(o, 0, 1)
    o = jnp.moveaxis(o, 3, 2).reshape(b, n * c, h, dv)[:, :t]
    return o, s_final


def gdn_branch(qkv, z, b_logit, a_logit, conv_buf, s0, conv_w, a_log, dt_bias, gdn_norm):
    b, t, _ = qkv.shape
    if conv_buf is None:
        conv_buf = jnp.zeros((b, CONV_WIDTH - 1, CONV_DIM), qkv.dtype)
    if s0 is None:
        s0 = jnp.zeros((b, GDN_HEADS, GDN_DK, GDN_DV), jnp.float32)
    qkv_c, new_buf = causal_conv(qkv, conv_buf, conv_w)
    q = qkv_c[..., :GDN_KEY_DIM].reshape(b, t, GDN_HEADS, GDN_DK)
    k = qkv_c[..., GDN_KEY_DIM:2 * GDN_KEY_DIM].reshape(b, t, GDN_HEADS, GDN_DK)
    v = qkv_c[..., 2 * GDN_KEY_DIM:].reshape(b, t, GDN_HEADS, GDN_DV)
    q = l2norm(q) * (GDN_DK ** -0.5)
    k = l2norm(k)
    beta = jax.nn.sigmoid(b_logit.astype(jnp.float32))
    g = -jnp.exp(a_log.astype(jnp.float32)) * jax.nn.softplus(a_logit.astype(jnp.float32) + dt_bias.astype(jnp.float32))
    o, s_new = gated_delta_rule(q, k, v, g, beta, s0)
    o = rmsnorm(o.astype(v.dtype), gdn_norm) * jax.nn.silu(z.reshape(b, t, GDN_HEADS, GDN_DV))
    return o.reshape(b, t, GDN_VAL_DIM), new_buf, s_new


def alibi_slopes():
    hh = jnp.arange(1, SWA_HEADS + 1, dtype=jnp.float32)
    return jnp.exp2(-8.0 * hh / SWA_HEADS).reshape(SWA_KV_HEADS, SWA_GROUP)


def swa_attend(q, k, v, q_pos, k_pos, sinks, slopes):
    s = jnp.einsum('bnqkgd,bnskd->bnkgqs', q, k).astype(jnp.float32) * (SWA_HD ** -0.5)
    dist = q_pos[:, :, None] - k_pos[:, None, :]
    ok = (dist >= 0) & (dist <= WINDOW) & (k_pos[:, None, :] >= 0)
    s = s - slopes[:, :, None, None] * dist.astype(jnp.float32)[:, None, None]
    s = jnp.where(ok[:, None, None], s, -jnp.inf)
    sink = jnp.broadcast_to(sinks.astype(jnp.float32)[:, :, None, None], s.shape[:-1] + (1,))
    p = jax.nn.softmax(jnp.concatenate([s, sink], axis=-1), axis=-1)[..., :-1]
    return jnp.einsum('bnkgqs,bnskd->bnqkgd', p.astype(v.dtype), v)


def swa_branch(sq, sk, sv, q_norm, k_norm, sinks, cache_k, cache_v):
    b, t, _ = sq.shape
    q = rmsnorm(sq.reshape(b, t, SWA_KV_HEADS, SWA_GROUP, SWA_HD), q_norm)
    k = rmsnorm(sk.reshape(b, t, SWA_KV_HEADS, SWA_HD), k_norm)
    v = sv.reshape(b, t, SWA_KV_HEADS, SWA_HD)
    slopes = alibi_slopes()
    sinks = sinks.reshape(SWA_KV_HEADS, SWA_GROUP)
    if cache_k is None:
        nb = t // WINDOW
        qb = q.reshape(b, nb, WINDOW, SWA_KV_HEADS, SWA_GROUP, SWA_HD)
        zpad = jnp.zeros((b, WINDOW, SWA_KV_HEADS, SWA_HD), k.dtype)

        def band(a):
            prev = jnp.concatenate([zpad, a[:, :-WINDOW]], axis=1).reshape(b, nb, WINDOW, SWA_KV_HEADS, SWA_HD)
            return jnp.concatenate([prev, a.reshape(b, nb, WINDOW, SWA_KV_HEADS, SWA_HD)], axis=2)

        q_pos = jnp.arange(t, dtype=jnp.int32).reshape(nb, WINDOW)
        k_pos = (jnp.arange(nb, dtype=jnp.int32)[:, None] - 1) * WINDOW + jnp.arange(2 * WINDOW, dtype=jnp.int32)[None, :]
        o = swa_attend(qb, band(k), band(v), q_pos, k_pos, sinks, slopes)
        k_all, v_all = k, v
    else:
        k_all = jnp.concatenate([cache_k.astype(k.dtype), k], axis=1)
        v_all = jnp.concatenate([cache_v.astype(v.dtype), v], axis=1)
        q_pos = (PAST_LEN + jnp.arange(t, dtype=jnp.int32))[None, :]
        k_pos = (PAST_LEN - WINDOW + jnp.arange(WINDOW + t, dtype=jnp.int32))[None, :]
        o = swa_attend(q[:, None], k_all[:, None], v_all[:, None], q_pos, k_pos, sinks, slopes)
    return o.reshape(b, t, SWA_Q_DIM), k_all[:, -WINDOW:], v_all[:, -WINDOW:]


def trunk_layer(h, p_l, conv_buf, gdn_s0, swa_k_buf, swa_v_buf, norm_mix, w_in, conv_w, a_log, dt_bias,
                gdn_norm, q_norm, k_norm, sinks, w_out, norm_ffn, w_up, w_down, norm_ple, w_ple_gate, w_ple_proj):
    proj = rmsnorm(h, norm_mix) @ w_in
    cuts = [int(c) for c in np.cumsum([CONV_DIM, GDN_VAL_DIM, GDN_HEADS, GDN_HEADS,
                                       SWA_Q_DIM, SWA_KV_DIM, SWA_KV_DIM])]
    qkv, z, b_logit, a_logit, sq, sk, sv, gates = jnp.split(proj, cuts, axis=-1)
    o_a, conv_new, s_new = gdn_branch(qkv, z, b_logit, a_logit, conv_buf, gdn_s0, conv_w, a_log, dt_bias, gdn_norm)
    o_b, k_new, v_new = swa_branch(sq, sk, sv, q_norm, k_norm, sinks, swa_k_buf, swa_v_buf)
    gates = jax.nn.sigmoid(gates)
    mix = gates[..., :D_MODEL] * o_a + gates[..., D_MODEL:] * o_b
    h = h + mix @ w_out
    hid = jax.nn.relu(rmsnorm(h, norm_ffn) @ w_up)
    h = h + (hid * hid) @ w_down
    h = h + jax.nn.sigmoid(rmsnorm(h, norm_ple) @ w_ple_gate) * (p_l @ w_ple_proj)
    return h, conv_new, s_new, k_new, v_new


def setup_inputs(seed: int = 0) -> dict:
    key = jax.random.key(seed)
    ks = jax.random.split(key, 32)
    f32 = jnp.float32
    nrm = lambda k, shape, s: jax.random.normal(k, shape, f32) * s
    dt = jnp.exp(jax.random.uniform(ks[10], (DEPTH, GDN_HEADS), f32, np.log(1e-3), np.log(1e-1)))
    return {
        'x_prompt': nrm(ks[0], (BATCH, SEQ, D_MODEL), 1.0),
        'x_sample': nrm(ks[1], (DEC_BATCH, DEC_SEQ, D_MODEL), 1.0),
        'cache_conv': nrm(ks[2], (DEPTH, DEC_BATCH, CONV_WIDTH - 1, CONV_DIM), 1.0),
        'state_gdn': nrm(ks[3], (DEPTH, DEC_BATCH, GDN_HEADS, GDN_DK, GDN_DV), 0.1),
        'cache_swa_k': nrm(ks[4], (DEPTH, DEC_BATCH, WINDOW, SWA_KV_HEADS, SWA_HD), 1.0),
        'cache_swa_v': nrm(ks[5], (DEPTH, DEC_BATCH, WINDOW, SWA_KV_HEADS, SWA_HD), 1.0),
        'p_prompt': nrm(ks[6], (DEPTH, BATCH, SEQ, PLE_DIM), 1.0),
        'p_sample': nrm(ks[7], (DEPTH, DEC_BATCH, DEC_SEQ, PLE_DIM), 1.0),
        'norm_mix': 1.0 + nrm(ks[8], (DEPTH, D_MODEL), 0.1),
        'w_in': nrm(ks[9], (DEPTH, D_MODEL, IN_DIM), D_MODEL ** -0.5),
        'conv_w': nrm(ks[11], (DEPTH, CONV_WIDTH, CONV_DIM), 0.5),
        'a_log': jnp.log(jax.random.uniform(ks[12], (DEPTH, GDN_HEADS), f32, 1.0, 16.0)),
        'dt_bias': dt + jnp.log(-jnp.expm1(-dt)),
        'gdn_norm': 1.0 + nrm(ks[13], (DEPTH, GDN_DV), 0.1),
        'q_norm': 1.0 + nrm(ks[14], (DEPTH, SWA_HD), 0.1),
        'k_norm': 1.0 + nrm(ks[15], (DEPTH, SWA_HD), 0.1),
        'attn_sinks': nrm(ks[16], (DEPTH, SWA_HEADS), 0.5),
        'w_out': nrm(ks[17], (DEPTH, D_MODEL, D_MODEL), D_MODEL ** -0.5),
        'norm_ffn': 1.0 + nrm(ks[18], (DEPTH, D_MODEL), 0.1),
        'w_up': nrm(ks[19], (DEPTH, D_MODEL, D_FF), D_MODEL ** -0.5),
        'w_down': nrm(ks[20], (DEPTH, D_FF, D_MODEL), D_FF ** -0.5),
        'norm_ple': 1.0 + nrm(ks[21], (DEPTH, D_MODEL), 0.1),
        'w_ple_gate': nrm(ks[22], (DEPTH, D_MODEL, D_MODEL), D_MODEL ** -0.5),
        'w_ple_proj': nrm(ks[23], (DEPTH, PLE_DIM, D_MODEL), PLE_DIM ** -0.5),
    }


def reference(x_prompt, x_sample, cache_conv, state_gdn, cache_swa_k, cache_swa_v, p_prompt, p_sample,
              norm_mix, w_in, conv_w, a_log, dt_bias, gdn_norm, q_norm, k_norm, attn_sinks, w_out,
              norm_ffn, w_up, w_down, norm_ple, w_ple_gate, w_ple_proj):
    hp, hs = x_prompt, x_sample
    conv_p, gdn_p, kp_l, vp_l = [], [], [], []
    conv_s, gdn_s, ks_l, vs_l = [], [], [], []
    for i in range(DEPTH):
        wts = (norm_mix[i], w_in[i], conv_w[i], a_log[i], dt_bias[i], gdn_norm[i], q_norm[i], k_norm[i],
               attn_sinks[i], w_out[i], norm_ffn[i], w_up[i], w_down[i], norm_ple[i], w_ple_gate[i], w_ple_proj[i])
        hp, c1, s1, k1, v1 = trunk_layer(hp, p_prompt[i], None, None, None, None, *wts)
        hs, c2, s2, k2, v2 = trunk_layer(hs, p_sample[i], cache_conv[i], state_gdn[i],
                                         cache_swa_k[i], cache_swa_v[i], *wts)
        conv_p.append(c1); gdn_p.append(s1); kp_l.append(k1); vp_l.append(v1)
        conv_s.append(c2); gdn_s.append(s2); ks_l.append(k2); vs_l.append(v2)
    return (hp, hs,
            jnp.stack(conv_p), jnp.stack(gdn_p), jnp.stack(kp_l), jnp.stack(vp_l),
            jnp.stack(conv_s), jnp.stack(gdn_s), jnp.stack(ks_l), jnp.stack(vs_l))
```

```python
import contextlib
import numpy as np
import concourse.bass as bass
import concourse.mybir as mybir
from concourse.bass_utils import run_bass_kernel_spmd

F32 = mybir.dt.float32
BF16 = mybir.dt.bfloat16
AF = mybir.ActivationFunctionType
ALU = mybir.AluOpType
AX = mybir.AxisListType

D_MODEL = 1024
KC = 8
EPS = 1e-6
BIG = 1.0e6
COMPUTE = ("pe", "act", "dve", "pool")
SLOPES = [float(2.0 ** (-8.0 * (h + 1) / 16.0)) for h in range(16)]


def _ap_box(ap):
    t = ap.tensor
    name = t.name
    dims = [(s, n) for (s, n) in ap.ap]
    off = ap.offset
    tn = type(t).__name__
    if "PSum" in tn:
        return (name, 0, 128, 0, 1 << 30, True)
    if "DRam" in tn:
        nz = [(abs(s), n) for (s, n) in dims if n > 1 and s != 0]
        if nz:
            smax, nmax = max(nz)
            rest = [(s, n) for (s, n) in dims if not (abs(s) == smax and n == nmax)]
            ext = sum(abs(s) * (n - 1) for (s, n) in rest if n > 1) + 1
            f0 = off % smax
            if len(rest) == len(dims) - 1 and f0 + ext <= smax and all(s >= 0 for s, n in dims):
                p_lo = off // smax
                return (name, p_lo, p_lo + nmax, f0, f0 + ext, False)
        lo = hi = off
        for s, n in dims:
            if n > 1:
                if s >= 0:
                    hi += s * (n - 1)
                else:
                    lo += s * (n - 1)
        return (name, 0, 1 << 30, lo, hi + 1, False)
    pst, pn = dims[0]
    if pst == 0:
        p_lo, f0 = 0, off
    else:
        p_lo = off // pst
        f0 = off - p_lo * pst
    lo = hi = f0
    for s, n in dims[1:]:
        if n > 1:
            if s >= 0:
                hi += s * (n - 1)
            else:
                lo += s * (n - 1)
    return (name, p_lo, p_lo + pn, lo, hi + 1, False)


class Op:
    __slots__ = ("eng", "fn", "reads", "writes", "deps", "signal", "count", "is_dma", "slot", "dval", "idx",
                 "dur", "succ", "fin", "st", "why", "tag", "aset", "lat")

    def __init__(self, eng, fn, reads, writes, is_dma, dur):
        self.eng = eng
        self.fn = fn
        self.reads = reads
        self.writes = writes
        self.deps = ()
        self.signal = False
        self.count = 0
        self.is_dma = is_dma
        self.slot = None
        self.dval = 0
        self.idx = -1
        self.dur = dur
        self.succ = []
        self.fin = 0.0
        self.st = 0.0
        self.why = None
        self.tag = ""
        self.aset = None


def _ov(a, b):
    return a[1] < b[2] and b[1] < a[2] and a[3] < b[4] and b[3] < a[4]


def _cov(b, e):
    return b[1] <= e[1] and e[2] <= b[2] and b[3] <= e[3] and e[4] <= b[4]


class Rec:
    SEM_LAT = 250.0

    def __init__(self, nc, slots):
        self.nc = nc
        self.ops = []
        self.wr = {}
        self.rd = {}
        self.slots = slots
        self.const_names = set()
        self.cur_tag = ""
        self.dry = False
        self.dma_hist = {}
        self.makespan = 0.0
        import os
        self.maxops = int(os.environ["K_MAXOPS"]) if "K_MAXOPS" in os.environ else None
        self.sched = os.environ.get("K_NOSCHED") is None

    def add(self, eng, fn, reads, writes, is_dma=False, dur=100.0, extra=()):
        if self.dry or (self.maxops is not None and len(self.ops) >= self.maxops):
            return None
        op = Op(eng, fn, [_ap_box(a) for a in reads], [_ap_box(a) for a in writes], is_dma, dur)
        op.idx = len(self.ops)
        op.tag = self.cur_tag
        self.ops.append(op)
        deps = set()
        for b in op.reads:
            for (ob, oop) in self.wr.get(b[0], ()):
                if _ov(ob, b):
                    deps.add(oop)
            if b[5]:
                for (ob, oop) in self.rd.get(b[0], ()):
                    if oop.eng != eng and _ov(ob, b):
                        deps.add(oop)
        for b in op.writes:
            for (ob, oop) in self.wr.get(b[0], ()):
                if _ov(ob, b):
                    deps.add(oop)
            for (ob, oop) in self.rd.get(b[0], ()):
                if _ov(ob, b):
                    deps.add(oop)
        for x in extra:
            if x is not None:
                deps.add(x)
        if is_dma:
            hist = self.dma_hist.setdefault(eng, [])
            ns = self.slots[eng]
            if len(hist) >= ns:
                deps.add(hist[-ns])
            hist.append(op)
        deps.discard(op)
        op.deps = tuple(deps)
        for d in deps:
            d.succ.append(op)
        for b in op.writes:
            for dct in (self.wr, self.rd):
                lst = dct.get(b[0])
                if lst:
                    lst[:] = [e for e in lst if not _cov(b, e[0])]
            self.wr.setdefault(b[0], []).append((b, op))
        for b in op.reads:
            if b[0] in self.const_names:
                continue
            self.rd.setdefault(b[0], []).append((b, op))
        return op

    def schedule(self):
        import heapq
        ops = self.ops
        indeg = [len(o.deps) for o in ops]
        heaps = {}
        free = {}
        L = self.SEM_LAT

        import os
        fbonus = float(os.environ.get("K_FBONUS", "0"))

        tru = {}

        def push(o, t):
            heaps.setdefault(o.eng, [])
            tru[o.idx] = t
            if fbonus and o.tag in ("7ffn", "8ple"):
                t = t - fbonus
            heapq.heappush(heaps[o.eng], (t, o.idx))

        for o in ops:
            if indeg[o.idx] == 0:
                push(o, 0.0)
        order = []
        n = len(ops)
        hp_t = {}
        last_on = {}
        cur_set = ["A"]
        while len(order) < n:
            best = None
            for e, hp in heaps.items():
                if not hp:
                    continue
                t, i = hp[0]
                stt_ = max(t, free.get(e, 0.0))
                if best is None or (stt_, i) < best[0]:
                    best = ((stt_, i), e)
            (stt_, i), e = best
            stt_ = max(tru[i], free.get(e, 0.0))
            if e == "act" and ops[i].aset is not None and ops[i].aset != cur_set[0] and len(heaps[e]) > 1:
                cands = heapq.nsmallest(6, heaps[e])
                pick = None
                for (t_, j_) in cands:
                    if ops[j_].aset in (None, cur_set[0]) and max(t_, free.get(e, 0.0)) <= stt_ + 1300.0:
                        pick = (t_, j_)
                        break
                if pick is not None:
                    heaps[e].remove(pick)
                    heapq.heapify(heaps[e])
                    i = pick[1]
                    stt_ = max(tru[i], free.get(e, 0.0))
                else:
                    heapq.heappop(heaps[e])
            else:
                heapq.heappop(heaps[e])
            o = ops[i]
            if e == "act" and o.aset is not None:
                if o.aset != cur_set[0]:
                    stt_ += 1300.0
                cur_set[0] = o.aset
            order.append(o)
            o.st = stt_
            if stt_ > hp_t.get(o.idx, 0.0) + 1e-9:
                o.why = last_on.get(e)
            last_on[e] = o
            if o.is_dma:
                free[e] = stt_ + 60.0
            else:
                free[e] = stt_ + o.dur
            o.fin = stt_ + o.dur
            for sc_ in o.succ:
                indeg[sc_.idx] -= 1
                if indeg[sc_.idx] == 0:
                    rt = 0.0
                    for d in sc_.deps:
                        lat = 0.0 if (d.eng == "pe" and sc_.eng == "pe" and not d.is_dma and not sc_.is_dma) else L
                        if d.fin + lat > rt:
                            rt = d.fin + lat
                            sc_.why = d
                    hp_t[sc_.idx] = rt
                    push(sc_, rt)
        self.makespan = max(o.fin for o in ops)
        return order

    def emit(self, stack):
        nc = self.nc
        engs = {"pe": nc.tensor, "act": nc.scalar, "dve": nc.vector, "pool": nc.gpsimd, "sp": nc.sync}
        order = self.schedule() if self.sched else list(self.ops)
        pos = {}
        for k, op in enumerate(order):
            pos[op.idx] = k
        for op in order:
            latest = {}
            for d in op.deps:
                if d.is_dma:
                    continue
                if d.eng == op.eng and not op.is_dma and d.eng == "pe":
                    continue
                cur = latest.get(d.eng)
                if cur is None or pos[d.idx] > pos[cur.idx]:
                    latest[d.eng] = d
            op.lat = latest
            for d in latest.values():
                d.signal = True
        cnt = {e: 0 for e in COMPUTE}
        dq = {}
        for op in order:
            if op.is_dma:
                k = dq.get(op.eng, 0)
                dq[op.eng] = k + 1
                ns = self.slots[op.eng]
                op.slot = (op.eng, k % ns)
                op.dval = 16 * (k // ns + 1)
            elif op.signal:
                cnt[op.eng] += 1
                op.count = cnt[op.eng]
        sems = {}
        for e in COMPUTE:
            sems[e] = stack.enter_context(nc.semaphore("s_" + e))
        for q in dq:
            for k in range(min(self.slots[q], dq[q])):
                sems[(q, k)] = stack.enter_context(nc.semaphore("d_%s_%d" % (q, k)))
        waited = {}
        nwaits = 0
        for op in order:
            e = engs[op.eng]
            need = {}
            for d in op.deps:
                if d.is_dma:
                    key, val = d.slot, d.dval
                    if need.get(key, 0) < val:
                        need[key] = val
            for eng_, d in op.lat.items():
                need[eng_] = d.count
            if op.is_dma and op.dval > 16:
                if need.get(op.slot, 0) < op.dval - 16:
                    need[op.slot] = op.dval - 16
            for key, val in need.items():
                wk = (op.eng, key)
                if waited.get(wk, 0) >= val:
                    continue
                waited[wk] = val
                e.wait_ge(sems[key], val)
                nwaits += 1
            ins = op.fn(e)
            if op.is_dma:
                ins.then_inc(sems[op.slot], 16)
            elif op.signal:
                ins.then_inc(sems[op.eng], 1)
        for q, n in dq.items():
            e = engs[q]
            ns = self.slots[q]
            for k in range(min(n, ns)):
                e.wait_ge(sems[(q, k)], 16 * ((n - 1 - k) // ns + 1))
        for ce in COMPUTE:
            if cnt[ce] > 0:
                nc.sync.wait_ge(sems[ce], cnt[ce])
        self.stats = dict(nops=len(self.ops), nwaits=nwaits, cnt=cnt, dq=dq, makespan_us=self.makespan / 1000.0)


C_QKV, C_Z, C_B, C_A, C_SQ, C_SK, C_SV, C_G = 0, 3072, 4096, 4104, 4112, 5136, 5392, 5648


def weight_groups():
    g = []
    r = lambda a, n: list(range(a, a + n))
    g.append(("tm", "w_in", 1024, r(C_B, 16) + r(C_SV, 256)))
    for hg in range(2):
        g.append(("q%d" % hg, "w_in", 1024, r(C_QKV + hg * 512, 512)))
        g.append(("k%d" % hg, "w_in", 1024, r(C_QKV + 1024 + hg * 512, 512)))
        g.append(("v%d" % hg, "w_in", 1024, r(C_QKV + 2048 + hg * 512, 512)))
        g.append(("z%d" % hg, "w_in", 1024, r(C_Z + hg * 512, 512)))
    for i in range(2):
        g.append(("sq%d" % i, "w_in", 1024, r(C_SQ + i * 512, 512)))
    cols = []
    for j in range(4):
        cols += r(C_SK + j * 64, 64) + r(C_SK + j * 64, 64)
    g.append(("skd", "w_in", 1024, cols))
    for i in range(2):
        g.append(("ga%d" % i, "w_in", 1024, r(C_G + i * 512, 512)))
        g.append(("gb%d" % i, "w_in", 1024, r(C_G + 1024 + i * 512, 512)))
    for i in range(2):
        g.append(("wo%d" % i, "w_out", 1024, r(i * 512, 512)))
    for hh in range(2):
        for gi in range(4):
            g.append(("up%d_%d" % (hh, gi), "w_up", 1024, r((hh * 16 + gi * 4) * 128, 512)))
        for cb in range(4):
            g.append(("dn%d_%d" % (hh, cb), "w_down", (hh * 2048, 2048), r(cb * 256, 256)))
    for i in range(2):
        g.append(("pg%d" % i, "w_ple_gate", 1024, r(i * 512, 512)))
    g.append(("pp", "w_ple_proj", 256, r(0, 1024)))
    return g


def group_offsets():
    offs = {}
    o = 0
    for (name, src, rows, cols) in weight_groups():
        k = rows[1] if isinstance(rows, tuple) else rows
        n = (k // 128) * len(cols)
        offs[name] = (o, k // 128, len(cols))
        o += n
    return offs, o


SP_NM, SP_NF, SP_NP, SP_CW, SP_GN, SP_QN, SP_KN, SP_SK, SP_AL, SP_DT = 0, 8, 16, 24, 120, 121, 122, 123, 131, 139
NSP = 147

CSTF_NAMES = ["ONES", "U", "U_S", "ONESSEQ_S", "DD", "DO", "DN_S"]
CSTF = {n: i * 128 for i, n in enumerate(CSTF_NAMES)}
CSTF["DC"] = len(CSTF_NAMES) * 128
CSTF["SM"] = CSTF["DC"] + 8
NCSTF = CSTF["SM"] + 16
CSTB_NAMES = ["IDENT", "ONES", "ONESB64", "L", "MS", "MI", "MS_S", "MI_S", "MD4", "ML4", "MLT4", "ML8", "MLT8",
              "ML16", "MLT16", "ML32", "MLT32", "ML64", "MLT64"]
CSTB = {n: i * 128 for i, n in enumerate(CSTB_NAMES)}
NCSTB = len(CSTB_NAMES) * 128


def make_consts():
    cf = np.zeros((128, NCSTF), np.float32)
    cb = np.zeros((128, NCSTB), np.float32)
    i = np.arange(128)
    P, Fq = np.meshgrid(i, i, indexing="ij")
    same = (P // 8) == (Fq // 8)

    def put(n, m):
        m = np.asarray(m, np.float32)
        if n in CSTF:
            cf[:, CSTF[n]:CSTF[n] + m.shape[1]] = m
        if n in CSTB:
            cb[:, CSTB[n]:CSTB[n] + m.shape[1]] = m
    put("IDENT", P == Fq)
    put("ONES", np.ones((128, 128)))
    put("ONESB64", (P // 64) == (Fq // 64))
    put("U", P <= Fq)
    put("L", P > Fq)
    put("MS", P > Fq)
    put("MI", P >= Fq)
    put("U_S", same & (P <= Fq))
    put("MS_S", same & (P > Fq))
    put("MI_S", same & (P >= Fq))
    put("ONESSEQ_S", same)
    put("DD", np.where(Fq >= P, Fq - P, BIG))
    put("DO", np.where(Fq <= P, Fq + 128 - P, BIG))
    put("DN_S", np.where(same & (Fq >= P), Fq - P, BIG))
    put("MD4", (P // 4) == (Fq // 4))
    for m in (4, 8, 16, 32, 64):
        ml = ((P // (2 * m)) == (Fq // (2 * m))) & ((P // m) == (Fq // m) + 1)
        put("ML%d" % m, ml)
        put("MLT%d" % m, ml.T)
    j = np.arange(128)[:, None]
    t = np.arange(8)[None, :]
    put("DC", np.where(j >= t, 128 + t - j, BIG))
    put("SM", (np.arange(128)[:, None] // 8) == np.arange(16)[None, :])
    return cf, cb


class Cfg:
    def __init__(self, depth=4, seq=2048, tt=256, sample=True, stages=("gdn", "swa", "ffn", "ple"), dbg=()):
        self.depth = depth
        self.seq = seq
        self.tt = tt
        self.sample = sample
        self.ntok = seq + (128 if sample else 0)
        self.stages = stages
        self.dbg = dbg
        import os
        self.nopair = os.environ.get("K_NOPAIR") is not None


def build(cfg):
    nc = bass.Bass("TRN2", target_bir_lowering=False)
    st = contextlib.ExitStack()
    DEPTH, SEQ, TT, NTOK = cfg.depth, cfg.seq, cfg.tt, cfg.ntok
    offs, TOT = group_offsets()
    import os as _os
    PEADD = _os.environ.get("K_PEADD", "0") == "1"
    R = Rec(nc, {"sp": 12, "pool": 3, "act": 4})

    def din(name, shape, dt=F32):
        return nc.dram_tensor(name, list(shape), dt, kind="ExternalInput").ap()

    def dout(name, shape, dt=F32):
        return nc.dram_tensor(name, list(shape), dt, kind="ExternalOutput").ap()

    def sb(name, shape, dt=F32):
        return st.enter_context(nc.sbuf_tensor(name, list(shape), dt))

    xT = din("xT", [128, KC, NTOK])
    pT = din("pT", [DEPTH, 128, 2, NTOK])
    wsrc = din("wsrc", [DEPTH, 128, TOT])
    spar = din("spar", [128, DEPTH, NSP])
    cstd = din("cst", [128, NCSTF])
    cstmd = din("cstm", [128, NCSTB])
    wbf = nc.dram_tensor("wbf", [DEPTH, 128, TOT], BF16, kind="Internal").ap()
    yT = dout("yT", [128, KC, NTOK])
    o_convp = dout("o_convp", [DEPTH, 128, 24, 3])
    o_gdnp = dout("o_gdnp", [DEPTH, 128, 8, 128])
    o_kp = dout("o_kp", [DEPTH, 128, 4, 128])
    o_vp = dout("o_vp", [DEPTH, 128, 256])
    if cfg.sample:
        i_cconv = din("i_cconv", [DEPTH, 128, 24, 16, 3])
        i_sgdn = din("i_sgdn", [DEPTH, 16, 8, 128, 128])
        i_kcT = din("i_kcT", [DEPTH, 4, 128, 16, 128])
        i_vc = din("i_vc", [DEPTH, 16, 128, 256])
        i_kc = din("i_kc", [DEPTH, 16, 128, 256])
        o_convs = dout("o_convs", [DEPTH, 128, 24, 16, 3])
        o_gdns = dout("o_gdns", [DEPTH, 16, 8, 128, 128])
        o_ksn = dout("o_ksn", [DEPTH, 128, 4, 128])
        o_vsn = dout("o_vsn", [DEPTH, 128, 256])
        o_kcopy = dout("o_kcopy", [DEPTH, 16, 120, 256])
        o_vcopy = dout("o_vcopy", [DEPTH, 16, 120, 256])
    dbg_out = {}

    cst = sb("cst_f", [128, NCSTF])
    cstb = sb("cst_b", [128, NCSTB], BF16)
    sp_t = sb("sp_t", [128, DEPTH, NSP])
    spd = sb("spd", [128, DEPTH, 32])
    hbuf = [sb("h%d" % i, [128, KC, TT]) for i in range(2)]
    xn = sb("xn", [128, KC, TT], BF16)
    xnF = sb("xnF", [128, KC, TT], BF16)
    arena = sb("arena", [128, 32, TT], BF16)
    arenaF = sb("arenaF", [128, 16, TT], BF16)
    gF = arenaF[:, 0:KC, :]
    qaT = sb("qaT", [128, KC, TT], BF16)
    yqF = sb("yqF", [128, TT])
    sqbF = [sb("sqbF%d" % i, [128, TT], BF16) for i in range(2)]
    rstdF = sb("rstdF", [128, TT])
    oaT = sb("oaT", [128, KC, TT], BF16)
    obT = sb("obT", [128, KC, TT], BF16)
    ND_ = 3
    rawx = [sb("rawx%d" % i, [128, TT + 48]) for i in range(ND_)]
    caccs = [sb("cacc%d" % i, [128, TT]) for i in range(ND_)]
    ctan = [sb("ctan%d" % i, [128, TT]) for i in range(ND_)]
    yqs = [sb("yq%d" % i, [128, TT]) for i in range(ND_)]
    sqb = [sb("sqb%d" % i, [128, TT], BF16) for i in range(2)]
    rstd = [sb("rstd%d" % i, [128, TT]) for i in range(2)]
    rot = [0]
    arena_f = arena[:, :, :].rearrange("p a t -> p (a t)").bitcast(F32)
    m1s = [arena_f[:, i * TT:(i + 1) * TT] for i in range(4)]
    m2s = [arena_f[:, (4 + i) * TT:(5 + i) * TT] for i in range(3)]
    NSLOT = int(_os.environ.get("K_NSLOT", "4"))
    wring = [sb("wring%d" % i, [128, 4096], BF16) for i in range(NSLOT)]
    pTf = sb("pTf", [128, 2, TT])
    pTb = sb("pTb", [128, 2, TT], BF16)
    NBMAX = TT // 128
    tmsm = sb("tmsm", [128, NBMAX, 16])
    gsm = sb("gsm", [128, NBMAX, 96])
    vext = sb("vext", [128, NBMAX + 1, 256], BF16)
    kdup = sb("kdup", [128, 4, (NBMAX + 1) * 128], BF16)
    S_all = sb("S_all", [128, max(DEPTH, 3), 8, 128])
    Sb = sb("Sb", [128, 8, 128], BF16)
    tails = sb("tails", [128, DEPTH, 24, 3])
    kprev = sb("kprev", [128, DEPTH, 4, 128], BF16)
    vprev = sb("vprev", [128, DEPTH, 256], BF16)
    GL = sb("GL", [128, 4, 128])
    Eraw = sb("Eraw", [128, 4, 128])
    Ems = sb("Ems", [128, 4, 128], BF16)
    Emi = sb("Emi", [128, 4, 128], BF16)
    Nb = sb("Nb", [128, 4, 128], BF16)
    Pm = [sb("Pm%d" % i, [128, 4, 128], BF16) for i in range(2)]
    PTm = [sb("PTm0", [128, 4, 128], BF16)]
    X1b = sb("X1b", [128, 4, 128], BF16)
    X2b = sb("X2b", [128, 4, 128], BF16)
    No_b = sb("No_b", [128, 4, 128], BF16)
    NoT_b = sb("NoT_b", [128, 4, 128], BF16)
    Tdn = sb("Tdn", [128, 4, 128], BF16)
    TTb = sb("TTb", [128, 4, 128], BF16)
    qkb = sb("qkb", [128, 4, 128], BF16)
    qkT = sb("qkT", [128, 4, 128], BF16)
    kbg, kdec, vbt, wTb = Ems, Emi, Pm[0], Pm[1]
    u_t = GL
    vnew = X1b
    tq = sb("tq", [128, 4, 128])
    o_t = Eraw
    osq = tq
    on_b = X2b
    osm = sb("osm", [128, 16])
    sc = [sb("sc%d" % i, [128, 4, 128]) for i in range(2)]
    PTa = [sb("PTa%d" % i, [128, 4, 128], BF16) for i in range(2)]
    rden = sb("rden", [128, 2, 128])
    kfin = sc[0]
    vfin = rden[:, :, :].rearrange("p a b -> p (a b)")
    if cfg.sample:
        ccin = sb("ccin", [128, 24, 16, 3])
        ccout = sb("ccout", [128, 24, 16, 3])
        Ssf = None
        Ssb = [sb("Ssb%d" % i, [128, 8, 128], BF16) for i in range(2)]
        kcb = sb("kcb", [128, 16, 128], BF16)
        vcb = sb("vcb", [128, 16, 64], BF16)
        kdm = [sb("kdm%d" % i, [128, 4, 128], BF16) for i in range(2)]
        scs = sb("scs", [128, 16, 4, 8])
        PTc = sb("PTc", [128, 16, 4, 9], BF16)
        qsT = No_b
        uT = GL
        vnT = NoT_b
        egls = sb("egls", [128, 16, 8])
        gsq = sb("gsq", [128, 16, 8])
        snew = None
    import sys as _sys
    print("[kernel] SBUF bytes/partition remaining:", nc.sbuf_bytes_remaining, file=_sys.stderr)
    banks = [st.enter_context(nc.psum_tensor("ps%d" % i, [128, 512], F32)) for i in range(8)]
    bank_i = [0]

    bank_f = [0]
    bank_g = [0]
    NG_, NM_, NF_ = [int(x) for x in _os.environ.get("K_BANKS", "2,4,2").split(",")]

    def bank(pool="M"):
        if pool == "F":
            b = banks[NG_ + NM_ + bank_f[0] % NF_]
            bank_f[0] += 1
        elif pool == "G":
            b = banks[bank_g[0] % NG_]
            bank_g[0] += 1
        else:
            b = banks[NG_ + bank_i[0] % NM_]
            bank_i[0] += 1
        return b

    def C(name, n=128):
        return cst[:, CSTF[name]:CSTF[name] + n]

    def CB(name):
        return cstb[:, CSTB[name]:CSTB[name] + 128]

    IDb = CB("IDENT")
    ONESb = CB("ONES")
    ONES64b = CB("ONESB64")

    isap = lambda x: not isinstance(x, (int, float))

    def vdur(eng, ap):
        n = fsz(ap)
        return (n + 70) / (0.96 if eng == "dve" else 0.5) + 60.0

    def fsz(ap):
        n = 1
        for d in ap.shape[1:]:
            n *= d
        return n

    def mm(out, lhsT, rhs, start=True, stop=True, skip=False):
        d = max(64, fsz(rhs)) / 2.4 * (4.0 if rhs.dtype == F32 else 1.0) + 8.0
        R.add("pe", lambda e: e.matmul(out, lhsT=lhsT, rhs=rhs, start=start, stop=stop, skip_group_check=skip),
              [lhsT, rhs], [out], dur=d)

    def tr(out, in_, ident):
        R.add("pe", lambda e: e.transpose(out, in_, ident), [in_, ident], [out], dur=128 / 2.4 + 8.0)

    def act(out, in_, func, scale=1.0, bias=0.0):
        rd = [in_] + [x for x in (scale, bias) if isap(x)]
        o_ = R.add("act", lambda e: e.activation(out=out, in_=in_, func=func, bias=bias, scale=scale), rd, [out],
                   dur=(fsz(in_) + 200) / 1.2)
        if o_ is not None:
            o_.aset = "B" if func == AF.Ln else ("A" if func == AF.Tanh else None)

    def tt(eng, out, a, b, op):
        R.add(eng, lambda e: e.tensor_tensor(out=out, in0=a, in1=b, op=op), [a, b], [out], dur=vdur(eng, a))

    def ts(eng, out, a, s1, op0, s2=None, op1=None):
        rd = [a] + [x for x in (s1, s2) if x is not None and isap(x)]
        if op1 is None:
            R.add(eng, lambda e: e.tensor_scalar(out=out, in0=a, scalar1=s1, scalar2=None, op0=op0), rd, [out],
                  dur=vdur(eng, a))
        else:
            R.add(eng, lambda e: e.tensor_scalar(out=out, in0=a, scalar1=s1, scalar2=s2, op0=op0, op1=op1), rd, [out],
                  dur=vdur(eng, a))

    def stt(out, a, s, b, op0, op1):
        rd = [a, b] + ([s] if isap(s) else [])
        R.add("dve", lambda e: e.scalar_tensor_tensor(out=out, in0=a, scalar=s, in1=b, op0=op0, op1=op1), rd, [out],
              dur=vdur("dve", a))

    def cp(eng, out, in_):
        if eng == "act":
            act(out, in_, AF.Copy)
        else:
            R.add(eng, lambda e: e.tensor_copy(out=out, in_=in_), [in_], [out], dur=vdur(eng, in_))

    def red(out, in_, op):
        R.add("dve", lambda e: e.tensor_reduce(out=out, in_=in_, axis=AX.X, op=op), [in_], [out], dur=vdur("dve", in_))

    def recip(out, in_):
        R.add("dve", lambda e: e.reciprocal(out=out, in_=in_), [in_], [out], dur=vdur("dve", in_))

    def memset(eng, out, v):
        R.add(eng, lambda e: e.memset(out, v), [], [out], dur=vdur(eng, out))

    def dma(q, out, in_, extra=(), **kw):
        nbytes = 1
        for d in out.shape:
            nbytes *= d
        nbytes *= (4 if out.dtype == F32 else 2) + (4 if in_.dtype == F32 else 2)
        return R.add(q, lambda e: e.dma_start(out=out, in_=in_, **kw), [in_], [out], is_dma=True,
                     dur=2000.0 + nbytes / 2 / 400.0, extra=extra)

    def dbg(name, ap, shape):
        if name in cfg.dbg:
            if name not in dbg_out:
                dbg_out[name] = dout("dbg_" + name, shape)
            dma("sp", dbg_out[name], ap)

    def rsqrt_ln(out, in_, scale, eps, mult=1.0):
        act(out, in_, AF.Ln, scale=scale, bias=eps)
        act(out, out, AF.Exp, scale=-0.5, bias=float(np.log(mult)))

    dma("sp", cst[:, :], cstd)
    dma("sp", sp_t[:, :, :], spar)
    step = 2 * 4096
    o = 0
    while o < TOT:
        n = min(step, TOT - o)
        dma("pool", wbf[0, :, o:o + n], wsrc[0, :, o:o + n], max_dma_last_dim=4096)
        o += n
    cast_done = set()
    stage = S_all[:, :, :, :].rearrange("p a b c -> p (a b c)")
    dma("sp", stage[:, 0:NCSTB], cstmd)
    cp("dve", cstb[:, :], stage[:, 0:NCSTB])
    for l in range(DEPTH):
        ts("dve", spd[:, l, 0:1], sp_t[:, l, SP_GN:SP_GN + 1], 0.5, ALU.mult)
        ts("dve", spd[:, l, 1:2], sp_t[:, l, SP_QN:SP_QN + 1], 0.125, ALU.mult)
        act(spd[:, l, 2:10], sp_t[:, l, SP_SK:SP_SK + 8], AF.Exp)
        act(spd[:, l, 10:18], sp_t[:, l, SP_AL:SP_AL + 8], AF.Exp)
        ts("dve", spd[:, l, 10:18], spd[:, l, 10:18], -1.0, ALU.mult)

    R.const_names.update(["cst_f", "cst_b", "sp_t", "spd", "xT", "pT", "wsrc", "spar", "cst", "cstm", "i_cconv", "i_sgdn",
                          "i_kcT", "i_vc", "i_kc"])

    plan = []
    wstate = dict(next_load=0, next_use=0)

    def wload(i):
        l, g = plan[i]
        o, kc, ncol = offs[g]
        n = kc * ncol
        op_ = dma("sp", wring[i % NSLOT][:, 0:n], wbf[l, :, o:o + n])
        if (l, g) not in cast_done and not R.dry:
            cast_done.add((l, g))
            if l + 1 < DEPTH:
                dma("pool", wbf[l + 1, :, o:o + n], wsrc[l + 1, :, o:o + n], max_dma_last_dim=4096, extra=(op_,))

    def wget(l, g):
        i = wstate["next_use"]
        wstate["next_use"] += 1
        o, kc, ncol = offs[g]
        if R.dry:
            plan.append((l, g))
        else:
            assert plan[i] == (l, g), (plan[i], l, g)
            while wstate["next_load"] < min(len(plan), i + NSLOT):
                wload(wstate["next_load"])
                wstate["next_load"] += 1
        return wring[i % NSLOT][:, 0:kc * ncol].rearrange("p (k n) -> p k n", k=kc)

    tiles = []
    t0 = 0
    while t0 < SEQ:
        tiles.append(dict(kind="p", t0=t0, nt=TT, first=(t0 == 0), last=(t0 + TT >= SEQ), idx=len(tiles)))
        t0 += TT
    if cfg.sample:
        tiles.append(dict(kind="s", t0=SEQ, nt=128, first=True, last=True, idx=len(tiles)))

    def rmsnorm_fm(hs, xd, l, col, nt, sq2, r, pool="M"):
        ps = bank(pool)
        for kc in range(KC):
            s = sq2[kc % 2]
            act(s[:, 0:nt], hs[:, kc, 0:nt], AF.Square)
            mm(ps[:, 0:nt], ONESb, s[:, 0:nt], start=(kc == 0), stop=(kc == KC - 1))
        rsqrt_ln(r[:, 0:nt], ps[:, 0:nt], 1.0 / D_MODEL, EPS)
        for kc in range(KC):
            stt(xd[:, kc, 0:nt], hs[:, kc, 0:nt], sp_t[:, l, col + kc:col + kc + 1], r[:, 0:nt], ALU.mult, ALU.mult)

    def proj_chunk(w, c, nt, rhs_t, pool="M"):
        ps = bank(pool)
        nk = w.shape[1]
        for kc in range(nk):
            mm(ps[:, 0:nt], w[:, kc, c * 128:(c + 1) * 128], rhs_t[:, kc, 0:nt], start=(kc == 0), stop=(kc == nk - 1))
        return ps

    def gen_M(tile, l):
        kind, nt = tile["kind"], tile["nt"]
        nb = nt // 128
        samp = (kind == "s")
        ST = cfg.stages
        h = hbuf[tile["idx"] % 2]
        if not samp:
            if tile["first"]:
                memset("pool", S_all[:, l, :, :], 0.0)
                memset("pool", tails[:, l, :, :], 0.0)
            cp("act", Sb[:, :, :], S_all[:, l, :, :])
            if not tile["first"]:
                cp("pool", vext[:, 0, :], vprev[:, l, :])
                cp("pool", kdup[:, :, 0:128], kprev[:, l, :, :])
        else:
            dma("sp", ccin[:, :, :, :], i_cconv[l])
        R.cur_tag = "1norm"
        rmsnorm_fm(h, xn, l, SP_NM, nt, sqb[0:2], rstd[0])
        R.cur_tag = "2tm"
        w = wget(l, "tm")
        for b in range(nb):
            ps = bank()
            for kc in range(KC):
                mm(ps[:, 0:272], xn[:, kc, b * 128:(b + 1) * 128], w[:, kc, 0:272], start=(kc == 0), stop=(kc == KC - 1))
            cp("act", tmsm[:, b, :], ps[:, 0:16])
            cp("dve", vext[:, b + 1, :], ps[:, 16:272])
            if tile["last"] and b == nb - 1:
                cp("act", vfin[:, :], ps[:, 16:272])
                dma("sp", (o_vsn if samp else o_vp)[l], vfin[:, :])
        yield
        R.cur_tag = "3conv"
        L_ = 8 if samp else nt
        nseq = 16 if samp else 1
        for hg in range(2):
            for typ in ("q", "k", "v"):
                w = wget(l, "%s%d" % (typ, hg))
                for c in range(4):
                    ch = {"q": 0, "k": 8, "v": 16}[typ] + hg * 4 + c
                    ps = proj_chunk(w, c, nt, xn)
                    if "gdn" not in ST:
                        continue
                    rot[0] += 1
                    ri = rot[0] % ND_
                    cacc, yq = caccs[ri], yqs[ri]
                    rx = rawx[ri]
                    rxv = rx[:, 0:nseq * (L_ + 3)].rearrange("p (s t) -> p s t", s=nseq)
                    psv = ps[:, 0:nt].rearrange("p (s t) -> p s t", s=nseq)
                    cp("act", rxv[:, :, 3:3 + L_], psv)
                    if samp:
                        cp("pool", rxv[:, :, 0:3], ccin[:, ch, :, :])
                    else:
                        cp("pool", rxv[:, :, 0:3], tails[:, l, ch:ch + 1, :])
                    av = cacc[:, 0:nt].rearrange("p (s t) -> p s t", s=nseq)
                    cw = lambda j: sp_t[:, l, SP_CW + ch * 4 + j:SP_CW + ch * 4 + j + 1]
                    act(av, psv, AF.Copy, scale=cw(3))
                    for j in range(0, 3):
                        stt(av, rxv[:, :, j:j + L_], cw(j), av, ALU.mult, ALU.add)
                    if samp:
                        cp("pool", ccout[:, ch, :, :], rxv[:, :, L_:L_ + 3])
                    else:
                        cp("pool", tails[:, l, ch:ch + 1, :], rxv[:, :, L_:L_ + 3])
                    tn = ctan[ri]
                    act(tn[:, 0:nt], cacc[:, 0:nt], AF.Tanh, scale=0.5)
                    if typ == "v":
                        stt(arena[:, hg * 16 + 8 + c, 0:nt], tn[:, 0:nt], 1.0, cacc[:, 0:nt], ALU.add, ALU.mult)
                    else:
                        stt(yq[:, 0:nt], tn[:, 0:nt], 1.0, cacc[:, 0:nt], ALU.add, ALU.mult)
                        s_ = sqb[ri % 2]
                        act(s_[:, 0:nt], yq[:, 0:nt], AF.Square)
                        pn = bank()
                        mm(pn[:, 0:nt], ONESb, s_[:, 0:nt])
                        r = rstd[ri % 2]
                        rsqrt_ln(r[:, 0:nt], pn[:, 0:nt], 1.0, 4.0 * EPS, mult=(128.0 ** -0.5 if typ == "q" else 1.0))
                        dst = arena[:, hg * 16 + (0 if typ == "q" else 4) + c, 0:nt]
                        tt("dve", dst, yq[:, 0:nt], r[:, 0:nt], ALU.mult)
                yield
            w = wget(l, "z%d" % hg)
            for c in range(4):
                ps = proj_chunk(w, c, nt, xn)
                if "gdn" not in ST:
                    continue
                tn = ctan[c % ND_]
                act(tn[:, 0:nt], ps[:, 0:nt], AF.Tanh, scale=0.5)
                stt(arena[:, hg * 16 + 12 + c, 0:nt], tn[:, 0:nt], 1.0, ps[:, 0:nt], ALU.add, ALU.mult)
            if "gdn" in ST:
                for b in range(nb):
                    R.cur_tag = "3gdn"
                    if hg == 0:
                        gdn_small(tile, l, b)
                    gdn_unit(tile, l, b, hg)
                R.cur_tag = "3conv"
            else:
                if hg == 0:
                    memset("pool", oaT[:, :, 0:nt], 0.0)
            yield
        if "gdn" in ST:
            if samp:
                dma("sp", o_convs[l], ccout[:, :, :, :])
            elif tile["last"]:
                dma("sp", o_convp[l], tails[:, l, :, :])
                dma("sp", o_gdnp[l], S_all[:, l, :, :])
        R.cur_tag = "4swa"
        qa = qaT
        for i in range(2):
            w = wget(l, "sq%d" % i)
            for c in range(4):
                ps = proj_chunk(w, c, nt, xn)
                if "swa" not in ST:
                    continue
                s_ = sqb[c % 2]
                act(s_[:, 0:nt], ps[:, 0:nt], AF.Square)
                pn = bank()
                mm(pn[:, 0:nt], ONES64b, s_[:, 0:nt])
                r = rstd[c % 2]
                rsqrt_ln(r[:, 0:nt], pn[:, 0:nt], 1.0 / 64.0, EPS)
                stt(qa[:, i * 4 + c, 0:nt], ps[:, 0:nt], spd[:, l, 1:2], r[:, 0:nt], ALU.mult, ALU.mult)
            yield
        w = wget(l, "skd")
        for c in range(4):
            ps = proj_chunk(w, c, nt, xn)
            if "swa" not in ST:
                continue
            s_ = sqb[c % 2]
            act(s_[:, 0:nt], ps[:, 0:nt], AF.Square)
            pn = bank()
            mm(pn[:, 0:nt], ONES64b, s_[:, 0:nt])
            r = rstd[c % 2]
            rsqrt_ln(r[:, 0:nt], pn[:, 0:nt], 1.0 / 64.0, EPS)
            stt(kdup[:, c, 128:128 + nt], ps[:, 0:nt], sp_t[:, l, SP_KN:SP_KN + 1], r[:, 0:nt], ALU.mult, ALU.mult)
            if tile["last"]:
                stt(kfin[:, c, :], ps[:, nt - 128:nt], sp_t[:, l, SP_KN:SP_KN + 1], r[:, nt - 128:nt], ALU.mult, ALU.mult)
        if "swa" in ST:
            if tile["last"]:
                dma("sp", (o_ksn if samp else o_kp)[l], kfin[:, :, :])
            for qb in range(nb):
                for kvh in range(4):
                    swa_unit(tile, l, qb, kvh)
            if not samp and not tile["last"]:
                cp("pool", vprev[:, l, :], vext[:, nb, :])
                cp("pool", kprev[:, l, :, :], kdup[:, :, nb * 128:(nb + 1) * 128])
            if samp:
                dma("act", o_kcopy[l], i_kc[l, :, 8:128, :])
                dma("act", o_vcopy[l], i_vc[l, :, 8:128, :])
        else:
            memset("pool", obT[:, :, 0:nt], 0.0)
        yield
        R.cur_tag = "5mix"
        for i in range(2):
            wa = wget(l, "ga%d" % i)
            for c in range(4):
                ps = proj_chunk(wa, c, nt, xn)
                yq = yqs[c % ND_]
                act(yq[:, 0:nt], ps[:, 0:nt], AF.Tanh, scale=0.5)
                stt(m1s[c][:, 0:nt], yq[:, 0:nt], 1.0, oaT[:, i * 4 + c, 0:nt], ALU.add, ALU.mult)
            yield
            wb_ = wget(l, "gb%d" % i)
            for c in range(4):
                ps = proj_chunk(wb_, c, nt, xn)
                yq = yqs[c % ND_]
                act(yq[:, 0:nt], ps[:, 0:nt], AF.Tanh, scale=0.5)
                m2 = m2s[c % 3]
                stt(m2[:, 0:nt], yq[:, 0:nt], 1.0, obT[:, i * 4 + c, 0:nt], ALU.add, ALU.mult)
                tt("dve", oaT[:, i * 4 + c, 0:nt], m1s[c][:, 0:nt], m2[:, 0:nt], ALU.add)
            yield
        R.cur_tag = "6wo"
        for i in range(2):
            w = wget(l, "wo%d" % i)
            for c in range(4):
                ps = proj_chunk(w, c, nt, oaT)
                stt(h[:, i * 4 + c, 0:nt], ps[:, 0:nt], 0.5, h[:, i * 4 + c, 0:nt], ALU.mult, ALU.add)
            yield

    def gen_F(tile, l):
        nt = tile["nt"]
        ST = cfg.stages
        h = hbuf[tile["idx"] % 2]
        R.cur_tag = "7ffn"
        if "ffn" in ST:
            rmsnorm_fm(h, xnF, l, SP_NF, nt, sqbF, rstdF, pool="F")
        hid = arenaF
        for hh in range(2):
            for gi in range(4):
                w = wget(l, "up%d_%d" % (hh, gi))
                if "ffn" in ST:
                    for c in range(4):
                        ps = proj_chunk(w, c, nt, xnF, pool="F")
                        act(yqF[:, 0:nt], ps[:, 0:nt], AF.Relu)
                        tt("pool", hid[:, gi * 4 + c, 0:nt], yqF[:, 0:nt], yqF[:, 0:nt], ALU.mult)
                yield
            for cb in range(4):
                w = wget(l, "dn%d_%d" % (hh, cb))
                if "ffn" in ST:
                    for c in range(2):
                        ps = proj_chunk(w, c, nt, hid, pool="F")
                        oc = cb * 2 + c
                        tt("dve", h[:, oc, 0:nt], ps[:, 0:nt], h[:, oc, 0:nt], ALU.add)
                yield
        R.cur_tag = "8ple"
        if "ple" in ST:
            rmsnorm_fm(h, xnF, l, SP_NP, nt, sqbF, rstdF, pool="F")
            dma("sp", pTf[:, :, 0:nt], pT[l, :, :, tile["t0"]:tile["t0"] + nt])
            cp("pool", pTb[:, :, 0:nt], pTf[:, :, 0:nt])
        for i in range(2):
            w = wget(l, "pg%d" % i)
            if "ple" in ST:
                for c in range(4):
                    ps = proj_chunk(w, c, nt, xnF, pool="F")
                    act(gF[:, i * 4 + c, 0:nt], ps[:, 0:nt], AF.Tanh, scale=0.5)
            yield
        wp = wget(l, "pp")
        if "ple" in ST:
            for oc in range(8):
                ps2 = proj_chunk(wp, oc, nt, pTb, pool="F")
                stt(yqF[:, 0:nt], gF[:, oc, 0:nt], 1.0, ps2[:, 0:nt], ALU.add, ALU.mult)
                stt(h[:, oc, 0:nt], yqF[:, 0:nt], 0.5, h[:, oc, 0:nt], ALU.mult, ALU.add)
        if l == DEPTH - 1:
            dma("sp", yT[:, :, tile["t0"]:tile["t0"] + nt], h[:, :, 0:nt])
        yield

    def gdn_small(tile, l, b):
        samp = tile["kind"] == "s"
        G = lambda a: gsm[:, b, a:a + 8]
        bl = tmsm[:, b, 0:8]
        al = tmsm[:, b, 8:16]
        act(G(80), bl, AF.Tanh, scale=0.5)
        ts("dve", G(0), G(80), 0.5, ALU.mult, 0.5, ALU.add)
        tt("dve", G(80), al, sp_t[:, l, SP_DT:SP_DT + 8], ALU.add)
        act(G(80), G(80), AF.Exp)
        act(G(80), G(80), AF.Ln, bias=1.0)
        tt("dve", G(8), G(80), spd[:, l, 10:18], ALU.mult)
        ps = bank("G")
        mm(ps[:, 0:8], C("U_S") if samp else C("U"), G(8))
        mm(ps[:, 8:16], C("ONESSEQ_S") if samp else C("ONES"), G(8))
        cp("dve", gsm[:, b, 16:32], ps[:, 0:16])
        act(G(32), G(16), AF.Exp)
        tt("dve", G(40), G(0), G(32), ALU.mult)
        tt("dve", G(80), G(24), G(16), ALU.subtract)
        act(G(48), G(80), AF.Exp)
        act(G(56), G(24), AF.Exp)
        ts("dve", G(64), G(0), -1.0, ALU.mult)
        ts("dve", G(72), G(0), 0.5, ALU.mult)
        if samp:
            tt("dve", gsq[:, :, :], G(8).unsqueeze(1).broadcast_to([128, 16, 8]),
               C("SM", 16).unsqueeze(2).broadcast_to([128, 16, 8]), ALU.mult)
            ps2 = bank("G")
            mm(ps2[:, 0:128], C("ONES"), gsq[:, :, :].rearrange("p s h -> p (s h)"))
            act(egls[:, :, :].rearrange("p s h -> p (s h)"), ps2[:, 0:128], AF.Exp)

    def bc4(ap):
        return ap.unsqueeze(2).broadcast_to([128, 4, 128])

    def hb(ap):
        return ap.unsqueeze(1).broadcast_to([128, 4, 128])

    def f4(ap):
        return ap.rearrange("p h n -> p (h n)")

    def v4(ap):
        return ap.rearrange("p (h n) -> p h n", h=4)

    def gdn_unit(tile, l, b, hg):
        samp = tile["kind"] == "s"
        H0 = hg * 4
        G = lambda a: gsm[:, b, a + H0:a + H0 + 4]
        blk = slice(b * 128, (b + 1) * 128)
        ab = hg * 16
        qT = lambda hh: arena[:, ab + 0 + hh, blk]
        kT = lambda hh: arena[:, ab + 4 + hh, blk]
        vT = lambda hh: arena[:, ab + 8 + hh, blk]
        Um = C("U_S") if samp else C("U")
        MSm = CB("MS_S") if samp else CB("MS")
        MIm = CB("MI_S") if samp else CB("MI")
        tt("dve", GL[:, :, :], bc4(G(8)), hb(CB("L")), ALU.mult)
        pd = bank("G")
        mm(pd[:, :], Um, f4(GL[:, :, :]))
        act(f4(Eraw[:, :, :]), pd[:, :], AF.Exp)
        tt("pool", Ems[:, :, :], Eraw[:, :, :], hb(MSm), ALU.mult)
        tt("pool", Emi[:, :, :], Eraw[:, :, :], hb(MIm), ALU.mult)
        pkk = bank("G")
        pqk = bank("G")
        for hh in range(4):
            mm(pkk[:, hh * 128:(hh + 1) * 128], kT(hh), kT(hh))
        for hh in range(4):
            mm(pqk[:, hh * 128:(hh + 1) * 128], qT(hh), kT(hh))
        for hh in range(4):
            stt(Nb[:, hh, :], pkk[:, hh * 128:(hh + 1) * 128], gsm[:, b, 64 + H0 + hh:64 + H0 + hh + 1], Ems[:, hh, :],
                ALU.mult, ALU.mult)
        tt("dve", f4(qkb[:, :, :]), pqk[:, :], f4(Emi[:, :, :]), ALU.mult)
        pt1 = bank("G")
        pt1b = pt1[:, :].bitcast(BF16)
        for hh in range(4):
            tr(pt1b[:, hh * 128:(hh + 1) * 128], Nb[:, hh, :], IDb)
        NTb = PTm[0]
        cp("act", f4(NTb[:, :, :]), pt1b[:, 0:512])
        pt2 = bank("G")
        pt2b = pt2[:, :].bitcast(BF16)
        for hh in range(4):
            tr(pt2b[:, hh * 128:(hh + 1) * 128], qkb[:, hh, :], IDb)
        cp("act", f4(qkT[:, :, :]), pt2b[:, 0:512])

        def mm4(lhs, rhs, add=None):
            p_ = bank("G")
            for hh in range(4):
                mm(p_[:, hh * 128:(hh + 1) * 128], lhs[:, hh, :], rhs[:, hh, :], start=True,
                   stop=(add is None or not PEADD))
                if add is not None and PEADD:
                    a_ = add if add is IDb else add[:, hh, :]
                    mm(p_[:, hh * 128:(hh + 1) * 128], IDb, a_, start=False, stop=True)
            return p_

        def evac_add(dst, p_, add, eng):
            if PEADD:
                cp(eng, f4(dst[:, :, :]), p_[:, :])
            elif add is IDb:
                tt("dve", dst[:, :, :], v4(p_[:, :]), hb(IDb), ALU.add)
            else:
                tt("dve", f4(dst[:, :, :]), p_[:, :], f4(add[:, :, :]), ALU.add)
        Nd, NdT = Pm[0], Pm[1]
        tt("dve", Nd[:, :, :], Nb[:, :, :], hb(CB("MD4")), ALU.mult)
        tt("dve", NdT[:, :, :], NTb[:, :, :], hb(CB("MD4")), ALU.mult)
        p1 = mm4(NdT, Nd, add=IDb)
        p2 = mm4(Nd, NdT, add=IDb)
        Q_, QT_, R_, RT_ = X1b, X2b, No_b, NoT_b
        evac_add(Q_, p1, IDb, "act")
        evac_add(QT_, p2, IDb, "act")
        tt("pool", R_[:, :, :], Nd[:, :, :], hb(CB("IDENT")), ALU.add)
        tt("pool", RT_[:, :, :], NdT[:, :, :], hb(CB("IDENT")), ALU.add)
        p3 = mm4(QT_, R_)
        p4 = mm4(R_, QT_)
        cp("act", f4(Tdn[:, :, :]), p3[:, :])
        cp("act", f4(TTb[:, :, :]), p4[:, :])
        levels = [4] if samp else [4, 8, 16, 32, 64]
        for li, m_ in enumerate(levels):
            last = (li == len(levels) - 1)
            tt("dve", No_b[:, :, :], Nb[:, :, :], hb(CB("ML%d" % m_)), ALU.mult)
            if not last:
                tt("dve", NoT_b[:, :, :], NTb[:, :, :], hb(CB("MLT%d" % m_)), ALU.mult)
                px1 = mm4(NoT_b, Tdn)
                cp("act", f4(X1b[:, :, :]), px1[:, :])
            px2 = mm4(No_b, TTb)
            cp("act", f4(X2b[:, :, :]), px2[:, :])
            if not last:
                py1 = mm4(TTb, X1b, add=Tdn)
            py2 = mm4(Tdn, X2b, add=TTb)
            if not last:
                evac_add(Tdn, py1, Tdn, "act")
            evac_add(TTb, py2, TTb, "dve")
        pk = bank("G")
        pkb = pk[:, :].bitcast(BF16)
        for hh in range(4):
            tr(pkb[:, hh * 128:(hh + 1) * 128], kT(hh), IDb)
        tt("dve", kbg[:, :, :], v4(pkb[:, 0:512]), bc4(G(40)), ALU.mult)
        tt("dve", kdec[:, :, :], v4(pkb[:, 0:512]), bc4(G(48)), ALU.mult)
        pv = bank("G")
        pvb = pv[:, :].bitcast(BF16)
        for hh in range(4):
            tr(pvb[:, hh * 128:(hh + 1) * 128], vT(hh), IDb)
        tt("dve", vbt[:, :, :], v4(pvb[:, 0:512]), bc4(G(72)), ALU.mult)
        pw = bank("G")
        for hh in range(4):
            mm(pw[:, hh * 128:(hh + 1) * 128], kbg[:, hh, :], TTb[:, hh, :])
        cp("act", f4(wTb[:, :, :]), pw[:, :])
        if not samp:
            pu = bank("G")
            for hh in range(4):
                mm(pu[:, hh * 128:(hh + 1) * 128], TTb[:, hh, :], vbt[:, hh, :])
            cp("act", f4(u_t[:, :, :]), pu[:, :])
            pws = bank("G")
            for hh in range(4):
                mm(pws[:, hh * 128:(hh + 1) * 128], wTb[:, hh, :], Sb[:, H0 + hh, :])
            tt("dve", f4(vnew[:, :, :]), f4(u_t[:, :, :]), pws[:, :], ALU.subtract)
            pqs = bank("G")
            for hh in range(4):
                mm(pqs[:, hh * 128:(hh + 1) * 128], qT(hh), Sb[:, H0 + hh, :])
            pin = bank("G")
            for hh in range(4):
                mm(pin[:, hh * 128:(hh + 1) * 128], qkT[:, hh, :], vnew[:, hh, :])
            tt("dve", tq[:, :, :], v4(pqs[:, :]), bc4(G(32)), ALU.mult)
            tt("dve", f4(o_t[:, :, :]), pin[:, :], f4(tq[:, :, :]), ALU.add)
            psu = bank("G")
            for hh in range(4):
                mm(psu[:, hh * 128:(hh + 1) * 128], kdec[:, hh, :], vnew[:, hh, :])
            Sv = S_all[:, l, H0:H0 + 4, :]
            tt("pool", Sv, Sv, bc4(G(56)), ALU.mult)
            tt("dve", Sv, v4(psu[:, :]), Sv, ALU.add)
            cp("act", Sb[:, H0:H0 + 4, :], Sv)
        else:
            pu = bank("G")
            for hh in range(4):
                mm(pu[:, hh * 128:(hh + 1) * 128], vbt[:, hh, :], TTb[:, hh, :])
            cp("act", f4(uT[:, :, :]), pu[:, :])
            pws = bank("G")
            pqs = bank("G")
            for sg in range(8):
                Sf_, Sb_ = S_all[:, sg % 2, :, :], Ssb[sg % 2]
                for si_ in range(2):
                    dma("sp", Sf_[:, si_ * 4:si_ * 4 + 4, :],
                        i_sgdn[l, sg * 2 + si_, H0:H0 + 4, :, :].rearrange("h d v -> d h v"))
                cp("pool", Sb_[:, :, :], Sf_[:, :, :])
                for si in range(2):
                    s = sg * 2 + si
                    cols = slice(s * 8, s * 8 + 8)
                    for hh in range(4):
                        mm(pws[:, hh * 128 + s * 8:hh * 128 + s * 8 + 8], Sb_[:, si * 4 + hh, :], wTb[:, hh, cols],
                           start=True, stop=True, skip=True)
                        mm(pqs[:, hh * 128 + s * 8:hh * 128 + s * 8 + 8], Sb_[:, si * 4 + hh, :],
                           arena[:, ab + hh, b * 128 + s * 8:b * 128 + s * 8 + 8], start=True, stop=True, skip=True)
            tt("dve", f4(vnT[:, :, :]), f4(uT[:, :, :]), pws[:, :], ALU.subtract)
            cp("act", f4(qsT[:, :, :]), pqs[:, :])
            pt3 = bank("G")
            pt3b = pt3[:, :].bitcast(BF16)
            for hh in range(4):
                tr(pt3b[:, hh * 128:(hh + 1) * 128], vnT[:, hh, :], IDb)
            cp("act", f4(vnew[:, :, :]), pt3b[:, 0:512])
            pt4 = bank("G")
            pt4b = pt4[:, :].bitcast(BF16)
            for hh in range(4):
                tr(pt4b[:, hh * 128:(hh + 1) * 128], qsT[:, hh, :], IDb)
            tt("dve", tq[:, :, :], v4(pt4b[:, 0:512]), bc4(G(32)), ALU.mult)
            pin = bank("G")
            for hh in range(4):
                mm(pin[:, hh * 128:(hh + 1) * 128], qkT[:, hh, :], vnew[:, hh, :])
            tt("dve", f4(o_t[:, :, :]), pin[:, :], f4(tq[:, :, :]), ALU.add)
            for sg in range(8):
                Sf_ = S_all[:, sg % 2, :, :]
                for si_ in range(2):
                    dma("sp", Sf_[:, si_ * 4:si_ * 4 + 4, :],
                        i_sgdn[l, sg * 2 + si_, H0:H0 + 4, :, :].rearrange("h d v -> d h v"))
                for si in range(2):
                    s = sg * 2 + si
                    km = kdm[s % 2]
                    ts("dve", km[:, :, :], kdec[:, :, :], C("SM", 16)[:, s:s + 1], ALU.mult)
                    psu = bank("G")
                    for hh in range(4):
                        mm(psu[:, hh * 128:(hh + 1) * 128], km[:, hh, :], vnew[:, hh, :])
                    sn = S_all[:, 2, (s % 2) * 4:(s % 2) * 4 + 4, :]
                    tt("pool", sn[:, :, :], Sf_[:, si * 4:si * 4 + 4, :], bc4(egls[:, s, H0:H0 + 4]), ALU.mult)
                    tt("dve", sn[:, :, :], v4(psu[:, :]), sn[:, :, :], ALU.add)
                    dst = o_gdns[l, s, H0:H0 + 4, :, :].rearrange("h d v -> d h v")
                    dma("sp", dst, sn[:, :, :])
        tt("pool", osq[:, :, :], o_t[:, :, :], o_t[:, :, :], ALU.mult)
        red(osm[:, 0:4], osq[:, :, :], ALU.add)
        rsqrt_ln(osm[:, 4:8], osm[:, 0:4], 1.0 / 128.0, EPS)
        tt("dve", on_b[:, :, :], o_t[:, :, :], bc4(osm[:, 4:8]), ALU.mult)
        po = bank("G")
        pob = po[:, :].bitcast(BF16)
        for hh in range(4):
            tr(pob[:, hh * 128:(hh + 1) * 128], on_b[:, hh, :], IDb)
        stt(oaT[:, H0:H0 + 4, blk], v4(pob[:, 0:512]), spd[:, l, 0:1], arena[:, ab + 12:ab + 16, blk], ALU.mult, ALU.mult)

    def swa_unit(tile, l, qb, kvh):
        samp = tile["kind"] == "s"
        qa = qaT
        qs = slice(qb * 128, (qb + 1) * 128)
        heads = [4 * kvh + 0, 4 * kvh + 2, 4 * kvh + 1, 4 * kvh + 3]
        kbs = []
        if not samp and not (tile["first"] and qb == 0):
            kbs.append((qb, C("DO")))
        kbs.append((qb + 1, C("DN_S") if samp else C("DD")))
        pts = []
        for i, (kb, Dm) in enumerate(kbs):
            pse, pso = bank(), bank()
            ks = slice(kb * 128, (kb + 1) * 128)
            mm(pse[:, 0:256], kdup[0:64, kvh, ks], qa[0:64, 2 * kvh:2 * kvh + 2, qs])
            mm(pso[:, 0:256], kdup[64:128, kvh, ks], qa[64:128, 2 * kvh:2 * kvh + 2, qs])
            s_ = sc[i]
            for j in range(4):
                src = (pse if j < 2 else pso)[:, (j % 2) * 128:(j % 2 + 1) * 128]
                stt(s_[:, j, :], Dm, -SLOPES[heads[j]], src, ALU.mult, ALU.add)
            act(PTa[i][:, :, :], s_[:, :, :], AF.Exp)
            pts.append((PTa[i], vext[:, kb, kvh * 64:(kvh + 1) * 64]))
        if samp:
            dma("pool", kcb[:, :, :], i_kcT[l, kvh])
            dma("pool", vcb[:, :, :], i_vc[l, :, :, kvh * 64:(kvh + 1) * 64].rearrange("s k d -> k s d"))
            pse, pso = bank(), bank()
            psve = pse[:, 0:256].rearrange("p (s j t) -> p s j t", s=16, j=2)
            psvo = pso[:, 0:256].rearrange("p (s j t) -> p s j t", s=16, j=2)
            for s in range(16):
                cols = slice(s * 8, s * 8 + 8)
                mm(psve[:, s, :, :], kcb[0:64, s, :], qa[0:64, 2 * kvh:2 * kvh + 2, cols])
                mm(psvo[:, s, :, :], kcb[64:128, s, :], qa[64:128, 2 * kvh:2 * kvh + 2, cols])
            for j in range(4):
                src = (psve if j < 2 else psvo)[:, :, j % 2, :]
                stt(scs[:, :, j, :], C("DC", 8).unsqueeze(1).broadcast_to([128, 16, 8]), -SLOPES[heads[j]],
                    src, ALU.mult, ALU.add)
            act(PTc[:, :, :, 0:8], scs[:, :, :, :], AF.Exp)
        nd = bank()
        ndv = nd[:, :].rearrange("p (a c q) -> p a c q", a=2, c=2)
        for a in range(2):
            for par in range(2):
                prt = slice(par * 64, par * 64 + 64)
                for i, (P_, v_) in enumerate(pts):
                    lhs = v_ if a == 0 else ONESb[:, 0:64]
                    mm(ndv[prt, a, :, :], lhs, P_[:, 2 * par:2 * par + 2, :], start=(i == 0),
                       stop=(i == len(pts) - 1 and not samp), skip=samp)
                if samp:
                    for s in range(16):
                        lhs = vcb[:, s, :] if a == 0 else ONESb[:, 0:64]
                        for c in range(2):
                            mm(ndv[prt, a, c, s * 8:s * 8 + 8], lhs, PTc[:, s, 2 * par + c, 0:8], start=False,
                               stop=(s == 15 and c == 1), skip=True)
        for c in range(2):
            ts("dve", rden[:, c, :], ndv[:, 1, c, :], spd[:, l, 2 + 2 * kvh + c:3 + 2 * kvh + c], ALU.add)
        recip(rden[:, :, :], rden[:, :, :])
        tt("dve", obT[:, 2 * kvh:2 * kvh + 2, qs], ndv[:, 0, :, :], rden[:, :, :], ALU.mult)

    def interleave(g1, g2):
        a_, b_ = g1, g2
        while a_ is not None or b_ is not None:
            if a_ is not None:
                try:
                    next(a_)
                except StopIteration:
                    a_ = None
            if b_ is not None:
                try:
                    next(b_)
                except StopIteration:
                    b_ = None

    def drive():
        units = []
        i = 0
        while i < len(tiles):
            if i + 1 < len(tiles) and tiles[i]["kind"] == "p" and tiles[i + 1]["kind"] == "p" and not cfg.nopair:
                for l in range(DEPTH):
                    units.append((tiles[i], l))
                    units.append((tiles[i + 1], l))
                i += 2
            else:
                for l in range(DEPTH):
                    units.append((tiles[i], l))
                i += 1
        prevF, prev_tile = None, None
        for (tile, l) in units:
            if prevF is not None and prev_tile is tile:
                interleave(prevF, None)
                prevF = None
            if l == 0:
                dma("sp", hbuf[tile["idx"] % 2][:, :, 0:tile["nt"]], xT[:, :, tile["t0"]:tile["t0"] + tile["nt"]])
            interleave(gen_M(tile, l), prevF)
            prevF, prev_tile = gen_F(tile, l), tile
        interleave(prevF, None)

    n_setup = len(R.ops)
    R.dry = True
    drive()
    R.dry = False
    assert len(R.ops) == n_setup
    bank_i[0] = 0
    bank_f[0] = 0
    bank_g[0] = 0
    wstate["next_load"] = 0
    wstate["next_use"] = 0
    drive()
    assert wstate["next_use"] == len(plan)
    R.emit(st)
    return nc, st, R, dbg_out


def pack_weights(inp, depth):
    offs, TOT = group_offsets()
    out = np.zeros((depth, 128, TOT), np.float32)
    for l in range(depth):
        for (name, src, rows, cols) in weight_groups():
            W = inp[src][l]
            if isinstance(rows, tuple):
                W = W[rows[0]:rows[0] + rows[1]]
            M = W[:, cols]
            kc = M.shape[0] // 128
            o, _, ncol = offs[name]
            out[l, :, o:o + kc * ncol] = M.reshape(kc, 128, ncol).transpose(1, 0, 2).reshape(128, kc * ncol)
    return out


def pack_small(inp, depth):
    sp = np.zeros((128, depth, NSP), np.float32)
    for l in range(depth):
        sp[:, l, SP_NM:SP_NM + 8] = inp["norm_mix"][l].reshape(8, 128).T
        sp[:, l, SP_NF:SP_NF + 8] = inp["norm_ffn"][l].reshape(8, 128).T
        sp[:, l, SP_NP:SP_NP + 8] = inp["norm_ple"][l].reshape(8, 128).T
        cw = inp["conv_w"][l]
        sp[:, l, SP_CW:SP_CW + 96] = cw.reshape(4, 24, 128).transpose(2, 1, 0).reshape(128, 96)
        sp[:, l, SP_GN] = inp["gdn_norm"][l]
        sp[:, l, SP_QN] = np.tile(inp["q_norm"][l], 2)
        sp[:, l, SP_KN] = np.tile(inp["k_norm"][l], 2)
        sk = inp["attn_sinks"][l]
        for c in range(8):
            sp[0:64, l, SP_SK + c] = sk[2 * c]
            sp[64:128, l, SP_SK + c] = sk[2 * c + 1]
        sp[:, l, SP_AL:SP_AL + 8] = inp["a_log"][l][None, :]
        sp[:, l, SP_DT:SP_DT + 8] = inp["dt_bias"][l][None, :]
    return sp


def fm(x):
    T, Dm = x.shape
    return np.ascontiguousarray(x.reshape(T, Dm // 128, 128).transpose(2, 1, 0))


def unfm(y):
    p, k, T = y.shape
    return np.ascontiguousarray(y.transpose(2, 1, 0).reshape(T, k * 128))


def make_in_maps(inp, cfg, ncores):
    depth = cfg.depth
    wsrc = pack_weights(inp, depth)
    spar = pack_small(inp, depth)
    cst = make_consts()
    maps = []
    for c in range(ncores):
        xs = [inp["x_prompt"][c, :cfg.seq]]
        ps = [inp["p_prompt"][:depth, c, :cfg.seq]]
        if cfg.sample:
            xs.append(inp["x_sample"][16 * c:16 * c + 16].reshape(128, D_MODEL))
            ps.append(inp["p_sample"][:depth, 16 * c:16 * c + 16].reshape(depth, 128, 256))
        x = np.concatenate(xs, 0)
        p = np.concatenate(ps, 1)
        m = {"xT": fm(x), "pT": np.stack([fm(p[l]) for l in range(depth)]), "wsrc": wsrc, "spar": spar,
             "cst": cst[0], "cstm": cst[1]}
        if cfg.sample:
            sl = slice(16 * c, 16 * c + 16)
            cc = inp["cache_conv"][:depth, sl]
            m["i_cconv"] = np.ascontiguousarray(cc.reshape(depth, 16, 3, 24, 128).transpose(0, 4, 3, 1, 2))
            m["i_sgdn"] = np.ascontiguousarray(inp["state_gdn"][:depth, sl])
            kc = inp["cache_swa_k"][:depth, sl]
            kt = kc.transpose(0, 3, 4, 1, 2)
            m["i_kcT"] = np.ascontiguousarray(np.concatenate([kt, kt], axis=2))
            m["i_vc"] = np.ascontiguousarray(inp["cache_swa_v"][:depth, sl].reshape(depth, 16, 128, 256))
            m["i_kc"] = np.ascontiguousarray(kc.reshape(depth, 16, 128, 256))
        maps.append(m)
    return maps


def assemble(results, cfg, ncores):
    depth, seq = cfg.depth, cfg.seq
    yp = np.zeros((ncores, seq, D_MODEL), np.float32)
    convp = np.zeros((depth, ncores, 3, 3072), np.float32)
    gdnp = np.zeros((depth, ncores, 8, 128, 128), np.float32)
    kp = np.zeros((depth, ncores, 128, 4, 64), np.float32)
    vp = np.zeros((depth, ncores, 128, 4, 64), np.float32)
    if cfg.sample:
        ys = np.zeros((ncores * 16, 8, D_MODEL), np.float32)
        convs = np.zeros((depth, ncores * 16, 3, 3072), np.float32)
        gdns = np.zeros((depth, ncores * 16, 8, 128, 128), np.float32)
        ks = np.zeros((depth, ncores * 16, 128, 4, 64), np.float32)
        vs = np.zeros((depth, ncores * 16, 128, 4, 64), np.float32)
    for c in range(ncores):
        r = results[c]
        y = unfm(r["yT"])
        yp[c] = y[:seq]
        convp[:, c] = r["o_convp"].transpose(0, 3, 2, 1).reshape(depth, 3, 3072)
        gdnp[:, c] = r["o_gdnp"].transpose(0, 2, 1, 3)
        kp[:, c] = r["o_kp"][:, 0:64].transpose(0, 3, 2, 1)
        vp[:, c] = r["o_vp"].reshape(depth, 128, 4, 64)
        if cfg.sample:
            sl = slice(16 * c, 16 * c + 16)
            ys[sl] = y[seq:].reshape(16, 8, D_MODEL)
            convs[:, sl] = r["o_convs"].transpose(0, 3, 4, 2, 1).reshape(depth, 16, 3, 3072)
            gdns[:, sl] = r["o_gdns"]
            ks[:, sl, 0:120] = r["o_kcopy"].reshape(depth, 16, 120, 4, 64)
            vs[:, sl, 0:120] = r["o_vcopy"].reshape(depth, 16, 120, 4, 64)
            kn = r["o_ksn"][:, 0:64].transpose(0, 3, 2, 1)
            ks[:, sl, 120:128] = kn.reshape(depth, 16, 8, 4, 64)
            vs[:, sl, 120:128] = r["o_vsn"].reshape(depth, 16, 8, 4, 64)
    if cfg.sample:
        return (yp, ys, convp, gdnp, kp, vp, convs, gdns, ks, vs)
    return (yp, convp, gdnp, kp, vp)


_CACHE = {}


def kernel(**inputs):
    inp = {k: np.asarray(v) for k, v in inputs.items()}
    cfg = Cfg()
    if "prog" not in _CACHE:
        _CACHE["prog"] = build(cfg)
    nc, st, R, _ = _CACHE["prog"]
    maps = make_in_maps(inp, cfg, 8)
    res = run_bass_kernel_spmd(nc, maps, core_ids=list(range(8)))
    return assemble(res.results, cfg, 8)
```

```python
import contextlib
import numpy as np
import concourse.bass as bass
import concourse.mybir as mybir
from concourse.bass_utils import run_bass_kernel_spmd

F32 = mybir.dt.float32
BF16 = mybir.dt.bfloat16
AF = mybir.ActivationFunctionType
ALU = mybir.AluOpType
AX = mybir.AxisListType

D_MODEL = 1024
KC = 8
EPS = 1e-6
BIG = 1.0e6
COMPUTE = ("pe", "act", "dve", "pool")
SLOPES = [float(2.0 ** (-8.0 * (h + 1) / 16.0)) for h in range(16)]


def _ap_box(ap):
    t = ap.tensor
    name = t.name
    dims = [(s, n) for (s, n) in ap.ap]
    off = ap.offset
    tn = type(t).__name__
    if "PSum" in tn:
        return (name, 0, 128, 0, 1 << 30, True)
    if "DRam" in tn:
        nz = [(abs(s), n) for (s, n) in dims if n > 1 and s != 0]
        if nz:
            smax, nmax = max(nz)
            rest = [(s, n) for (s, n) in dims if not (abs(s) == smax and n == nmax)]
            ext = sum(abs(s) * (n - 1) for (s, n) in rest if n > 1) + 1
            f0 = off % smax
            if len(rest) == len(dims) - 1 and f0 + ext <= smax and all(s >= 0 for s, n in dims):
                p_lo = off // smax
                return (name, p_lo, p_lo + nmax, f0, f0 + ext, False)
        lo = hi = off
        for s, n in dims:
            if n > 1:
                if s >= 0:
                    hi += s * (n - 1)
                else:
                    lo += s * (n - 1)
        return (name, 0, 1 << 30, lo, hi + 1, False)
    pst, pn = dims[0]
    if pst == 0:
        p_lo, f0 = 0, off
    else:
        p_lo = off // pst
        f0 = off - p_lo * pst
    lo = hi = f0
    for s, n in dims[1:]:
        if n > 1:
            if s >= 0:
                hi += s * (n - 1)
            else:
                lo += s * (n - 1)
    return (name, p_lo, p_lo + pn, lo, hi + 1, False)


class Op:
    __slots__ = ("eng", "fn", "reads", "writes", "deps", "signal", "count", "is_dma", "slot", "dval", "idx",
                 "dur", "succ", "fin", "st", "why", "tag", "aset", "lat")

    def __init__(self, eng, fn, reads, writes, is_dma, dur):
        self.eng = eng
        self.fn = fn
        self.reads = reads
        self.writes = writes
        self.deps = ()
        self.signal = False
        self.count = 0
        self.is_dma = is_dma
        self.slot = None
        self.dval = 0
        self.idx = -1
        self.dur = dur
        self.succ = []
        self.fin = 0.0
        self.st = 0.0
        self.why = None
        self.tag = ""
        self.aset = None


def _ov(a, b):
    return a[1] < b[2] and b[1] < a[2] and a[3] < b[4] and b[3] < a[4]


def _cov(b, e):
    return b[1] <= e[1] and e[2] <= b[2] and b[3] <= e[3] and e[4] <= b[4]


class Rec:
    SEM_LAT = 250.0

    def __init__(self, nc, slots):
        self.nc = nc
        self.ops = []
        self.wr = {}
        self.rd = {}
        self.slots = slots
        self.const_names = set()
        self.cur_tag = ""
        self.dry = False
        self.dma_hist = {}
        self.makespan = 0.0
        import os
        self.maxops = int(os.environ["K_MAXOPS"]) if "K_MAXOPS" in os.environ else None
        self.sched = os.environ.get("K_NOSCHED") is None

    def add(self, eng, fn, reads, writes, is_dma=False, dur=100.0, extra=()):
        if self.dry or (self.maxops is not None and len(self.ops) >= self.maxops):
            return None
        op = Op(eng, fn, [_ap_box(a) for a in reads], [_ap_box(a) for a in writes], is_dma, dur)
        op.idx = len(self.ops)
        op.tag = self.cur_tag
        self.ops.append(op)
        deps = set()
        for b in op.reads:
            for (ob, oop) in self.wr.get(b[0], ()):
                if _ov(ob, b):
                    deps.add(oop)
            if b[5]:
                for (ob, oop) in self.rd.get(b[0], ()):
                    if oop.eng != eng and _ov(ob, b):
                        deps.add(oop)
        for b in op.writes:
            for (ob, oop) in self.wr.get(b[0], ()):
                if _ov(ob, b):
                    deps.add(oop)
            for (ob, oop) in self.rd.get(b[0], ()):
                if _ov(ob, b):
                    deps.add(oop)
        for x in extra:
            if x is not None:
                deps.add(x)
        if is_dma:
            hist = self.dma_hist.setdefault(eng, [])
            ns = self.slots[eng]
            if len(hist) >= ns:
                deps.add(hist[-ns])
            hist.append(op)
        deps.discard(op)
        op.deps = tuple(deps)
        for d in deps:
            d.succ.append(op)
        for b in op.writes:
            for dct in (self.wr, self.rd):
                lst = dct.get(b[0])
                if lst:
                    lst[:] = [e for e in lst if not _cov(b, e[0])]
            self.wr.setdefault(b[0], []).append((b, op))
        for b in op.reads:
            if b[0] in self.const_names:
                continue
            self.rd.setdefault(b[0], []).append((b, op))
        return op

    def schedule(self):
        import heapq
        ops = self.ops
        indeg = [len(o.deps) for o in ops]
        heaps = {}
        free = {}
        L = self.SEM_LAT

        import os
        fbonus = float(os.environ.get("K_FBONUS", "0"))

        tru = {}

        def push(o, t):
            heaps.setdefault(o.eng, [])
            tru[o.idx] = t
            if fbonus and o.tag in ("7ffn", "8ple"):
                t = t - fbonus
            heapq.heappush(heaps[o.eng], (t, o.idx))

        for o in ops:
            if indeg[o.idx] == 0:
                push(o, 0.0)
        order = []
        n = len(ops)
        hp_t = {}
        last_on = {}
        cur_set = ["A"]
        while len(order) < n:
            best = None
            for e, hp in heaps.items():
                if not hp:
                    continue
                t, i = hp[0]
                stt_ = max(t, free.get(e, 0.0))
                if best is None or (stt_, i) < best[0]:
                    best = ((stt_, i), e)
            (stt_, i), e = best
            stt_ = max(tru[i], free.get(e, 0.0))
            if e == "act" and ops[i].aset is not None and ops[i].aset != cur_set[0] and len(heaps[e]) > 1:
                cands = heapq.nsmallest(6, heaps[e])
                pick = None
                for (t_, j_) in cands:
                    if ops[j_].aset in (None, cur_set[0]) and max(t_, free.get(e, 0.0)) <= stt_ + 1300.0:
                        pick = (t_, j_)
                        break
                if pick is not None:
                    heaps[e].remove(pick)
                    heapq.heapify(heaps[e])
                    i = pick[1]
                    stt_ = max(tru[i], free.get(e, 0.0))
                else:
                    heapq.heappop(heaps[e])
            else:
                heapq.heappop(heaps[e])
            o = ops[i]
            if e == "act" and o.aset is not None:
                if o.aset != cur_set[0]:
                    stt_ += 1300.0
                cur_set[0] = o.aset
            order.append(o)
            o.st = stt_
            if stt_ > hp_t.get(o.idx, 0.0) + 1e-9:
                o.why = last_on.get(e)
            last_on[e] = o
            if o.is_dma:
                free[e] = stt_ + 60.0
            else:
                free[e] = stt_ + o.dur
            o.fin = stt_ + o.dur
            for sc_ in o.succ:
                indeg[sc_.idx] -= 1
                if indeg[sc_.idx] == 0:
                    rt = 0.0
                    for d in sc_.deps:
                        lat = 0.0 if (d.eng == "pe" and sc_.eng == "pe" and not d.is_dma and not sc_.is_dma) else L
                        if d.fin + lat > rt:
                            rt = d.fin + lat
                            sc_.why = d
                    hp_t[sc_.idx] = rt
                    push(sc_, rt)
        self.makespan = max(o.fin for o in ops)
        return order

    def schedule_hlfet(self):
        ops = self.ops
        L = self.SEM_LAT
        n = len(ops)
        bl = [0.0] * n
        for o in reversed(ops):
            m = 0.0
            for sc_ in o.succ:
                lat = 0.0 if (o.eng == "pe" and sc_.eng == "pe" and not o.is_dma and not sc_.is_dma) else L
                v = lat + bl[sc_.idx]
                if v > m:
                    m = v
            bl[o.idx] = o.dur + m
        indeg = [len(o.deps) for o in ops]
        ready = {}
        rt = {}
        free = {}
        for o in ops:
            if indeg[o.idx] == 0:
                ready.setdefault(o.eng, []).append(o.idx)
                rt[o.idx] = 0.0
        order = []
        cur_set = "A"
        while len(order) < n:
            best = None
            for e, lst in ready.items():
                if not lst:
                    continue
                f = free.get(e, 0.0)
                avail = [i for i in lst if rt[i] <= f]
                if avail:
                    if e == "act":
                        same = [i for i in avail if ops[i].aset in (None, cur_set)]
                        if same:
                            avail = same
                    c = max(avail, key=lambda i: (bl[i], -i))
                    stt_ = f
                else:
                    c = min(lst, key=lambda i: (rt[i], -bl[i]))
                    stt_ = rt[c]
                key = (stt_, -bl[c])
                if best is None or key < best[0]:
                    best = (key, e, c, stt_)
            _, e, c, stt_ = best
            ready[e].remove(c)
            o = ops[c]
            if e == "act" and o.aset is not None:
                if o.aset != cur_set:
                    stt_ += 1300.0
                cur_set = o.aset
            order.append(o)
            o.st = stt_
            free[e] = stt_ + (60.0 if o.is_dma else o.dur)
            o.fin = stt_ + o.dur
            for sc_ in o.succ:
                indeg[sc_.idx] -= 1
                if indeg[sc_.idx] == 0:
                    r = 0.0
                    for d in sc_.deps:
                        lat = 0.0 if (d.eng == "pe" and sc_.eng == "pe" and not d.is_dma and not sc_.is_dma) else L
                        if d.fin + lat > r:
                            r = d.fin + lat
                    rt[sc_.idx] = r
                    ready.setdefault(sc_.eng, []).append(sc_.idx)
        self.makespan = max(o.fin for o in ops)
        return order

    def emit(self, stack):
        nc = self.nc
        engs = {"pe": nc.tensor, "act": nc.scalar, "dve": nc.vector, "pool": nc.gpsimd, "sp": nc.sync}
        import os
        if not self.sched:
            order = list(self.ops)
        elif os.environ.get("K_SCHED", "hlfet") == "hlfet":
            order = self.schedule_hlfet()
        else:
            order = self.schedule()
        pos = {}
        for k, op in enumerate(order):
            pos[op.idx] = k
        for op in order:
            latest = {}
            for d in op.deps:
                if d.is_dma:
                    continue
                if d.eng == op.eng and not op.is_dma and d.eng == "pe":
                    continue
                cur = latest.get(d.eng)
                if cur is None or pos[d.idx] > pos[cur.idx]:
                    latest[d.eng] = d
            op.lat = latest
            for d in latest.values():
                d.signal = True
        cnt = {e: 0 for e in COMPUTE}
        dq = {}
        for op in order:
            if op.is_dma:
                k = dq.get(op.eng, 0)
                dq[op.eng] = k + 1
                ns = self.slots[op.eng]
                op.slot = (op.eng, k % ns)
                op.dval = 16 * (k // ns + 1)
            elif op.signal:
                cnt[op.eng] += 1
                op.count = cnt[op.eng]
        sems = {}
        for e in COMPUTE:
            sems[e] = stack.enter_context(nc.semaphore("s_" + e))
        for q in dq:
            for k in range(min(self.slots[q], dq[q])):
                sems[(q, k)] = stack.enter_context(nc.semaphore("d_%s_%d" % (q, k)))
        waited = {}
        nwaits = 0
        for op in order:
            e = engs[op.eng]
            need = {}
            for d in op.deps:
                if d.is_dma:
                    key, val = d.slot, d.dval
                    if need.get(key, 0) < val:
                        need[key] = val
            for eng_, d in op.lat.items():
                need[eng_] = d.count
            if op.is_dma and op.dval > 16:
                if need.get(op.slot, 0) < op.dval - 16:
                    need[op.slot] = op.dval - 16
            for key, val in need.items():
                wk = (op.eng, key)
                if waited.get(wk, 0) >= val:
                    continue
                waited[wk] = val
                e.wait_ge(sems[key], val)
                nwaits += 1
            ins = op.fn(e)
            if op.is_dma:
                ins.then_inc(sems[op.slot], 16)
            elif op.signal:
                ins.then_inc(sems[op.eng], 1)
        for q, n in dq.items():
            e = engs[q]
            ns = self.slots[q]
            for k in range(min(n, ns)):
                e.wait_ge(sems[(q, k)], 16 * ((n - 1 - k) // ns + 1))
        for ce in COMPUTE:
            if cnt[ce] > 0:
                nc.sync.wait_ge(sems[ce], cnt[ce])
        self.stats = dict(nops=len(self.ops), nwaits=nwaits, cnt=cnt, dq=dq, makespan_us=self.makespan / 1000.0)


C_QKV, C_Z, C_B, C_A, C_SQ, C_SK, C_SV, C_G = 0, 3072, 4096, 4104, 4112, 5136, 5392, 5648


def weight_groups():
    g = []
    r = lambda a, n: list(range(a, a + n))
    g.append(("tm", "w_in", 1024, r(C_B, 16) + r(C_SV, 256)))
    for hg in range(2):
        g.append(("q%d" % hg, "w_in", 1024, r(C_QKV + hg * 512, 512)))
        g.append(("k%d" % hg, "w_in", 1024, r(C_QKV + 1024 + hg * 512, 512)))
        g.append(("v%d" % hg, "w_in", 1024, r(C_QKV + 2048 + hg * 512, 512)))
        g.append(("z%d" % hg, "w_in", 1024, r(C_Z + hg * 512, 512)))
    for i in range(2):
        g.append(("sq%d" % i, "w_in", 1024, r(C_SQ + i * 512, 512)))
    cols = []
    for j in range(4):
        cols += r(C_SK + j * 64, 64) + r(C_SK + j * 64, 64)
    g.append(("skd", "w_in", 1024, cols))
    for i in range(2):
        g.append(("ga%d" % i, "w_in", 1024, r(C_G + i * 512, 512)))
        g.append(("gb%d" % i, "w_in", 1024, r(C_G + 1024 + i * 512, 512)))
    for i in range(2):
        g.append(("wo%d" % i, "w_out", 1024, r(i * 512, 512)))
    for hh in range(2):
        for gi in range(4):
            g.append(("up%d_%d" % (hh, gi), "w_up", 1024, r((hh * 16 + gi * 4) * 128, 512)))
        for cb in range(4):
            g.append(("dn%d_%d" % (hh, cb), "w_down", (hh * 2048, 2048), r(cb * 256, 256)))
    for i in range(2):
        g.append(("pg%d" % i, "w_ple_gate", 1024, r(i * 512, 512)))
    g.append(("pp", "w_ple_proj", 256, r(0, 1024)))
    return g


def group_offsets():
    offs = {}
    o = 0
    for (name, src, rows, cols) in weight_groups():
        k = rows[1] if isinstance(rows, tuple) else rows
        n = (k // 128) * len(cols)
        offs[name] = (o, k // 128, len(cols))
        o += n
    return offs, o


SP_NM, SP_NF, SP_NP, SP_CW, SP_GN, SP_QN, SP_KN, SP_SK, SP_AL, SP_DT = 0, 8, 16, 24, 120, 121, 122, 123, 131, 139
NSP = 147

CSTF_NAMES = ["ONES", "U", "U_S", "ONESSEQ_S", "DD", "DO", "DN_S"]
CSTF = {n: i * 128 for i, n in enumerate(CSTF_NAMES)}
CSTF["DC"] = len(CSTF_NAMES) * 128
CSTF["SM"] = CSTF["DC"] + 8
NCSTF = CSTF["SM"] + 16
CSTB_NAMES = ["IDENT", "ONES", "ONESB64", "L", "MS", "MI", "MS_S", "MI_S", "MD4", "ML4", "MLT4", "ML8", "MLT8",
              "ML16", "MLT16", "ML32", "MLT32", "ML64", "MLT64"]
CSTB = {n: i * 128 for i, n in enumerate(CSTB_NAMES)}
NCSTB = len(CSTB_NAMES) * 128


def make_consts():
    cf = np.zeros((128, NCSTF), np.float32)
    cb = np.zeros((128, NCSTB), np.float32)
    i = np.arange(128)
    P, Fq = np.meshgrid(i, i, indexing="ij")
    same = (P // 8) == (Fq // 8)

    def put(n, m):
        m = np.asarray(m, np.float32)
        if n in CSTF:
            cf[:, CSTF[n]:CSTF[n] + m.shape[1]] = m
        if n in CSTB:
            cb[:, CSTB[n]:CSTB[n] + m.shape[1]] = m
    put("IDENT", P == Fq)
    put("ONES", np.ones((128, 128)))
    put("ONESB64", (P // 64) == (Fq // 64))
    put("U", P <= Fq)
    put("L", P > Fq)
    put("MS", P > Fq)
    put("MI", P >= Fq)
    put("U_S", same & (P <= Fq))
    put("MS_S", same & (P > Fq))
    put("MI_S", same & (P >= Fq))
    put("ONESSEQ_S", same)
    put("DD", np.where(Fq >= P, Fq - P, BIG))
    put("DO", np.where(Fq <= P, Fq + 128 - P, BIG))
    put("DN_S", np.where(same & (Fq >= P), Fq - P, BIG))
    put("MD4", (P // 4) == (Fq // 4))
    for m in (4, 8, 16, 32, 64):
        ml = ((P // (2 * m)) == (Fq // (2 * m))) & ((P // m) == (Fq // m) + 1)
        put("ML%d" % m, ml)
        put("MLT%d" % m, ml.T)
    j = np.arange(128)[:, None]
    t = np.arange(8)[None, :]
    put("DC", np.where(j >= t, 128 + t - j, BIG))
    put("SM", (np.arange(128)[:, None] // 8) == np.arange(16)[None, :])
    return cf, cb


class Cfg:
    def __init__(self, depth=4, seq=2048, tt=256, sample=True, stages=("gdn", "swa", "ffn", "ple"), dbg=()):
        self.depth = depth
        self.seq = seq
        self.tt = tt
        self.sample = sample
        self.ntok = seq + (128 if sample else 0)
        self.stages = stages
        self.dbg = dbg
        import os
        self.nopair = os.environ.get("K_NOPAIR") is not None


def build(cfg):
    nc = bass.Bass("TRN2", target_bir_lowering=False)
    st = contextlib.ExitStack()
    DEPTH, SEQ, TT, NTOK = cfg.depth, cfg.seq, cfg.tt, cfg.ntok
    offs, TOT = group_offsets()
    import os as _os
    PEADD = _os.environ.get("K_PEADD", "0") == "1"
    SQPOOL = _os.environ.get("K_SQPOOL", "0") == "1"
    R = Rec(nc, {"sp": 12, "pool": 3, "act": 4})

    def din(name, shape, dt=F32):
        return nc.dram_tensor(name, list(shape), dt, kind="ExternalInput").ap()

    def dout(name, shape, dt=F32):
        return nc.dram_tensor(name, list(shape), dt, kind="ExternalOutput").ap()

    def sb(name, shape, dt=F32):
        return st.enter_context(nc.sbuf_tensor(name, list(shape), dt))

    xT = din("xT", [128, KC, NTOK])
    pT = din("pT", [DEPTH, 128, 2, NTOK])
    wsrc = din("wsrc", [DEPTH, 128, TOT])
    spar = din("spar", [128, DEPTH, NSP])
    cstd = din("cst", [128, NCSTF])
    cstmd = din("cstm", [128, NCSTB])
    wbf = nc.dram_tensor("wbf", [DEPTH, 128, TOT], BF16, kind="Internal").ap()
    yT = dout("yT", [128, KC, NTOK])
    o_convp = dout("o_convp", [DEPTH, 128, 24, 3])
    o_gdnp = dout("o_gdnp", [DEPTH, 128, 8, 128])
    o_kp = dout("o_kp", [DEPTH, 128, 4, 128])
    o_vp = dout("o_vp", [DEPTH, 128, 256])
    if cfg.sample:
        i_cconv = din("i_cconv", [DEPTH, 128, 24, 16, 3])
        i_sgdn = din("i_sgdn", [DEPTH, 16, 8, 128, 128])
        i_kcT = din("i_kcT", [DEPTH, 4, 128, 16, 128])
        i_vc = din("i_vc", [DEPTH, 16, 128, 256])
        i_kc = din("i_kc", [DEPTH, 16, 128, 256])
        o_convs = dout("o_convs", [DEPTH, 128, 24, 16, 3])
        o_gdns = dout("o_gdns", [DEPTH, 16, 8, 128, 128])
        o_ksn = dout("o_ksn", [DEPTH, 128, 4, 128])
        o_vsn = dout("o_vsn", [DEPTH, 128, 256])
        o_kcopy = dout("o_kcopy", [DEPTH, 16, 120, 256])
        o_vcopy = dout("o_vcopy", [DEPTH, 16, 120, 256])
    dbg_out = {}

    cst = sb("cst_f", [128, NCSTF])
    cstb = sb("cst_b", [128, NCSTB], BF16)
    sp_t = sb("sp_t", [128, DEPTH, NSP])
    spd = sb("spd", [128, DEPTH, 32])
    hbuf = [sb("h%d" % i, [128, KC, TT]) for i in range(2)]
    xn = sb("xn", [128, KC, TT], BF16)
    xnF = sb("xnF", [128, KC, TT], BF16)
    arena = sb("arena", [128, 32, TT], BF16)
    arenaF = sb("arenaF", [128, 16, TT], BF16)
    gF = arenaF[:, 0:KC, :]
    qaT = sb("qaT", [128, KC, TT], BF16)
    yqF = sb("yqF", [128, TT])
    sqbF = [sb("sqbF%d" % i, [128, TT], BF16) for i in range(2)]
    rstdF = sb("rstdF", [128, TT])
    oaT = sb("oaT", [128, KC, TT], BF16)
    obT = sb("obT", [128, KC, TT], BF16)
    ND_ = 3
    rawx = [sb("rawx%d" % i, [128, TT + 48]) for i in range(ND_)]
    caccs = [sb("cacc%d" % i, [128, TT]) for i in range(ND_)]
    ctan = [sb("ctan%d" % i, [128, TT]) for i in range(ND_)]
    yqs = [sb("yq%d" % i, [128, TT]) for i in range(ND_)]
    sqb = [sb("sqb%d" % i, [128, TT], BF16) for i in range(2)]
    rstd = [sb("rstd%d" % i, [128, TT]) for i in range(2)]
    rot = [0]
    arena_f = arena[:, :, :].rearrange("p a t -> p (a t)").bitcast(F32)
    m1s = [arena_f[:, i * TT:(i + 1) * TT] for i in range(4)]
    m2s = [arena_f[:, (4 + i) * TT:(5 + i) * TT] for i in range(3)]
    NSLOT = int(_os.environ.get("K_NSLOT", "4"))
    wring = [sb("wring%d" % i, [128, 4096], BF16) for i in range(NSLOT)]
    pTf = sb("pTf", [128, 2, TT])
    pTb = sb("pTb", [128, 2, TT], BF16)
    NBMAX = TT // 128
    tmsm = sb("tmsm", [128, NBMAX, 16])
    gsm = sb("gsm", [128, NBMAX, 96])
    vext = sb("vext", [128, NBMAX + 1, 256], BF16)
    kdup = sb("kdup", [128, 4, (NBMAX + 1) * 128], BF16)
    S_all = sb("S_all", [128, max(DEPTH, 3), 8, 128])
    Sb = sb("Sb", [128, 8, 128], BF16)
    tails = sb("tails", [128, DEPTH, 24, 3])
    kprev = sb("kprev", [128, DEPTH, 4, 128], BF16)
    vprev = sb("vprev", [128, DEPTH, 256], BF16)
    GL = sb("GL", [128, 4, 128])
    Eraw = sb("Eraw", [128, 4, 128])
    Ems = sb("Ems", [128, 4, 128], BF16)
    Emi = sb("Emi", [128, 4, 128], BF16)
    Nb = sb("Nb", [128, 4, 128], BF16)
    Pm = [sb("Pm%d" % i, [128, 4, 128], BF16) for i in range(2)]
    PTm = [sb("PTm0", [128, 4, 128], BF16)]
    X1b = sb("X1b", [128, 4, 128], BF16)
    X2b = sb("X2b", [128, 4, 128], BF16)
    No_b = sb("No_b", [128, 4, 128], BF16)
    NoT_b = sb("NoT_b", [128, 4, 128], BF16)
    Tdn = sb("Tdn", [128, 4, 128], BF16)
    TTb = sb("TTb", [128, 4, 128], BF16)
    qkb = sb("qkb", [128, 4, 128], BF16)
    qkT = sb("qkT", [128, 4, 128], BF16)
    kbg, kdec, vbt, wTb = Ems, Emi, Pm[0], Pm[1]
    u_t = GL
    vnew = X1b
    tq = sb("tq", [128, 4, 128])
    o_t = Eraw
    osq = tq
    on_b = X2b
    osm = sb("osm", [128, 16])
    sc = [sb("sc%d" % i, [128, 4, 128]) for i in range(2)]
    PTa = [sb("PTa%d" % i, [128, 4, 128], BF16) for i in range(2)]
    rden = sb("rden", [128, 2, 128])
    kfin = sc[0]
    vfin = rden[:, :, :].rearrange("p a b -> p (a b)")
    if cfg.sample:
        ccin = sb("ccin", [128, 24, 16, 3])
        ccout = sb("ccout", [128, 24, 16, 3])
        Ssf = None
        Ssb = [sb("Ssb%d" % i, [128, 8, 128], BF16) for i in range(2)]
        kcb = sb("kcb", [128, 16, 128], BF16)
        vcb = sb("vcb", [128, 16, 64], BF16)
        kdm = [sb("kdm%d" % i, [128, 4, 128], BF16) for i in range(2)]
        scs = sb("scs", [128, 16, 4, 8])
        PTc = sb("PTc", [128, 16, 4, 9], BF16)
        qsT = No_b
        uT = GL
        vnT = NoT_b
        egls = sb("egls", [128, 16, 8])
        gsq = sb("gsq", [128, 16, 8])
        snew = None
    import sys as _sys
    print("[kernel] SBUF bytes/partition remaining:", nc.sbuf_bytes_remaining, file=_sys.stderr)
    banks = [st.enter_context(nc.psum_tensor("ps%d" % i, [128, 512], F32)) for i in range(8)]
    bank_i = [0]

    bank_f = [0]
    bank_g = [0]
    NG_, NM_, NF_ = [int(x) for x in _os.environ.get("K_BANKS", "2,4,2").split(",")]

    def bank(pool="M"):
        if pool == "F":
            b = banks[NG_ + NM_ + bank_f[0] % NF_]
            bank_f[0] += 1
        elif pool == "G":
            b = banks[bank_g[0] % NG_]
            bank_g[0] += 1
        else:
            b = banks[NG_ + bank_i[0] % NM_]
            bank_i[0] += 1
        return b

    def C(name, n=128):
        return cst[:, CSTF[name]:CSTF[name] + n]

    def CB(name):
        return cstb[:, CSTB[name]:CSTB[name] + 128]

    IDb = CB("IDENT")
    ONESb = CB("ONES")
    ONES64b = CB("ONESB64")

    isap = lambda x: not isinstance(x, (int, float))

    def vdur(eng, ap):
        n = fsz(ap)
        return (n + 70) / (0.96 if eng == "dve" else 0.5) + 60.0

    def fsz(ap):
        n = 1
        for d in ap.shape[1:]:
            n *= d
        return n

    def mm(out, lhsT, rhs, start=True, stop=True, skip=False):
        d = max(64, fsz(rhs)) / 2.4 * (4.0 if rhs.dtype == F32 else 1.0) + 8.0
        R.add("pe", lambda e: e.matmul(out, lhsT=lhsT, rhs=rhs, start=start, stop=stop, skip_group_check=skip),
              [lhsT, rhs], [out], dur=d)

    def tr(out, in_, ident):
        R.add("pe", lambda e: e.transpose(out, in_, ident), [in_, ident], [out], dur=128 / 2.4 + 8.0)

    def act(out, in_, func, scale=1.0, bias=0.0):
        rd = [in_] + [x for x in (scale, bias) if isap(x)]
        o_ = R.add("act", lambda e: e.activation(out=out, in_=in_, func=func, bias=bias, scale=scale), rd, [out],
                   dur=(fsz(in_) + 200) / 1.2)
        if o_ is not None:
            o_.aset = "B" if func == AF.Ln else ("A" if func == AF.Tanh else None)

    def tt(eng, out, a, b, op):
        R.add(eng, lambda e: e.tensor_tensor(out=out, in0=a, in1=b, op=op), [a, b], [out], dur=vdur(eng, a))

    def ts(eng, out, a, s1, op0, s2=None, op1=None):
        rd = [a] + [x for x in (s1, s2) if x is not None and isap(x)]
        if op1 is None:
            R.add(eng, lambda e: e.tensor_scalar(out=out, in0=a, scalar1=s1, scalar2=None, op0=op0), rd, [out],
                  dur=vdur(eng, a))
        else:
            R.add(eng, lambda e: e.tensor_scalar(out=out, in0=a, scalar1=s1, scalar2=s2, op0=op0, op1=op1), rd, [out],
                  dur=vdur(eng, a))

    def stt(out, a, s, b, op0, op1):
        rd = [a, b] + ([s] if isap(s) else [])
        R.add("dve", lambda e: e.scalar_tensor_tensor(out=out, in0=a, scalar=s, in1=b, op0=op0, op1=op1), rd, [out],
              dur=vdur("dve", a))

    def cp(eng, out, in_):
        if eng == "act":
            act(out, in_, AF.Copy)
        else:
            R.add(eng, lambda e: e.tensor_copy(out=out, in_=in_), [in_], [out], dur=vdur(eng, in_))

    def red(out, in_, op):
        R.add("dve", lambda e: e.tensor_reduce(out=out, in_=in_, axis=AX.X, op=op), [in_], [out], dur=vdur("dve", in_))

    def recip(out, in_):
        R.add("dve", lambda e: e.reciprocal(out=out, in_=in_), [in_], [out], dur=vdur("dve", in_))

    def memset(eng, out, v):
        R.add(eng, lambda e: e.memset(out, v), [], [out], dur=vdur(eng, out))

    def dma(q, out, in_, extra=(), **kw):
        nbytes = 1
        for d in out.shape:
            nbytes *= d
        nbytes *= (4 if out.dtype == F32 else 2) + (4 if in_.dtype == F32 else 2)
        return R.add(q, lambda e: e.dma_start(out=out, in_=in_, **kw), [in_], [out], is_dma=True,
                     dur=2000.0 + nbytes / 2 / 400.0, extra=extra)

    def dbg(name, ap, shape):
        if name in cfg.dbg:
            if name not in dbg_out:
                dbg_out[name] = dout("dbg_" + name, shape)
            dma("sp", dbg_out[name], ap)

    def rsqrt_ln(out, in_, scale, eps, mult=1.0):
        act(out, in_, AF.Ln, scale=scale, bias=eps)
        act(out, out, AF.Exp, scale=-0.5, bias=float(np.log(mult)))

    dma("sp", cst[:, :], cstd)
    dma("sp", sp_t[:, :, :], spar)
    step = 2 * 4096
    o = 0
    while o < TOT:
        n = min(step, TOT - o)
        dma("pool", wbf[0, :, o:o + n], wsrc[0, :, o:o + n], max_dma_last_dim=4096)
        o += n
    cast_done = set()
    stage = S_all[:, :, :, :].rearrange("p a b c -> p (a b c)")
    dma("sp", stage[:, 0:NCSTB], cstmd)
    cp("dve", cstb[:, :], stage[:, 0:NCSTB])
    for l in range(DEPTH):
        ts("dve", spd[:, l, 0:1], sp_t[:, l, SP_GN:SP_GN + 1], 0.5, ALU.mult)
        ts("dve", spd[:, l, 1:2], sp_t[:, l, SP_QN:SP_QN + 1], 0.125, ALU.mult)
        act(spd[:, l, 2:10], sp_t[:, l, SP_SK:SP_SK + 8], AF.Exp)
        act(spd[:, l, 10:18], sp_t[:, l, SP_AL:SP_AL + 8], AF.Exp)
        ts("dve", spd[:, l, 10:18], spd[:, l, 10:18], -1.0, ALU.mult)

    R.const_names.update(["cst_f", "cst_b", "sp_t", "spd", "xT", "pT", "wsrc", "spar", "cst", "cstm", "i_cconv", "i_sgdn",
                          "i_kcT", "i_vc", "i_kc"])

    plan = []
    wstate = dict(next_load=0, next_use=0)

    def wload(i):
        l, g = plan[i]
        o, kc, ncol = offs[g]
        n = kc * ncol
        op_ = dma("sp", wring[i % NSLOT][:, 0:n], wbf[l, :, o:o + n])
        if (l, g) not in cast_done and not R.dry:
            cast_done.add((l, g))
            if l + 1 < DEPTH:
                dma("pool", wbf[l + 1, :, o:o + n], wsrc[l + 1, :, o:o + n], max_dma_last_dim=4096, extra=(op_,))

    def wget(l, g):
        i = wstate["next_use"]
        wstate["next_use"] += 1
        o, kc, ncol = offs[g]
        if R.dry:
            plan.append((l, g))
        else:
            assert plan[i] == (l, g), (plan[i], l, g)
            while wstate["next_load"] < min(len(plan), i + NSLOT):
                wload(wstate["next_load"])
                wstate["next_load"] += 1
        return wring[i % NSLOT][:, 0:kc * ncol].rearrange("p (k n) -> p k n", k=kc)

    tiles = []
    t0 = 0
    while t0 < SEQ:
        tiles.append(dict(kind="p", t0=t0, nt=TT, first=(t0 == 0), last=(t0 + TT >= SEQ), idx=len(tiles)))
        t0 += TT
    if cfg.sample:
        tiles.append(dict(kind="s", t0=SEQ, nt=128, first=True, last=True, idx=len(tiles)))

    def rmsnorm_fm(hs, xd, l, col, nt, sq2, r, pool="M"):
        ps = bank(pool)
        for kc in range(KC):
            s = sq2[kc % 2]
            act(s[:, 0:nt], hs[:, kc, 0:nt], AF.Square)
            mm(ps[:, 0:nt], ONESb, s[:, 0:nt], start=(kc == 0), stop=(kc == KC - 1))
        rsqrt_ln(r[:, 0:nt], ps[:, 0:nt], 1.0 / D_MODEL, EPS)
        for kc in range(KC):
            stt(xd[:, kc, 0:nt], hs[:, kc, 0:nt], sp_t[:, l, col + kc:col + kc + 1], r[:, 0:nt], ALU.mult, ALU.mult)

    def proj_chunk(w, c, nt, rhs_t, pool="M"):
        ps = bank(pool)
        nk = w.shape[1]
        for kc in range(nk):
            mm(ps[:, 0:nt], w[:, kc, c * 128:(c + 1) * 128], rhs_t[:, kc, 0:nt], start=(kc == 0), stop=(kc == nk - 1))
        return ps

    def gen_M(tile, l):
        kind, nt = tile["kind"], tile["nt"]
        nb = nt // 128
        samp = (kind == "s")
        ST = cfg.stages
        h = hbuf[tile["idx"] % 2]
        if not samp:
            if tile["first"]:
                memset("pool", S_all[:, l, :, :], 0.0)
                memset("pool", tails[:, l, :, :], 0.0)
            cp("act", Sb[:, :, :], S_all[:, l, :, :])
            if not tile["first"]:
                cp("pool", vext[:, 0, :], vprev[:, l, :])
                cp("pool", kdup[:, :, 0:128], kprev[:, l, :, :])
        else:
            dma("sp", ccin[:, :, :, :], i_cconv[l])
        R.cur_tag = "1norm"
        rmsnorm_fm(h, xn, l, SP_NM, nt, sqb[0:2], rstd[0])
        R.cur_tag = "2tm"
        w = wget(l, "tm")
        for b in range(nb):
            ps = bank()
            for kc in range(KC):
                mm(ps[:, 0:272], xn[:, kc, b * 128:(b + 1) * 128], w[:, kc, 0:272], start=(kc == 0), stop=(kc == KC - 1))
            cp("act", tmsm[:, b, :], ps[:, 0:16])
            cp("dve", vext[:, b + 1, :], ps[:, 16:272])
            if tile["last"] and b == nb - 1:
                cp("act", vfin[:, :], ps[:, 16:272])
                dma("sp", (o_vsn if samp else o_vp)[l], vfin[:, :])
        yield
        R.cur_tag = "3conv"
        L_ = 8 if samp else nt
        nseq = 16 if samp else 1
        for hg in range(2):
            for typ in ("q", "k", "v"):
                w = wget(l, "%s%d" % (typ, hg))
                for c in range(4):
                    ch = {"q": 0, "k": 8, "v": 16}[typ] + hg * 4 + c
                    ps = proj_chunk(w, c, nt, xn)
                    if "gdn" not in ST:
                        continue
                    rot[0] += 1
                    ri = rot[0] % ND_
                    cacc, yq = caccs[ri], yqs[ri]
                    rx = rawx[ri]
                    rxv = rx[:, 0:nseq * (L_ + 3)].rearrange("p (s t) -> p s t", s=nseq)
                    psv = ps[:, 0:nt].rearrange("p (s t) -> p s t", s=nseq)
                    cp("act", rxv[:, :, 3:3 + L_], psv)
                    if samp:
                        cp("pool", rxv[:, :, 0:3], ccin[:, ch, :, :])
                    else:
                        cp("pool", rxv[:, :, 0:3], tails[:, l, ch:ch + 1, :])
                    av = cacc[:, 0:nt].rearrange("p (s t) -> p s t", s=nseq)
                    cw = lambda j: sp_t[:, l, SP_CW + ch * 4 + j:SP_CW + ch * 4 + j + 1]
                    act(av, psv, AF.Copy, scale=cw(3))
                    for j in range(0, 3):
                        stt(av, rxv[:, :, j:j + L_], cw(j), av, ALU.mult, ALU.add)
                    if samp:
                        cp("pool", ccout[:, ch, :, :], rxv[:, :, L_:L_ + 3])
                    else:
                        cp("pool", tails[:, l, ch:ch + 1, :], rxv[:, :, L_:L_ + 3])
                    tn = ctan[ri]
                    act(tn[:, 0:nt], cacc[:, 0:nt], AF.Tanh, scale=0.5)
                    if typ == "v":
                        stt(arena[:, hg * 16 + 8 + c, 0:nt], tn[:, 0:nt], 1.0, cacc[:, 0:nt], ALU.add, ALU.mult)
                    else:
                        stt(yq[:, 0:nt], tn[:, 0:nt], 1.0, cacc[:, 0:nt], ALU.add, ALU.mult)
                        s_ = sqb[ri % 2]
                        if SQPOOL:
                            tt("pool", s_[:, 0:nt], yq[:, 0:nt], yq[:, 0:nt], ALU.mult)
                        else:
                            act(s_[:, 0:nt], yq[:, 0:nt], AF.Square)
                        pn = bank()
                        mm(pn[:, 0:nt], ONESb, s_[:, 0:nt])
                        r = rstd[ri % 2]
                        rsqrt_ln(r[:, 0:nt], pn[:, 0:nt], 1.0, 4.0 * EPS, mult=(128.0 ** -0.5 if typ == "q" else 1.0))
                        dst = arena[:, hg * 16 + (0 if typ == "q" else 4) + c, 0:nt]
                        tt("dve", dst, yq[:, 0:nt], r[:, 0:nt], ALU.mult)
                yield
            w = wget(l, "z%d" % hg)
            for c in range(4):
                ps = proj_chunk(w, c, nt, xn)
                if "gdn" not in ST:
                    continue
                tn = ctan[c % ND_]
                act(tn[:, 0:nt], ps[:, 0:nt], AF.Tanh, scale=0.5)
                stt(arena[:, hg * 16 + 12 + c, 0:nt], tn[:, 0:nt], 1.0, ps[:, 0:nt], ALU.add, ALU.mult)
            if "gdn" in ST:
                for b in range(nb):
                    R.cur_tag = "3gdn"
                    if hg == 0:
                        gdn_small(tile, l, b)
                    gdn_unit(tile, l, b, hg)
                R.cur_tag = "3conv"
            else:
                if hg == 0:
                    memset("pool", oaT[:, :, 0:nt], 0.0)
            yield
        if "gdn" in ST:
            if samp:
                dma("sp", o_convs[l], ccout[:, :, :, :])
            elif tile["last"]:
                dma("sp", o_convp[l], tails[:, l, :, :])
                dma("sp", o_gdnp[l], S_all[:, l, :, :])
        R.cur_tag = "4swa"
        qa = qaT
        for i in range(2):
            w = wget(l, "sq%d" % i)
            for c in range(4):
                ps = proj_chunk(w, c, nt, xn)
                if "swa" not in ST:
                    continue
                s_ = sqb[c % 2]
                act(s_[:, 0:nt], ps[:, 0:nt], AF.Square)
                pn = bank()
                mm(pn[:, 0:nt], ONES64b, s_[:, 0:nt])
                r = rstd[c % 2]
                rsqrt_ln(r[:, 0:nt], pn[:, 0:nt], 1.0 / 64.0, EPS)
                stt(qa[:, i * 4 + c, 0:nt], ps[:, 0:nt], spd[:, l, 1:2], r[:, 0:nt], ALU.mult, ALU.mult)
            yield
        w = wget(l, "skd")
        for c in range(4):
            ps = proj_chunk(w, c, nt, xn)
            if "swa" not in ST:
                continue
            s_ = sqb[c % 2]
            act(s_[:, 0:nt], ps[:, 0:nt], AF.Square)
            pn = bank()
            mm(pn[:, 0:nt], ONES64b, s_[:, 0:nt])
            r = rstd[c % 2]
            rsqrt_ln(r[:, 0:nt], pn[:, 0:nt], 1.0 / 64.0, EPS)
            stt(kdup[:, c, 128:128 + nt], ps[:, 0:nt], sp_t[:, l, SP_KN:SP_KN + 1], r[:, 0:nt], ALU.mult, ALU.mult)
            if tile["last"]:
                stt(kfin[:, c, :], ps[:, nt - 128:nt], sp_t[:, l, SP_KN:SP_KN + 1], r[:, nt - 128:nt], ALU.mult, ALU.mult)
        if "swa" in ST:
            if tile["last"]:
                dma("sp", (o_ksn if samp else o_kp)[l], kfin[:, :, :])
            for qb in range(nb):
                for kvh in range(4):
                    swa_unit(tile, l, qb, kvh)
            if not samp and not tile["last"]:
                cp("pool", vprev[:, l, :], vext[:, nb, :])
                cp("pool", kprev[:, l, :, :], kdup[:, :, nb * 128:(nb + 1) * 128])
            if samp:
                dma("act", o_kcopy[l], i_kc[l, :, 8:128, :])
                dma("act", o_vcopy[l], i_vc[l, :, 8:128, :])
        else:
            memset("pool", obT[:, :, 0:nt], 0.0)
        yield
        R.cur_tag = "5mix"
        for i in range(2):
            wa = wget(l, "ga%d" % i)
            for c in range(4):
                ps = proj_chunk(wa, c, nt, xn)
                yq = yqs[c % ND_]
                act(yq[:, 0:nt], ps[:, 0:nt], AF.Tanh, scale=0.5)
                stt(m1s[c][:, 0:nt], yq[:, 0:nt], 1.0, oaT[:, i * 4 + c, 0:nt], ALU.add, ALU.mult)
            yield
            wb_ = wget(l, "gb%d" % i)
            for c in range(4):
                ps = proj_chunk(wb_, c, nt, xn)
                yq = yqs[c % ND_]
                act(yq[:, 0:nt], ps[:, 0:nt], AF.Tanh, scale=0.5)
                m2 = m2s[c % 3]
                stt(m2[:, 0:nt], yq[:, 0:nt], 1.0, obT[:, i * 4 + c, 0:nt], ALU.add, ALU.mult)
                tt("dve", oaT[:, i * 4 + c, 0:nt], m1s[c][:, 0:nt], m2[:, 0:nt], ALU.add)
            yield
        R.cur_tag = "6wo"
        for i in range(2):
            w = wget(l, "wo%d" % i)
            for c in range(4):
                ps = proj_chunk(w, c, nt, oaT)
                stt(h[:, i * 4 + c, 0:nt], ps[:, 0:nt], 0.5, h[:, i * 4 + c, 0:nt], ALU.mult, ALU.add)
            yield

    def gen_F(tile, l):
        nt = tile["nt"]
        ST = cfg.stages
        h = hbuf[tile["idx"] % 2]
        R.cur_tag = "7ffn"
        if "ffn" in ST:
            rmsnorm_fm(h, xnF, l, SP_NF, nt, sqbF, rstdF, pool="F")
        hid = arenaF
        for hh in range(2):
            for gi in range(4):
                w = wget(l, "up%d_%d" % (hh, gi))
                if "ffn" in ST:
                    for c in range(4):
                        ps = proj_chunk(w, c, nt, xnF, pool="F")
                        act(yqF[:, 0:nt], ps[:, 0:nt], AF.Relu)
                        tt("pool", hid[:, gi * 4 + c, 0:nt], yqF[:, 0:nt], yqF[:, 0:nt], ALU.mult)
                yield
            for cb in range(4):
                w = wget(l, "dn%d_%d" % (hh, cb))
                if "ffn" in ST:
                    for c in range(2):
                        ps = proj_chunk(w, c, nt, hid, pool="F")
                        oc = cb * 2 + c
                        tt("dve", h[:, oc, 0:nt], ps[:, 0:nt], h[:, oc, 0:nt], ALU.add)
                yield
        R.cur_tag = "8ple"
        if "ple" in ST:
            rmsnorm_fm(h, xnF, l, SP_NP, nt, sqbF, rstdF, pool="F")
            dma("sp", pTf[:, :, 0:nt], pT[l, :, :, tile["t0"]:tile["t0"] + nt])
            cp("pool", pTb[:, :, 0:nt], pTf[:, :, 0:nt])
        for i in range(2):
            w = wget(l, "pg%d" % i)
            if "ple" in ST:
                for c in range(4):
                    ps = proj_chunk(w, c, nt, xnF, pool="F")
                    act(gF[:, i * 4 + c, 0:nt], ps[:, 0:nt], AF.Tanh, scale=0.5)
            yield
        wp = wget(l, "pp")
        if "ple" in ST:
            for oc in range(8):
                ps2 = proj_chunk(wp, oc, nt, pTb, pool="F")
                stt(yqF[:, 0:nt], gF[:, oc, 0:nt], 1.0, ps2[:, 0:nt], ALU.add, ALU.mult)
                stt(h[:, oc, 0:nt], yqF[:, 0:nt], 0.5, h[:, oc, 0:nt], ALU.mult, ALU.add)
        if l == DEPTH - 1:
            dma("sp", yT[:, :, tile["t0"]:tile["t0"] + nt], h[:, :, 0:nt])
        yield

    def gdn_small(tile, l, b):
        samp = tile["kind"] == "s"
        G = lambda a: gsm[:, b, a:a + 8]
        bl = tmsm[:, b, 0:8]
        al = tmsm[:, b, 8:16]
        act(G(80), bl, AF.Tanh, scale=0.5)
        ts("dve", G(0), G(80), 0.5, ALU.mult, 0.5, ALU.add)
        tt("dve", G(80), al, sp_t[:, l, SP_DT:SP_DT + 8], ALU.add)
        act(G(80), G(80), AF.Exp)
        act(G(80), G(80), AF.Ln, bias=1.0)
        tt("dve", G(8), G(80), spd[:, l, 10:18], ALU.mult)
        ps = bank("G")
        mm(ps[:, 0:8], C("U_S") if samp else C("U"), G(8))
        mm(ps[:, 8:16], C("ONESSEQ_S") if samp else C("ONES"), G(8))
        cp("dve", gsm[:, b, 16:32], ps[:, 0:16])
        act(G(32), G(16), AF.Exp)
        tt("dve", G(40), G(0), G(32), ALU.mult)
        tt("dve", G(80), G(24), G(16), ALU.subtract)
        act(G(48), G(80), AF.Exp)
        act(G(56), G(24), AF.Exp)
        ts("dve", G(64), G(0), -1.0, ALU.mult)
        ts("dve", G(72), G(0), 0.5, ALU.mult)
        if samp:
            tt("dve", gsq[:, :, :], G(8).unsqueeze(1).broadcast_to([128, 16, 8]),
               C("SM", 16).unsqueeze(2).broadcast_to([128, 16, 8]), ALU.mult)
            ps2 = bank("G")
            mm(ps2[:, 0:128], C("ONES"), gsq[:, :, :].rearrange("p s h -> p (s h)"))
            act(egls[:, :, :].rearrange("p s h -> p (s h)"), ps2[:, 0:128], AF.Exp)

    def bc4(ap):
        return ap.unsqueeze(2).broadcast_to([128, 4, 128])

    def hb(ap):
        return ap.unsqueeze(1).broadcast_to([128, 4, 128])

    def f4(ap):
        return ap.rearrange("p h n -> p (h n)")

    def v4(ap):
        return ap.rearrange("p (h n) -> p h n", h=4)

    def gdn_unit(tile, l, b, hg):
        samp = tile["kind"] == "s"
        H0 = hg * 4
        G = lambda a: gsm[:, b, a + H0:a + H0 + 4]
        blk = slice(b * 128, (b + 1) * 128)
        ab = hg * 16
        qT = lambda hh: arena[:, ab + 0 + hh, blk]
        kT = lambda hh: arena[:, ab + 4 + hh, blk]
        vT = lambda hh: arena[:, ab + 8 + hh, blk]
        Um = C("U_S") if samp else C("U")
        MSm = CB("MS_S") if samp else CB("MS")
        MIm = CB("MI_S") if samp else CB("MI")
        tt("dve", GL[:, :, :], bc4(G(8)), hb(CB("L")), ALU.mult)
        pd = bank("G")
        mm(pd[:, :], Um, f4(GL[:, :, :]))
        act(f4(Eraw[:, :, :]), pd[:, :], AF.Exp)
        tt("pool", Ems[:, :, :], Eraw[:, :, :], hb(MSm), ALU.mult)
        tt("pool", Emi[:, :, :], Eraw[:, :, :], hb(MIm), ALU.mult)
        pkk = bank("G")
        pqk = bank("G")
        for hh in range(4):
            mm(pkk[:, hh * 128:(hh + 1) * 128], kT(hh), kT(hh))
        for hh in range(4):
            mm(pqk[:, hh * 128:(hh + 1) * 128], qT(hh), kT(hh))
        for hh in range(4):
            stt(Nb[:, hh, :], pkk[:, hh * 128:(hh + 1) * 128], gsm[:, b, 64 + H0 + hh:64 + H0 + hh + 1], Ems[:, hh, :],
                ALU.mult, ALU.mult)
        tt("dve", f4(qkb[:, :, :]), pqk[:, :], f4(Emi[:, :, :]), ALU.mult)
        pt1 = bank("G")
        pt1b = pt1[:, :].bitcast(BF16)
        for hh in range(4):
            tr(pt1b[:, hh * 128:(hh + 1) * 128], Nb[:, hh, :], IDb)
        NTb = PTm[0]
        cp("act", f4(NTb[:, :, :]), pt1b[:, 0:512])
        pt2 = bank("G")
        pt2b = pt2[:, :].bitcast(BF16)
        for hh in range(4):
            tr(pt2b[:, hh * 128:(hh + 1) * 128], qkb[:, hh, :], IDb)
        cp("act", f4(qkT[:, :, :]), pt2b[:, 0:512])

        def mm4(lhs, rhs, add=None):
            p_ = bank("G")
            for hh in range(4):
                mm(p_[:, hh * 128:(hh + 1) * 128], lhs[:, hh, :], rhs[:, hh, :], start=True,
                   stop=(add is None or not PEADD))
                if add is not None and PEADD:
                    a_ = add if add is IDb else add[:, hh, :]
                    mm(p_[:, hh * 128:(hh + 1) * 128], IDb, a_, start=False, stop=True)
            return p_

        def evac_add(dst, p_, add, eng):
            if PEADD:
                cp(eng, f4(dst[:, :, :]), p_[:, :])
            elif add is IDb:
                tt("dve", dst[:, :, :], v4(p_[:, :]), hb(IDb), ALU.add)
            else:
                tt("dve", f4(dst[:, :, :]), p_[:, :], f4(add[:, :, :]), ALU.add)
        Nd, NdT = Pm[0], Pm[1]
        tt("dve", Nd[:, :, :], Nb[:, :, :], hb(CB("MD4")), ALU.mult)
        tt("dve", NdT[:, :, :], NTb[:, :, :], hb(CB("MD4")), ALU.mult)
        p1 = mm4(NdT, Nd, add=IDb)
        p2 = mm4(Nd, NdT, add=IDb)
        Q_, QT_, R_, RT_ = X1b, X2b, No_b, NoT_b
        evac_add(Q_, p1, IDb, "act")
        evac_add(QT_, p2, IDb, "act")
        tt("pool", R_[:, :, :], Nd[:, :, :], hb(CB("IDENT")), ALU.add)
        tt("pool", RT_[:, :, :], NdT[:, :, :], hb(CB("IDENT")), ALU.add)
        p3 = mm4(QT_, R_)
        p4 = mm4(R_, QT_)
        cp("act", f4(Tdn[:, :, :]), p3[:, :])
        cp("act", f4(TTb[:, :, :]), p4[:, :])
        levels = [4] if samp else [4, 8, 16, 32, 64]
        for li, m_ in enumerate(levels):
            last = (li == len(levels) - 1)
            tt("dve", No_b[:, :, :], Nb[:, :, :], hb(CB("ML%d" % m_)), ALU.mult)
            if not last:
                tt("dve", NoT_b[:, :, :], NTb[:, :, :], hb(CB("MLT%d" % m_)), ALU.mult)
                px1 = mm4(NoT_b, Tdn)
                cp("act", f4(X1b[:, :, :]), px1[:, :])
            px2 = mm4(No_b, TTb)
            cp("act", f4(X2b[:, :, :]), px2[:, :])
            if not last:
                py1 = mm4(TTb, X1b, add=Tdn)
            py2 = mm4(Tdn, X2b, add=TTb)
            if not last:
                evac_add(Tdn, py1, Tdn, "act")
            evac_add(TTb, py2, TTb, "dve")
        pk = bank("G")
        pkb = pk[:, :].bitcast(BF16)
        for hh in range(4):
            tr(pkb[:, hh * 128:(hh + 1) * 128], kT(hh), IDb)
        tt("dve", kbg[:, :, :], v4(pkb[:, 0:512]), bc4(G(40)), ALU.mult)
        tt("dve", kdec[:, :, :], v4(pkb[:, 0:512]), bc4(G(48)), ALU.mult)
        pv = bank("G")
        pvb = pv[:, :].bitcast(BF16)
        for hh in range(4):
            tr(pvb[:, hh * 128:(hh + 1) * 128], vT(hh), IDb)
        tt("dve", vbt[:, :, :], v4(pvb[:, 0:512]), bc4(G(72)), ALU.mult)
        pw = bank("G")
        for hh in range(4):
            mm(pw[:, hh * 128:(hh + 1) * 128], kbg[:, hh, :], TTb[:, hh, :])
        cp("act", f4(wTb[:, :, :]), pw[:, :])
        if not samp:
            pu = bank("G")
            for hh in range(4):
                mm(pu[:, hh * 128:(hh + 1) * 128], TTb[:, hh, :], vbt[:, hh, :])
            cp("act", f4(u_t[:, :, :]), pu[:, :])
            pws = bank("G")
            for hh in range(4):
                mm(pws[:, hh * 128:(hh + 1) * 128], wTb[:, hh, :], Sb[:, H0 + hh, :])
            tt("dve", f4(vnew[:, :, :]), f4(u_t[:, :, :]), pws[:, :], ALU.subtract)
            pqs = bank("G")
            for hh in range(4):
                mm(pqs[:, hh * 128:(hh + 1) * 128], qT(hh), Sb[:, H0 + hh, :])
            pin = bank("G")
            for hh in range(4):
                mm(pin[:, hh * 128:(hh + 1) * 128], qkT[:, hh, :], vnew[:, hh, :])
            tt("dve", tq[:, :, :], v4(pqs[:, :]), bc4(G(32)), ALU.mult)
            tt("dve", f4(o_t[:, :, :]), pin[:, :], f4(tq[:, :, :]), ALU.add)
            psu = bank("G")
            for hh in range(4):
                mm(psu[:, hh * 128:(hh + 1) * 128], kdec[:, hh, :], vnew[:, hh, :])
            Sv = S_all[:, l, H0:H0 + 4, :]
            tt("pool", Sv, Sv, bc4(G(56)), ALU.mult)
            tt("dve", Sv, v4(psu[:, :]), Sv, ALU.add)
            cp("act", Sb[:, H0:H0 + 4, :], Sv)
        else:
            pu = bank("G")
            for hh in range(4):
                mm(pu[:, hh * 128:(hh + 1) * 128], vbt[:, hh, :], TTb[:, hh, :])
            cp("act", f4(uT[:, :, :]), pu[:, :])
            pws = bank("G")
            pqs = bank("G")
            for sg in range(8):
                Sf_, Sb_ = S_all[:, sg % 2, :, :], Ssb[sg % 2]
                for si_ in range(2):
                    dma("sp", Sf_[:, si_ * 4:si_ * 4 + 4, :],
                        i_sgdn[l, sg * 2 + si_, H0:H0 + 4, :, :].rearrange("h d v -> d h v"))
                cp("pool", Sb_[:, :, :], Sf_[:, :, :])
                for si in range(2):
                    s = sg * 2 + si
                    cols = slice(s * 8, s * 8 + 8)
                    for hh in range(4):
                        mm(pws[:, hh * 128 + s * 8:hh * 128 + s * 8 + 8], Sb_[:, si * 4 + hh, :], wTb[:, hh, cols],
                           start=True, stop=True, skip=True)
                        mm(pqs[:, hh * 128 + s * 8:hh * 128 + s * 8 + 8], Sb_[:, si * 4 + hh, :],
                           arena[:, ab + hh, b * 128 + s * 8:b * 128 + s * 8 + 8], start=True, stop=True, skip=True)
            tt("dve", f4(vnT[:, :, :]), f4(uT[:, :, :]), pws[:, :], ALU.subtract)
            cp("act", f4(qsT[:, :, :]), pqs[:, :])
            pt3 = bank("G")
            pt3b = pt3[:, :].bitcast(BF16)
            for hh in range(4):
                tr(pt3b[:, hh * 128:(hh + 1) * 128], vnT[:, hh, :], IDb)
            cp("act", f4(vnew[:, :, :]), pt3b[:, 0:512])
            pt4 = bank("G")
            pt4b = pt4[:, :].bitcast(BF16)
            for hh in range(4):
                tr(pt4b[:, hh * 128:(hh + 1) * 128], qsT[:, hh, :], IDb)
            tt("dve", tq[:, :, :], v4(pt4b[:, 0:512]), bc4(G(32)), ALU.mult)
            pin = bank("G")
            for hh in range(4):
                mm(pin[:, hh * 128:(hh + 1) * 128], qkT[:, hh, :], vnew[:, hh, :])
            tt("dve", f4(o_t[:, :, :]), pin[:, :], f4(tq[:, :, :]), ALU.add)
            for sg in range(8):
                Sf_ = S_all[:, sg % 2, :, :]
                for si_ in range(2):
                    dma("sp", Sf_[:, si_ * 4:si_ * 4 + 4, :],
                        i_sgdn[l, sg * 2 + si_, H0:H0 + 4, :, :].rearrange("h d v -> d h v"))
                for si in range(2):
                    s = sg * 2 + si
                    km = kdm[s % 2]
                    ts("dve", km[:, :, :], kdec[:, :, :], C("SM", 16)[:, s:s + 1], ALU.mult)
                    psu = bank("G")
                    for hh in range(4):
                        mm(psu[:, hh * 128:(hh + 1) * 128], km[:, hh, :], vnew[:, hh, :])
                    sn = S_all[:, 2, (s % 2) * 4:(s % 2) * 4 + 4, :]
                    tt("pool", sn[:, :, :], Sf_[:, si * 4:si * 4 + 4, :], bc4(egls[:, s, H0:H0 + 4]), ALU.mult)
                    tt("dve", sn[:, :, :], v4(psu[:, :]), sn[:, :, :], ALU.add)
                    dst = o_gdns[l, s, H0:H0 + 4, :, :].rearrange("h d v -> d h v")
                    dma("sp", dst, sn[:, :, :])
        tt("pool", osq[:, :, :], o_t[:, :, :], o_t[:, :, :], ALU.mult)
        red(osm[:, 0:4], osq[:, :, :], ALU.add)
        rsqrt_ln(osm[:, 4:8], osm[:, 0:4], 1.0 / 128.0, EPS)
        tt("dve", on_b[:, :, :], o_t[:, :, :], bc4(osm[:, 4:8]), ALU.mult)
        po = bank("G")
        pob = po[:, :].bitcast(BF16)
        for hh in range(4):
            tr(pob[:, hh * 128:(hh + 1) * 128], on_b[:, hh, :], IDb)
        stt(oaT[:, H0:H0 + 4, blk], v4(pob[:, 0:512]), spd[:, l, 0:1], arena[:, ab + 12:ab + 16, blk], ALU.mult, ALU.mult)

    def swa_unit(tile, l, qb, kvh):
        samp = tile["kind"] == "s"
        qa = qaT
        qs = slice(qb * 128, (qb + 1) * 128)
        heads = [4 * kvh + 0, 4 * kvh + 2, 4 * kvh + 1, 4 * kvh + 3]
        kbs = []
        if not samp and not (tile["first"] and qb == 0):
            kbs.append((qb, C("DO")))
        kbs.append((qb + 1, C("DN_S") if samp else C("DD")))
        pts = []
        for i, (kb, Dm) in enumerate(kbs):
            pse, pso = bank(), bank()
            ks = slice(kb * 128, (kb + 1) * 128)
            mm(pse[:, 0:256], kdup[0:64, kvh, ks], qa[0:64, 2 * kvh:2 * kvh + 2, qs])
            mm(pso[:, 0:256], kdup[64:128, kvh, ks], qa[64:128, 2 * kvh:2 * kvh + 2, qs])
            s_ = sc[i]
            for j in range(4):
                src = (pse if j < 2 else pso)[:, (j % 2) * 128:(j % 2 + 1) * 128]
                stt(s_[:, j, :], Dm, -SLOPES[heads[j]], src, ALU.mult, ALU.add)
            act(PTa[i][:, :, :], s_[:, :, :], AF.Exp)
            pts.append((PTa[i], vext[:, kb, kvh * 64:(kvh + 1) * 64]))
        if samp:
            dma("pool", kcb[:, :, :], i_kcT[l, kvh])
            dma("pool", vcb[:, :, :], i_vc[l, :, :, kvh * 64:(kvh + 1) * 64].rearrange("s k d -> k s d"))
            pse, pso = bank(), bank()
            psve = pse[:, 0:256].rearrange("p (s j t) -> p s j t", s=16, j=2)
            psvo = pso[:, 0:256].rearrange("p (s j t) -> p s j t", s=16, j=2)
            for s in range(16):
                cols = slice(s * 8, s * 8 + 8)
                mm(psve[:, s, :, :], kcb[0:64, s, :], qa[0:64, 2 * kvh:2 * kvh + 2, cols])
                mm(psvo[:, s, :, :], kcb[64:128, s, :], qa[64:128, 2 * kvh:2 * kvh + 2, cols])
            for j in range(4):
                src = (psve if j < 2 else psvo)[:, :, j % 2, :]
                stt(scs[:, :, j, :], C("DC", 8).unsqueeze(1).broadcast_to([128, 16, 8]), -SLOPES[heads[j]],
                    src, ALU.mult, ALU.add)
            act(PTc[:, :, :, 0:8], scs[:, :, :, :], AF.Exp)
        nd = bank()
        ndv = nd[:, :].rearrange("p (a c q) -> p a c q", a=2, c=2)
        for a in range(2):
            for par in range(2):
                prt = slice(par * 64, par * 64 + 64)
                for i, (P_, v_) in enumerate(pts):
                    lhs = v_ if a == 0 else ONESb[:, 0:64]
                    mm(ndv[prt, a, :, :], lhs, P_[:, 2 * par:2 * par + 2, :], start=(i == 0),
                       stop=(i == len(pts) - 1 and not samp), skip=samp)
                if samp:
                    for s in range(16):
                        lhs = vcb[:, s, :] if a == 0 else ONESb[:, 0:64]
                        for c in range(2):
                            mm(ndv[prt, a, c, s * 8:s * 8 + 8], lhs, PTc[:, s, 2 * par + c, 0:8], start=False,
                               stop=(s == 15 and c == 1), skip=True)
        for c in range(2):
            ts("dve", rden[:, c, :], ndv[:, 1, c, :], spd[:, l, 2 + 2 * kvh + c:3 + 2 * kvh + c], ALU.add)
        recip(rden[:, :, :], rden[:, :, :])
        tt("dve", obT[:, 2 * kvh:2 * kvh + 2, qs], ndv[:, 0, :, :], rden[:, :, :], ALU.mult)

    def interleave(g1, g2):
        a_, b_ = g1, g2
        while a_ is not None or b_ is not None:
            if a_ is not None:
                try:
                    next(a_)
                except StopIteration:
                    a_ = None
            if b_ is not None:
                try:
                    next(b_)
                except StopIteration:
                    b_ = None

    def drive():
        units = []
        i = 0
        while i < len(tiles):
            if i + 1 < len(tiles) and tiles[i]["kind"] == "p" and tiles[i + 1]["kind"] == "p" and not cfg.nopair:
                for l in range(DEPTH):
                    units.append((tiles[i], l))
                    units.append((tiles[i + 1], l))
                i += 2
            else:
                for l in range(DEPTH):
                    units.append((tiles[i], l))
                i += 1
        prevF, prev_tile = None, None
        for (tile, l) in units:
            if prevF is not None and prev_tile is tile:
                interleave(prevF, None)
                prevF = None
            if l == 0:
                dma("sp", hbuf[tile["idx"] % 2][:, :, 0:tile["nt"]], xT[:, :, tile["t0"]:tile["t0"] + tile["nt"]])
            interleave(gen_M(tile, l), prevF)
            prevF, prev_tile = gen_F(tile, l), tile
        interleave(prevF, None)

    n_setup = len(R.ops)
    R.dry = True
    drive()
    R.dry = False
    assert len(R.ops) == n_setup
    bank_i[0] = 0
    bank_f[0] = 0
    bank_g[0] = 0
    wstate["next_load"] = 0
    wstate["next_use"] = 0
    drive()
    assert wstate["next_use"] == len(plan)
    R.emit(st)
    return nc, st, R, dbg_out


def pack_weights(inp, depth):
    offs, TOT = group_offsets()
    out = np.zeros((depth, 128, TOT), np.float32)
    for l in range(depth):
        for (name, src, rows, cols) in weight_groups():
            W = inp[src][l]
            if isinstance(rows, tuple):
                W = W[rows[0]:rows[0] + rows[1]]
            M = W[:, cols]
            kc = M.shape[0] // 128
            o, _, ncol = offs[name]
            out[l, :, o:o + kc * ncol] = M.reshape(kc, 128, ncol).transpose(1, 0, 2).reshape(128, kc * ncol)
    return out


def pack_small(inp, depth):
    sp = np.zeros((128, depth, NSP), np.float32)
    for l in range(depth):
        sp[:, l, SP_NM:SP_NM + 8] = inp["norm_mix"][l].reshape(8, 128).T
        sp[:, l, SP_NF:SP_NF + 8] = inp["norm_ffn"][l].reshape(8, 128).T
        sp[:, l, SP_NP:SP_NP + 8] = inp["norm_ple"][l].reshape(8, 128).T
        cw = inp["conv_w"][l]
        sp[:, l, SP_CW:SP_CW + 96] = cw.reshape(4, 24, 128).transpose(2, 1, 0).reshape(128, 96)
        sp[:, l, SP_GN] = inp["gdn_norm"][l]
        sp[:, l, SP_QN] = np.tile(inp["q_norm"][l], 2)
        sp[:, l, SP_KN] = np.tile(inp["k_norm"][l], 2)
        sk = inp["attn_sinks"][l]
        for c in range(8):
            sp[0:64, l, SP_SK + c] = sk[2 * c]
            sp[64:128, l, SP_SK + c] = sk[2 * c + 1]
        sp[:, l, SP_AL:SP_AL + 8] = inp["a_log"][l][None, :]
        sp[:, l, SP_DT:SP_DT + 8] = inp["dt_bias"][l][None, :]
    return sp


def fm(x):
    T, Dm = x.shape
    return np.ascontiguousarray(x.reshape(T, Dm // 128, 128).transpose(2, 1, 0))


def unfm(y):
    p, k, T = y.shape
    return np.ascontiguousarray(y.transpose(2, 1, 0).reshape(T, k * 128))


def make_in_maps(inp, cfg, ncores):
    depth = cfg.depth
    wsrc = pack_weights(inp, depth)
    spar = pack_small(inp, depth)
    cst = make_consts()
    maps = []
    for c in range(ncores):
        xs = [inp["x_prompt"][c, :cfg.seq]]
        ps = [inp["p_prompt"][:depth, c, :cfg.seq]]
        if cfg.sample:
            xs.append(inp["x_sample"][16 * c:16 * c + 16].reshape(128, D_MODEL))
            ps.append(inp["p_sample"][:depth, 16 * c:16 * c + 16].reshape(depth, 128, 256))
        x = np.concatenate(xs, 0)
        p = np.concatenate(ps, 1)
        m = {"xT": fm(x), "pT": np.stack([fm(p[l]) for l in range(depth)]), "wsrc": wsrc, "spar": spar,
             "cst": cst[0], "cstm": cst[1]}
        if cfg.sample:
            sl = slice(16 * c, 16 * c + 16)
            cc = inp["cache_conv"][:depth, sl]
            m["i_cconv"] = np.ascontiguousarray(cc.reshape(depth, 16, 3, 24, 128).transpose(0, 4, 3, 1, 2))
            m["i_sgdn"] = np.ascontiguousarray(inp["state_gdn"][:depth, sl])
            kc = inp["cache_swa_k"][:depth, sl]
            kt = kc.transpose(0, 3, 4, 1, 2)
            m["i_kcT"] = np.ascontiguousarray(np.concatenate([kt, kt], axis=2))
            m["i_vc"] = np.ascontiguousarray(inp["cache_swa_v"][:depth, sl].reshape(depth, 16, 128, 256))
            m["i_kc"] = np.ascontiguousarray(kc.reshape(depth, 16, 128, 256))
        maps.append(m)
    return maps


def assemble(results, cfg, ncores):
    depth, seq = cfg.depth, cfg.seq
    yp = np.zeros((ncores, seq, D_MODEL), np.float32)
    convp = np.zeros((depth, ncores, 3, 3072), np.float32)
    gdnp = np.zeros((depth, ncores, 8, 128, 128), np.float32)
    kp = np.zeros((depth, ncores, 128, 4, 64), np.float32)
    vp = np.zeros((depth, ncores, 128, 4, 64), np.float32)
    if cfg.sample:
        ys = np.zeros((ncores * 16, 8, D_MODEL), np.float32)
        convs = np.zeros((depth, ncores * 16, 3, 3072), np.float32)
        gdns = np.zeros((depth, ncores * 16, 8, 128, 128), np.float32)
        ks = np.zeros((depth, ncores * 16, 128, 4, 64), np.float32)
        vs = np.zeros((depth, ncores * 16, 128, 4, 64), np.float32)
    for c in range(ncores):
        r = results[c]
        y = unfm(r["yT"])
        yp[c] = y[:seq]
        convp[:, c] = r["o_convp"].transpose(0, 3, 2, 1).reshape(depth, 3, 3072)
        gdnp[:, c] = r["o_gdnp"].transpose(0, 2, 1, 3)
        kp[:, c] = r["o_kp"][:, 0:64].transpose(0, 3, 2, 1)
        vp[:, c] = r["o_vp"].reshape(depth, 128, 4, 64)
        if cfg.sample:
            sl = slice(16 * c, 16 * c + 16)
            ys[sl] = y[seq:].reshape(16, 8, D_MODEL)
            convs[:, sl] = r["o_convs"].transpose(0, 3, 4, 2, 1).reshape(depth, 16, 3, 3072)
            gdns[:, sl] = r["o_gdns"]
            ks[:, sl, 0:120] = r["o_kcopy"].reshape(depth, 16, 120, 4, 64)
            vs[:, sl, 0:120] = r["o_vcopy"].reshape(depth, 16, 120, 4, 64)
            kn = r["o_ksn"][:, 0:64].transpose(0, 3, 2, 1)
            ks[:, sl, 120:128] = kn.reshape(depth, 16, 8, 4, 64)
            vs[:, sl, 120:128] = r["o_vsn"].reshape(depth, 16, 8, 4, 64)
    if cfg.sample:
        return (yp, ys, convp, gdnp, kp, vp, convs, gdns, ks, vs)
    return (yp, convp, gdnp, kp, vp)


_CACHE = {}


def kernel(**inputs):
    inp = {k: np.asarray(v) for k, v in inputs.items()}
    cfg = Cfg()
    if "prog" not in _CACHE:
        _CACHE["prog"] = build(cfg)
    nc, st, R, _ = _CACHE["prog"]
    maps = make_in_maps(inp, cfg, 8)
    res = run_bass_kernel_spmd(nc, maps, core_ids=list(range(8)))
    return assemble(res.results, cfg, 8)
```

```python
import contextlib
import numpy as np
import concourse.bass as bass
import concourse.mybir as mybir
from concourse.bass_utils import run_bass_kernel_spmd

F32 = mybir.dt.float32
BF16 = mybir.dt.bfloat16
AF = mybir.ActivationFunctionType
ALU = mybir.AluOpType
AX = mybir.AxisListType

D_MODEL = 1024
KC = 8
EPS = 1e-6
BIG = 1.0e6
COMPUTE = ("pe", "act", "dve", "pool")
SLOPES = [float(2.0 ** (-8.0 * (h + 1) / 16.0)) for h in range(16)]


def _ap_box(ap):
    t = ap.tensor
    name = t.name
    dims = [(s, n) for (s, n) in ap.ap]
    off = ap.offset
    tn = type(t).__name__
    if "PSum" in tn:
        return (name, 0, 128, 0, 1 << 30, True)
    if "DRam" in tn:
        nz = [(abs(s), n) for (s, n) in dims if n > 1 and s != 0]
        if nz:
            smax, nmax = max(nz)
            rest = [(s, n) for (s, n) in dims if not (abs(s) == smax and n == nmax)]
            ext = sum(abs(s) * (n - 1) for (s, n) in rest if n > 1) + 1
            f0 = off % smax
            if len(rest) == len(dims) - 1 and f0 + ext <= smax and all(s >= 0 for s, n in dims):
                p_lo = off // smax
                return (name, p_lo, p_lo + nmax, f0, f0 + ext, False)
        lo = hi = off
        for s, n in dims:
            if n > 1:
                if s >= 0:
                    hi += s * (n - 1)
                else:
                    lo += s * (n - 1)
        return (name, 0, 1 << 30, lo, hi + 1, False)
    pst, pn = dims[0]
    if pst == 0:
        p_lo, f0 = 0, off
    else:
        p_lo = off // pst
        f0 = off - p_lo * pst
    lo = hi = f0
    for s, n in dims[1:]:
        if n > 1:
            if s >= 0:
                hi += s * (n - 1)
            else:
                lo += s * (n - 1)
    return (name, p_lo, p_lo + pn, lo, hi + 1, False)


class Op:
    __slots__ = ("eng", "fn", "reads", "writes", "deps", "signal", "count", "is_dma", "slot", "dval", "idx",
                 "dur", "succ", "fin", "st", "why", "tag", "aset", "lat")

    def __init__(self, eng, fn, reads, writes, is_dma, dur):
        self.eng = eng
        self.fn = fn
        self.reads = reads
        self.writes = writes
        self.deps = ()
        self.signal = False
        self.count = 0
        self.is_dma = is_dma
        self.slot = None
        self.dval = 0
        self.idx = -1
        self.dur = dur
        self.succ = []
        self.fin = 0.0
        self.st = 0.0
        self.why = None
        self.tag = ""
        self.aset = None


def _ov(a, b):
    return a[1] < b[2] and b[1] < a[2] and a[3] < b[4] and b[3] < a[4]


def _cov(b, e):
    return b[1] <= e[1] and e[2] <= b[2] and b[3] <= e[3] and e[4] <= b[4]


class Rec:
    SEM_LAT = 250.0

    def __init__(self, nc, slots):
        self.nc = nc
        self.ops = []
        self.wr = {}
        self.rd = {}
        self.slots = slots
        self.const_names = set()
        self.cur_tag = ""
        self.dry = False
        self.dma_hist = {}
        self.makespan = 0.0
        import os
        self.maxops = int(os.environ["K_MAXOPS"]) if "K_MAXOPS" in os.environ else None
        self.sched = os.environ.get("K_NOSCHED") is None

    def add(self, eng, fn, reads, writes, is_dma=False, dur=100.0, extra=()):
        if self.dry or (self.maxops is not None and len(self.ops) >= self.maxops):
            return None
        op = Op(eng, fn, [_ap_box(a) for a in reads], [_ap_box(a) for a in writes], is_dma, dur)
        op.idx = len(self.ops)
        op.tag = self.cur_tag
        self.ops.append(op)
        deps = set()
        for b in op.reads:
            for (ob, oop) in self.wr.get(b[0], ()):
                if _ov(ob, b):
                    deps.add(oop)
            if b[5]:
                for (ob, oop) in self.rd.get(b[0], ()):
                    if oop.eng != eng and _ov(ob, b):
                        deps.add(oop)
        for b in op.writes:
            for (ob, oop) in self.wr.get(b[0], ()):
                if _ov(ob, b):
                    deps.add(oop)
            for (ob, oop) in self.rd.get(b[0], ()):
                if _ov(ob, b):
                    deps.add(oop)
        for x in extra:
            if x is not None:
                deps.add(x)
        if is_dma:
            hist = self.dma_hist.setdefault(eng, [])
            ns = self.slots[eng]
            if len(hist) >= ns:
                deps.add(hist[-ns])
            hist.append(op)
        deps.discard(op)
        op.deps = tuple(deps)
        for d in deps:
            d.succ.append(op)
        for b in op.writes:
            for dct in (self.wr, self.rd):
                lst = dct.get(b[0])
                if lst:
                    lst[:] = [e for e in lst if not _cov(b, e[0])]
            self.wr.setdefault(b[0], []).append((b, op))
        for b in op.reads:
            if b[0] in self.const_names:
                continue
            self.rd.setdefault(b[0], []).append((b, op))
        return op

    def schedule(self):
        import heapq
        ops = self.ops
        indeg = [len(o.deps) for o in ops]
        heaps = {}
        free = {}
        L = self.SEM_LAT

        import os
        fbonus = float(os.environ.get("K_FBONUS", "0"))

        tru = {}

        def push(o, t):
            heaps.setdefault(o.eng, [])
            tru[o.idx] = t
            if fbonus and o.tag in ("7ffn", "8ple"):
                t = t - fbonus
            heapq.heappush(heaps[o.eng], (t, o.idx))

        for o in ops:
            if indeg[o.idx] == 0:
                push(o, 0.0)
        order = []
        n = len(ops)
        hp_t = {}
        last_on = {}
        cur_set = ["A"]
        while len(order) < n:
            best = None
            for e, hp in heaps.items():
                if not hp:
                    continue
                t, i = hp[0]
                stt_ = max(t, free.get(e, 0.0))
                if best is None or (stt_, i) < best[0]:
                    best = ((stt_, i), e)
            (stt_, i), e = best
            stt_ = max(tru[i], free.get(e, 0.0))
            if e == "act" and ops[i].aset is not None and ops[i].aset != cur_set[0] and len(heaps[e]) > 1:
                cands = heapq.nsmallest(6, heaps[e])
                pick = None
                for (t_, j_) in cands:
                    if ops[j_].aset in (None, cur_set[0]) and max(t_, free.get(e, 0.0)) <= stt_ + 1300.0:
                        pick = (t_, j_)
                        break
                if pick is not None:
                    heaps[e].remove(pick)
                    heapq.heapify(heaps[e])
                    i = pick[1]
                    stt_ = max(tru[i], free.get(e, 0.0))
                else:
                    heapq.heappop(heaps[e])
            else:
                heapq.heappop(heaps[e])
            o = ops[i]
            if e == "act" and o.aset is not None:
                if o.aset != cur_set[0]:
                    stt_ += 1300.0
                cur_set[0] = o.aset
            order.append(o)
            o.st = stt_
            if stt_ > hp_t.get(o.idx, 0.0) + 1e-9:
                o.why = last_on.get(e)
            last_on[e] = o
            if o.is_dma:
                free[e] = stt_ + 60.0
            else:
                free[e] = stt_ + o.dur
            o.fin = stt_ + o.dur
            for sc_ in o.succ:
                indeg[sc_.idx] -= 1
                if indeg[sc_.idx] == 0:
                    rt = 0.0
                    for d in sc_.deps:
                        lat = 0.0 if (d.eng == "pe" and sc_.eng == "pe" and not d.is_dma and not sc_.is_dma) else L
                        if d.fin + lat > rt:
                            rt = d.fin + lat
                            sc_.why = d
                    hp_t[sc_.idx] = rt
                    push(sc_, rt)
        self.makespan = max(o.fin for o in ops)
        return order

    def schedule_hlfet(self):
        ops = self.ops
        L = self.SEM_LAT
        n = len(ops)
        bl = [0.0] * n
        for o in reversed(ops):
            m = 0.0
            for sc_ in o.succ:
                lat = 0.0 if (o.eng == "pe" and sc_.eng == "pe" and not o.is_dma and not sc_.is_dma) else L
                v = lat + bl[sc_.idx]
                if v > m:
                    m = v
            bl[o.idx] = o.dur + m
        indeg = [len(o.deps) for o in ops]
        ready = {}
        rt = {}
        free = {}
        for o in ops:
            if indeg[o.idx] == 0:
                ready.setdefault(o.eng, []).append(o.idx)
                rt[o.idx] = 0.0
        order = []
        cur_set = "A"
        while len(order) < n:
            best = None
            for e, lst in ready.items():
                if not lst:
                    continue
                f = free.get(e, 0.0)
                avail = [i for i in lst if rt[i] <= f]
                if avail:
                    if e == "act":
                        same = [i for i in avail if ops[i].aset in (None, cur_set)]
                        if same:
                            avail = same
                    c = max(avail, key=lambda i: (bl[i], -i))
                    stt_ = f
                else:
                    c = min(lst, key=lambda i: (rt[i], -bl[i]))
                    stt_ = rt[c]
                key = (stt_, -bl[c])
                if best is None or key < best[0]:
                    best = (key, e, c, stt_)
            _, e, c, stt_ = best
            ready[e].remove(c)
            o = ops[c]
            if e == "act" and o.aset is not None:
                if o.aset != cur_set:
                    stt_ += 1300.0
                cur_set = o.aset
            order.append(o)
            o.st = stt_
            free[e] = stt_ + (60.0 if o.is_dma else o.dur)
            o.fin = stt_ + o.dur
            for sc_ in o.succ:
                indeg[sc_.idx] -= 1
                if indeg[sc_.idx] == 0:
                    r = 0.0
                    for d in sc_.deps:
                        lat = 0.0 if (d.eng == "pe" and sc_.eng == "pe" and not d.is_dma and not sc_.is_dma) else L
                        if d.fin + lat > r:
                            r = d.fin + lat
                    rt[sc_.idx] = r
                    ready.setdefault(sc_.eng, []).append(sc_.idx)
        self.makespan = max(o.fin for o in ops)
        return order

    def emit(self, stack):
        nc = self.nc
        engs = {"pe": nc.tensor, "act": nc.scalar, "dve": nc.vector, "pool": nc.gpsimd, "sp": nc.sync}
        import os
        if not self.sched:
            order = list(self.ops)
        elif os.environ.get("K_SCHED", "hlfet") == "hlfet":
            order = self.schedule_hlfet()
        else:
            order = self.schedule()
        pos = {}
        for k, op in enumerate(order):
            pos[op.idx] = k
        for op in order:
            latest = {}
            for d in op.deps:
                if d.is_dma:
                    continue
                if d.eng == op.eng and not op.is_dma and d.eng == "pe":
                    continue
                cur = latest.get(d.eng)
                if cur is None or pos[d.idx] > pos[cur.idx]:
                    latest[d.eng] = d
            op.lat = latest
            for d in latest.values():
                d.signal = True
        cnt = {e: 0 for e in COMPUTE}
        dq = {}
        for op in order:
            if op.is_dma:
                k = dq.get(op.eng, 0)
                dq[op.eng] = k + 1
                ns = self.slots[op.eng]
                op.slot = (op.eng, k % ns)
                op.dval = 16 * (k // ns + 1)
            elif op.signal:
                cnt[op.eng] += 1
                op.count = cnt[op.eng]
        sems = {}
        for e in COMPUTE:
            sems[e] = stack.enter_context(nc.semaphore("s_" + e))
        for q in dq:
            for k in range(min(self.slots[q], dq[q])):
                sems[(q, k)] = stack.enter_context(nc.semaphore("d_%s_%d" % (q, k)))
        waited = {}
        nwaits = 0
        for op in order:
            e = engs[op.eng]
            need = {}
            for d in op.deps:
                if d.is_dma:
                    key, val = d.slot, d.dval
                    if need.get(key, 0) < val:
                        need[key] = val
            for eng_, d in op.lat.items():
                need[eng_] = d.count
            if op.is_dma and op.dval > 16:
                if need.get(op.slot, 0) < op.dval - 16:
                    need[op.slot] = op.dval - 16
            for key, val in need.items():
                wk = (op.eng, key)
                if waited.get(wk, 0) >= val:
                    continue
                waited[wk] = val
                e.wait_ge(sems[key], val)
                nwaits += 1
            ins = op.fn(e)
            if op.is_dma:
                ins.then_inc(sems[op.slot], 16)
            elif op.signal:
                ins.then_inc(sems[op.eng], 1)
        for q, n in dq.items():
            e = engs[q]
            ns = self.slots[q]
            for k in range(min(n, ns)):
                e.wait_ge(sems[(q, k)], 16 * ((n - 1 - k) // ns + 1))
        for ce in COMPUTE:
            if cnt[ce] > 0:
                nc.sync.wait_ge(sems[ce], cnt[ce])
        self.stats = dict(nops=len(self.ops), nwaits=nwaits, cnt=cnt, dq=dq, makespan_us=self.makespan / 1000.0)


C_QKV, C_Z, C_B, C_A, C_SQ, C_SK, C_SV, C_G = 0, 3072, 4096, 4104, 4112, 5136, 5392, 5648


def weight_groups():
    g = []
    r = lambda a, n: list(range(a, a + n))
    g.append(("tm", "w_in", 1024, r(C_B, 16) + r(C_SV, 256)))
    for hg in range(2):
        g.append(("q%d" % hg, "w_in", 1024, r(C_QKV + hg * 512, 512)))
        g.append(("k%d" % hg, "w_in", 1024, r(C_QKV + 1024 + hg * 512, 512)))
        g.append(("v%d" % hg, "w_in", 1024, r(C_QKV + 2048 + hg * 512, 512)))
        g.append(("z%d" % hg, "w_in", 1024, r(C_Z + hg * 512, 512)))
    for i in range(2):
        g.append(("sq%d" % i, "w_in", 1024, r(C_SQ + i * 512, 512)))
    cols = []
    for j in range(4):
        cols += r(C_SK + j * 64, 64) + r(C_SK + j * 64, 64)
    g.append(("skd", "w_in", 1024, cols))
    for i in range(2):
        g.append(("ga%d" % i, "w_in", 1024, r(C_G + i * 512, 512)))
        g.append(("gb%d" % i, "w_in", 1024, r(C_G + 1024 + i * 512, 512)))
    for i in range(2):
        g.append(("wo%d" % i, "w_out", 1024, r(i * 512, 512)))
    for hh in range(2):
        for gi in range(4):
            g.append(("up%d_%d" % (hh, gi), "w_up", 1024, r((hh * 16 + gi * 4) * 128, 512)))
        for cb in range(4):
            g.append(("dn%d_%d" % (hh, cb), "w_down", (hh * 2048, 2048), r(cb * 256, 256)))
    for i in range(2):
        g.append(("pg%d" % i, "w_ple_gate", 1024, r(i * 512, 512)))
    g.append(("pp", "w_ple_proj", 256, r(0, 1024)))
    return g


def group_offsets():
    offs = {}
    o = 0
    for (name, src, rows, cols) in weight_groups():
        k = rows[1] if isinstance(rows, tuple) else rows
        n = (k // 128) * len(cols)
        offs[name] = (o, k // 128, len(cols))
        o += n
    return offs, o


SP_NM, SP_NF, SP_NP, SP_CW, SP_GN, SP_QN, SP_KN, SP_SK, SP_AL, SP_DT = 0, 8, 16, 24, 120, 121, 122, 123, 131, 139
NSP = 147

CSTF_NAMES = ["ONES", "U", "U_S", "ONESSEQ_S", "DD", "DO", "DN_S"]
CSTF = {n: i * 128 for i, n in enumerate(CSTF_NAMES)}
CSTF["DC"] = len(CSTF_NAMES) * 128
CSTF["SM"] = CSTF["DC"] + 8
NCSTF = CSTF["SM"] + 16
CSTB_NAMES = ["IDENT", "ONES", "ONESB64", "L", "MS", "MI", "MS_S", "MI_S", "MD4", "ML4", "MLT4", "ML8", "MLT8",
              "ML16", "MLT16", "ML32", "MLT32", "ML64", "MLT64"]
CSTB = {n: i * 128 for i, n in enumerate(CSTB_NAMES)}
NCSTB = len(CSTB_NAMES) * 128


def make_consts():
    cf = np.zeros((128, NCSTF), np.float32)
    cb = np.zeros((128, NCSTB), np.float32)
    i = np.arange(128)
    P, Fq = np.meshgrid(i, i, indexing="ij")
    same = (P // 8) == (Fq // 8)

    def put(n, m):
        m = np.asarray(m, np.float32)
        if n in CSTF:
            cf[:, CSTF[n]:CSTF[n] + m.shape[1]] = m
        if n in CSTB:
            cb[:, CSTB[n]:CSTB[n] + m.shape[1]] = m
    put("IDENT", P == Fq)
    put("ONES", np.ones((128, 128)))
    put("ONESB64", (P // 64) == (Fq // 64))
    put("U", P <= Fq)
    put("L", P > Fq)
    put("MS", P > Fq)
    put("MI", P >= Fq)
    put("U_S", same & (P <= Fq))
    put("MS_S", same & (P > Fq))
    put("MI_S", same & (P >= Fq))
    put("ONESSEQ_S", same)
    put("DD", np.where(Fq >= P, Fq - P, BIG))
    put("DO", np.where(Fq <= P, Fq + 128 - P, BIG))
    put("DN_S", np.where(same & (Fq >= P), Fq - P, BIG))
    put("MD4", (P // 4) == (Fq // 4))
    for m in (4, 8, 16, 32, 64):
        ml = ((P // (2 * m)) == (Fq // (2 * m))) & ((P // m) == (Fq // m) + 1)
        put("ML%d" % m, ml)
        put("MLT%d" % m, ml.T)
    j = np.arange(128)[:, None]
    t = np.arange(8)[None, :]
    put("DC", np.where(j >= t, 128 + t - j, BIG))
    put("SM", (np.arange(128)[:, None] // 8) == np.arange(16)[None, :])
    return cf, cb


class Cfg:
    def __init__(self, depth=4, seq=2048, tt=256, sample=True, stages=("gdn", "swa", "ffn", "ple"), dbg=()):
        self.depth = depth
        self.seq = seq
        self.tt = tt
        self.sample = sample
        self.ntok = seq + (128 if sample else 0)
        self.stages = stages
        self.dbg = dbg
        import os
        self.nopair = os.environ.get("K_NOPAIR") is not None


def build(cfg):
    nc = bass.Bass("TRN2", target_bir_lowering=False)
    st = contextlib.ExitStack()
    DEPTH, SEQ, TT, NTOK = cfg.depth, cfg.seq, cfg.tt, cfg.ntok
    offs, TOT = group_offsets()
    import os as _os
    PEADD = _os.environ.get("K_PEADD", "1") == "1"
    SQPOOL = _os.environ.get("K_SQPOOL", "0") == "1"
    R = Rec(nc, {"sp": 12, "pool": 3, "act": 4})

    def din(name, shape, dt=F32):
        return nc.dram_tensor(name, list(shape), dt, kind="ExternalInput").ap()

    def dout(name, shape, dt=F32):
        return nc.dram_tensor(name, list(shape), dt, kind="ExternalOutput").ap()

    def sb(name, shape, dt=F32):
        return st.enter_context(nc.sbuf_tensor(name, list(shape), dt))

    xT = din("xT", [128, KC, NTOK])
    pT = din("pT", [DEPTH, 128, 2, NTOK])
    wsrc = din("wsrc", [DEPTH, 128, TOT])
    spar = din("spar", [128, DEPTH, NSP])
    cstd = din("cst", [128, NCSTF])
    cstmd = din("cstm", [128, NCSTB])
    wbf = nc.dram_tensor("wbf", [DEPTH, 128, TOT], BF16, kind="Internal").ap()
    yT = dout("yT", [128, KC, NTOK])
    o_convp = dout("o_convp", [DEPTH, 128, 24, 3])
    o_gdnp = dout("o_gdnp", [DEPTH, 128, 8, 128])
    o_kp = dout("o_kp", [DEPTH, 128, 4, 128])
    o_vp = dout("o_vp", [DEPTH, 128, 256])
    if cfg.sample:
        i_cconv = din("i_cconv", [DEPTH, 128, 24, 16, 3])
        i_sgdn = din("i_sgdn", [DEPTH, 16, 8, 128, 128])
        i_kcT = din("i_kcT", [DEPTH, 4, 128, 16, 128])
        i_vc = din("i_vc", [DEPTH, 16, 128, 256])
        i_kc = din("i_kc", [DEPTH, 16, 128, 256])
        o_convs = dout("o_convs", [DEPTH, 128, 24, 16, 3])
        o_gdns = dout("o_gdns", [DEPTH, 16, 8, 128, 128])
        o_ksn = dout("o_ksn", [DEPTH, 128, 4, 128])
        o_vsn = dout("o_vsn", [DEPTH, 128, 256])
        o_kcopy = dout("o_kcopy", [DEPTH, 16, 120, 256])
        o_vcopy = dout("o_vcopy", [DEPTH, 16, 120, 256])
    dbg_out = {}

    cst = sb("cst_f", [128, NCSTF])
    cstb = sb("cst_b", [128, NCSTB], BF16)
    sp_t = sb("sp_t", [128, DEPTH, NSP])
    spd = sb("spd", [128, DEPTH, 32])
    hbuf = [sb("h%d" % i, [128, KC, TT]) for i in range(2)]
    xn = sb("xn", [128, KC, TT], BF16)
    xnF = sb("xnF", [128, KC, TT], BF16)
    arena = sb("arena", [128, 32, TT], BF16)
    arenaF = sb("arenaF", [128, 16, TT], BF16)
    gF = arenaF[:, 0:KC, :]
    qaT = sb("qaT", [128, KC, TT], BF16)
    yqF = sb("yqF", [128, TT])
    sqbF = [sb("sqbF%d" % i, [128, TT], BF16) for i in range(2)]
    rstdF = sb("rstdF", [128, TT])
    oaT = sb("oaT", [128, KC, TT], BF16)
    obT = sb("obT", [128, KC, TT], BF16)
    ND_ = 3
    rawx = [sb("rawx%d" % i, [128, TT + 48]) for i in range(ND_)]
    caccs = [sb("cacc%d" % i, [128, TT]) for i in range(ND_)]
    ctan = [sb("ctan%d" % i, [128, TT]) for i in range(ND_)]
    yqs = [sb("yq%d" % i, [128, TT]) for i in range(ND_)]
    sqb = [sb("sqb%d" % i, [128, TT], BF16) for i in range(2)]
    rstd = [sb("rstd%d" % i, [128, TT]) for i in range(2)]
    rot = [0]
    arena_f = arena[:, :, :].rearrange("p a t -> p (a t)").bitcast(F32)
    m1s = [arena_f[:, i * TT:(i + 1) * TT] for i in range(4)]
    m2s = [arena_f[:, (4 + i) * TT:(5 + i) * TT] for i in range(3)]
    NSLOT = int(_os.environ.get("K_NSLOT", "4"))
    wring = [sb("wring%d" % i, [128, 4096], BF16) for i in range(NSLOT)]
    pTf = sb("pTf", [128, 2, TT])
    pTb = sb("pTb", [128, 2, TT], BF16)
    NBMAX = TT // 128
    tmsm = sb("tmsm", [128, NBMAX, 16])
    gsm = sb("gsm", [128, NBMAX, 96])
    vext = sb("vext", [128, NBMAX + 1, 256], BF16)
    kdup = sb("kdup", [128, 4, (NBMAX + 1) * 128], BF16)
    S_all = sb("S_all", [128, max(DEPTH, 3), 8, 128])
    Sb = sb("Sb", [128, 8, 128], BF16)
    tails = sb("tails", [128, DEPTH, 24, 3])
    kprev = sb("kprev", [128, DEPTH, 4, 128], BF16)
    vprev = sb("vprev", [128, DEPTH, 256], BF16)
    GL = sb("GL", [128, 4, 128])
    Eraw = sb("Eraw", [128, 4, 128])
    Ems = sb("Ems", [128, 4, 128], BF16)
    Emi = sb("Emi", [128, 4, 128], BF16)
    Nb = sb("Nb", [128, 4, 128], BF16)
    Pm = [sb("Pm%d" % i, [128, 4, 128], BF16) for i in range(2)]
    PTm = [sb("PTm0", [128, 4, 128], BF16)]
    X1b = sb("X1b", [128, 4, 128], BF16)
    X2b = sb("X2b", [128, 4, 128], BF16)
    No_b = sb("No_b", [128, 4, 128], BF16)
    NoT_b = sb("NoT_b", [128, 4, 128], BF16)
    Tdn = sb("Tdn", [128, 4, 128], BF16)
    TTb = sb("TTb", [128, 4, 128], BF16)
    qkb = sb("qkb", [128, 4, 128], BF16)
    qkT = sb("qkT", [128, 4, 128], BF16)
    kbg, kdec, vbt, wTb = Ems, Emi, Pm[0], Pm[1]
    u_t = GL
    vnew = X1b
    tq = sb("tq", [128, 4, 128])
    o_t = Eraw
    osq = tq
    on_b = X2b
    osm = sb("osm", [128, 16])
    sc = [sb("sc%d" % i, [128, 4, 128]) for i in range(2)]
    PTa = [sb("PTa%d" % i, [128, 4, 128], BF16) for i in range(2)]
    rden = sb("rden", [128, 2, 128])
    kfin = sc[0]
    vfin = rden[:, :, :].rearrange("p a b -> p (a b)")
    if cfg.sample:
        ccin = sb("ccin", [128, 24, 16, 3])
        ccout = sb("ccout", [128, 24, 16, 3])
        Ssf = None
        Ssb = [sb("Ssb%d" % i, [128, 8, 128], BF16) for i in range(2)]
        kcb = sb("kcb", [128, 16, 128], BF16)
        vcb = sb("vcb", [128, 16, 64], BF16)
        kdm = [sb("kdm%d" % i, [128, 4, 128], BF16) for i in range(2)]
        scs = sb("scs", [128, 16, 4, 8])
        PTc = sb("PTc", [128, 16, 4, 9], BF16)
        qsT = No_b
        uT = GL
        vnT = NoT_b
        egls = sb("egls", [128, 16, 8])
        gsq = sb("gsq", [128, 16, 8])
        snew = None
    import sys as _sys
    print("[kernel] SBUF bytes/partition remaining:", nc.sbuf_bytes_remaining, file=_sys.stderr)
    banks = [st.enter_context(nc.psum_tensor("ps%d" % i, [128, 512], F32)) for i in range(8)]
    bank_i = [0]

    bank_f = [0]
    bank_g = [0]
    NG_, NM_, NF_ = [int(x) for x in _os.environ.get("K_BANKS", "2,4,2").split(",")]

    def bank(pool="M"):
        if pool == "F":
            b = banks[NG_ + NM_ + bank_f[0] % NF_]
            bank_f[0] += 1
        elif pool == "G":
            b = banks[bank_g[0] % NG_]
            bank_g[0] += 1
        else:
            b = banks[NG_ + bank_i[0] % NM_]
            bank_i[0] += 1
        return b

    def C(name, n=128):
        return cst[:, CSTF[name]:CSTF[name] + n]

    def CB(name):
        return cstb[:, CSTB[name]:CSTB[name] + 128]

    IDb = CB("IDENT")
    ONESb = CB("ONES")
    ONES64b = CB("ONESB64")

    isap = lambda x: not isinstance(x, (int, float))

    def vdur(eng, ap):
        n = fsz(ap)
        return (n + 70) / (0.96 if eng == "dve" else 0.5) + 60.0

    def fsz(ap):
        n = 1
        for d in ap.shape[1:]:
            n *= d
        return n

    def mm(out, lhsT, rhs, start=True, stop=True, skip=False):
        d = max(64, fsz(rhs)) / 2.4 * (4.0 if rhs.dtype == F32 else 1.0) + 8.0
        R.add("pe", lambda e: e.matmul(out, lhsT=lhsT, rhs=rhs, start=start, stop=stop, skip_group_check=skip),
              [lhsT, rhs], [out], dur=d)

    def tr(out, in_, ident):
        R.add("pe", lambda e: e.transpose(out, in_, ident), [in_, ident], [out], dur=128 / 2.4 + 8.0)

    def act(out, in_, func, scale=1.0, bias=0.0):
        rd = [in_] + [x for x in (scale, bias) if isap(x)]
        o_ = R.add("act", lambda e: e.activation(out=out, in_=in_, func=func, bias=bias, scale=scale), rd, [out],
                   dur=(fsz(in_) + 200) / 1.2)
        if o_ is not None:
            o_.aset = "B" if func == AF.Ln else ("A" if func == AF.Tanh else None)

    def tt(eng, out, a, b, op):
        R.add(eng, lambda e: e.tensor_tensor(out=out, in0=a, in1=b, op=op), [a, b], [out], dur=vdur(eng, a))

    def ts(eng, out, a, s1, op0, s2=None, op1=None):
        rd = [a] + [x for x in (s1, s2) if x is not None and isap(x)]
        if op1 is None:
            R.add(eng, lambda e: e.tensor_scalar(out=out, in0=a, scalar1=s1, scalar2=None, op0=op0), rd, [out],
                  dur=vdur(eng, a))
        else:
            R.add(eng, lambda e: e.tensor_scalar(out=out, in0=a, scalar1=s1, scalar2=s2, op0=op0, op1=op1), rd, [out],
                  dur=vdur(eng, a))

    def stt(out, a, s, b, op0, op1):
        rd = [a, b] + ([s] if isap(s) else [])
        R.add("dve", lambda e: e.scalar_tensor_tensor(out=out, in0=a, scalar=s, in1=b, op0=op0, op1=op1), rd, [out],
              dur=vdur("dve", a))

    def cp(eng, out, in_):
        if eng == "act":
            act(out, in_, AF.Copy)
        else:
            R.add(eng, lambda e: e.tensor_copy(out=out, in_=in_), [in_], [out], dur=vdur(eng, in_))

    def red(out, in_, op):
        R.add("dve", lambda e: e.tensor_reduce(out=out, in_=in_, axis=AX.X, op=op), [in_], [out], dur=vdur("dve", in_))

    def recip(out, in_):
        R.add("dve", lambda e: e.reciprocal(out=out, in_=in_), [in_], [out], dur=vdur("dve", in_))

    def memset(eng, out, v):
        R.add(eng, lambda e: e.memset(out, v), [], [out], dur=vdur(eng, out))

    def dma(q, out, in_, extra=(), **kw):
        nbytes = 1
        for d in out.shape:
            nbytes *= d
        nbytes *= (4 if out.dtype == F32 else 2) + (4 if in_.dtype == F32 else 2)
        return R.add(q, lambda e: e.dma_start(out=out, in_=in_, **kw), [in_], [out], is_dma=True,
                     dur=2000.0 + nbytes / 2 / 400.0, extra=extra)

    def dbg(name, ap, shape):
        if name in cfg.dbg:
            if name not in dbg_out:
                dbg_out[name] = dout("dbg_" + name, shape)
            dma("sp", dbg_out[name], ap)

    def rsqrt_ln(out, in_, scale, eps, mult=1.0):
        act(out, in_, AF.Ln, scale=scale, bias=eps)
        act(out, out, AF.Exp, scale=-0.5, bias=float(np.log(mult)))

    dma("sp", cst[:, :], cstd)
    dma("sp", sp_t[:, :, :], spar)
    step = 2 * 4096
    o = 0
    while o < TOT:
        n = min(step, TOT - o)
        dma("pool", wbf[0, :, o:o + n], wsrc[0, :, o:o + n], max_dma_last_dim=4096)
        o += n
    cast_done = set()
    stage = S_all[:, :, :, :].rearrange("p a b c -> p (a b c)")
    dma("sp", stage[:, 0:NCSTB], cstmd)
    cp("dve", cstb[:, :], stage[:, 0:NCSTB])
    for l in range(DEPTH):
        ts("dve", spd[:, l, 0:1], sp_t[:, l, SP_GN:SP_GN + 1], 0.5, ALU.mult)
        ts("dve", spd[:, l, 1:2], sp_t[:, l, SP_QN:SP_QN + 1], 0.125, ALU.mult)
        act(spd[:, l, 2:10], sp_t[:, l, SP_SK:SP_SK + 8], AF.Exp)
        act(spd[:, l, 10:18], sp_t[:, l, SP_AL:SP_AL + 8], AF.Exp)
        ts("dve", spd[:, l, 10:18], spd[:, l, 10:18], -1.0, ALU.mult)

    R.const_names.update(["cst_f", "cst_b", "sp_t", "spd", "xT", "pT", "wsrc", "spar", "cst", "cstm", "i_cconv", "i_sgdn",
                          "i_kcT", "i_vc", "i_kc"])

    plan = []
    wstate = dict(next_load=0, next_use=0)

    def wload(i):
        l, g = plan[i]
        o, kc, ncol = offs[g]
        n = kc * ncol
        op_ = dma("sp", wring[i % NSLOT][:, 0:n], wbf[l, :, o:o + n])
        if (l, g) not in cast_done and not R.dry:
            cast_done.add((l, g))
            if l + 1 < DEPTH:
                dma("pool", wbf[l + 1, :, o:o + n], wsrc[l + 1, :, o:o + n], max_dma_last_dim=4096, extra=(op_,))

    def wget(l, g):
        i = wstate["next_use"]
        wstate["next_use"] += 1
        o, kc, ncol = offs[g]
        if R.dry:
            plan.append((l, g))
        else:
            assert plan[i] == (l, g), (plan[i], l, g)
            while wstate["next_load"] < min(len(plan), i + NSLOT):
                wload(wstate["next_load"])
                wstate["next_load"] += 1
        return wring[i % NSLOT][:, 0:kc * ncol].rearrange("p (k n) -> p k n", k=kc)

    tiles = []
    t0 = 0
    while t0 < SEQ:
        tiles.append(dict(kind="p", t0=t0, nt=TT, first=(t0 == 0), last=(t0 + TT >= SEQ), idx=len(tiles)))
        t0 += TT
    if cfg.sample:
        tiles.append(dict(kind="s", t0=SEQ, nt=128, first=True, last=True, idx=len(tiles)))

    def rmsnorm_fm(hs, xd, l, col, nt, sq2, r, pool="M"):
        ps = bank(pool)
        for kc in range(KC):
            s = sq2[kc % 2]
            act(s[:, 0:nt], hs[:, kc, 0:nt], AF.Square)
            mm(ps[:, 0:nt], ONESb, s[:, 0:nt], start=(kc == 0), stop=(kc == KC - 1))
        rsqrt_ln(r[:, 0:nt], ps[:, 0:nt], 1.0 / D_MODEL, EPS)
        for kc in range(KC):
            stt(xd[:, kc, 0:nt], hs[:, kc, 0:nt], sp_t[:, l, col + kc:col + kc + 1], r[:, 0:nt], ALU.mult, ALU.mult)

    def proj_chunk(w, c, nt, rhs_t, pool="M"):
        ps = bank(pool)
        nk = w.shape[1]
        for kc in range(nk):
            mm(ps[:, 0:nt], w[:, kc, c * 128:(c + 1) * 128], rhs_t[:, kc, 0:nt], start=(kc == 0), stop=(kc == nk - 1))
        return ps

    def gen_M(tile, l):
        kind, nt = tile["kind"], tile["nt"]
        nb = nt // 128
        samp = (kind == "s")
        ST = cfg.stages
        h = hbuf[tile["idx"] % 2]
        if not samp:
            if tile["first"]:
                memset("pool", S_all[:, l, :, :], 0.0)
                memset("pool", tails[:, l, :, :], 0.0)
            cp("act", Sb[:, :, :], S_all[:, l, :, :])
            if not tile["first"]:
                cp("pool", vext[:, 0, :], vprev[:, l, :])
                cp("pool", kdup[:, :, 0:128], kprev[:, l, :, :])
        else:
            dma("sp", ccin[:, :, :, :], i_cconv[l])
        R.cur_tag = "1norm"
        rmsnorm_fm(h, xn, l, SP_NM, nt, sqb[0:2], rstd[0])
        R.cur_tag = "2tm"
        w = wget(l, "tm")
        for b in range(nb):
            ps = bank()
            for kc in range(KC):
                mm(ps[:, 0:272], xn[:, kc, b * 128:(b + 1) * 128], w[:, kc, 0:272], start=(kc == 0), stop=(kc == KC - 1))
            cp("act", tmsm[:, b, :], ps[:, 0:16])
            cp("dve", vext[:, b + 1, :], ps[:, 16:272])
            if tile["last"] and b == nb - 1:
                cp("act", vfin[:, :], ps[:, 16:272])
                dma("sp", (o_vsn if samp else o_vp)[l], vfin[:, :])
        yield
        R.cur_tag = "3conv"
        L_ = 8 if samp else nt
        nseq = 16 if samp else 1
        for hg in range(2):
            for typ in ("q", "k", "v"):
                w = wget(l, "%s%d" % (typ, hg))
                for c in range(4):
                    ch = {"q": 0, "k": 8, "v": 16}[typ] + hg * 4 + c
                    ps = proj_chunk(w, c, nt, xn)
                    if "gdn" not in ST:
                        continue
                    rot[0] += 1
                    ri = rot[0] % ND_
                    cacc, yq = caccs[ri], yqs[ri]
                    rx = rawx[ri]
                    rxv = rx[:, 0:nseq * (L_ + 3)].rearrange("p (s t) -> p s t", s=nseq)
                    psv = ps[:, 0:nt].rearrange("p (s t) -> p s t", s=nseq)
                    cp("act", rxv[:, :, 3:3 + L_], psv)
                    if samp:
                        cp("pool", rxv[:, :, 0:3], ccin[:, ch, :, :])
                    else:
                        cp("pool", rxv[:, :, 0:3], tails[:, l, ch:ch + 1, :])
                    av = cacc[:, 0:nt].rearrange("p (s t) -> p s t", s=nseq)
                    cw = lambda j: sp_t[:, l, SP_CW + ch * 4 + j:SP_CW + ch * 4 + j + 1]
                    act(av, psv, AF.Copy, scale=cw(3))
                    for j in range(0, 3):
                        stt(av, rxv[:, :, j:j + L_], cw(j), av, ALU.mult, ALU.add)
                    if samp:
                        cp("pool", ccout[:, ch, :, :], rxv[:, :, L_:L_ + 3])
                    else:
                        cp("pool", tails[:, l, ch:ch + 1, :], rxv[:, :, L_:L_ + 3])
                    tn = ctan[ri]
                    act(tn[:, 0:nt], cacc[:, 0:nt], AF.Tanh, scale=0.5)
                    if typ == "v":
                        stt(arena[:, hg * 16 + 8 + c, 0:nt], tn[:, 0:nt], 1.0, cacc[:, 0:nt], ALU.add, ALU.mult)
                    else:
                        stt(yq[:, 0:nt], tn[:, 0:nt], 1.0, cacc[:, 0:nt], ALU.add, ALU.mult)
                        s_ = sqb[ri % 2]
                        if SQPOOL:
                            tt("pool", s_[:, 0:nt], yq[:, 0:nt], yq[:, 0:nt], ALU.mult)
                        else:
                            act(s_[:, 0:nt], yq[:, 0:nt], AF.Square)
                        pn = bank()
                        mm(pn[:, 0:nt], ONESb, s_[:, 0:nt])
                        r = rstd[ri % 2]
                        rsqrt_ln(r[:, 0:nt], pn[:, 0:nt], 1.0, 4.0 * EPS, mult=(128.0 ** -0.5 if typ == "q" else 1.0))
                        dst = arena[:, hg * 16 + (0 if typ == "q" else 4) + c, 0:nt]
                        tt("dve", dst, yq[:, 0:nt], r[:, 0:nt], ALU.mult)
                yield
            w = wget(l, "z%d" % hg)
            for c in range(4):
                ps = proj_chunk(w, c, nt, xn)
                if "gdn" not in ST:
                    continue
                tn = ctan[c % ND_]
                act(tn[:, 0:nt], ps[:, 0:nt], AF.Tanh, scale=0.5)
                stt(arena[:, hg * 16 + 12 + c, 0:nt], tn[:, 0:nt], 1.0, ps[:, 0:nt], ALU.add, ALU.mult)
            if "gdn" in ST:
                for b in range(nb):
                    R.cur_tag = "3gdn"
                    if hg == 0:
                        gdn_small(tile, l, b)
                    gdn_unit(tile, l, b, hg)
                R.cur_tag = "3conv"
            else:
                if hg == 0:
                    memset("pool", oaT[:, :, 0:nt], 0.0)
            yield
        if "gdn" in ST:
            if samp:
                dma("sp", o_convs[l], ccout[:, :, :, :])
            elif tile["last"]:
                dma("sp", o_convp[l], tails[:, l, :, :])
                dma("sp", o_gdnp[l], S_all[:, l, :, :])
        R.cur_tag = "4swa"
        qa = qaT
        for i in range(2):
            w = wget(l, "sq%d" % i)
            for c in range(4):
                ps = proj_chunk(w, c, nt, xn)
                if "swa" not in ST:
                    continue
                s_ = sqb[c % 2]
                act(s_[:, 0:nt], ps[:, 0:nt], AF.Square)
                pn = bank()
                mm(pn[:, 0:nt], ONES64b, s_[:, 0:nt])
                r = rstd[c % 2]
                rsqrt_ln(r[:, 0:nt], pn[:, 0:nt], 1.0 / 64.0, EPS)
                stt(qa[:, i * 4 + c, 0:nt], ps[:, 0:nt], spd[:, l, 1:2], r[:, 0:nt], ALU.mult, ALU.mult)
            yield
        w = wget(l, "skd")
        for c in range(4):
            ps = proj_chunk(w, c, nt, xn)
            if "swa" not in ST:
                continue
            s_ = sqb[c % 2]
            act(s_[:, 0:nt], ps[:, 0:nt], AF.Square)
            pn = bank()
            mm(pn[:, 0:nt], ONES64b, s_[:, 0:nt])
            r = rstd[c % 2]
            rsqrt_ln(r[:, 0:nt], pn[:, 0:nt], 1.0 / 64.0, EPS)
            stt(kdup[:, c, 128:128 + nt], ps[:, 0:nt], sp_t[:, l, SP_KN:SP_KN + 1], r[:, 0:nt], ALU.mult, ALU.mult)
            if tile["last"]:
                stt(kfin[:, c, :], ps[:, nt - 128:nt], sp_t[:, l, SP_KN:SP_KN + 1], r[:, nt - 128:nt], ALU.mult, ALU.mult)
        if "swa" in ST:
            if tile["last"]:
                dma("sp", (o_ksn if samp else o_kp)[l], kfin[:, :, :])
            for qb in range(nb):
                for kvh in range(4):
                    swa_unit(tile, l, qb, kvh)
            if not samp and not tile["last"]:
                cp("pool", vprev[:, l, :], vext[:, nb, :])
                cp("pool", kprev[:, l, :, :], kdup[:, :, nb * 128:(nb + 1) * 128])
            if samp:
                dma("act", o_kcopy[l], i_kc[l, :, 8:128, :])
                dma("act", o_vcopy[l], i_vc[l, :, 8:128, :])
        else:
            memset("pool", obT[:, :, 0:nt], 0.0)
        yield
        R.cur_tag = "5mix"
        for i in range(2):
            wa = wget(l, "ga%d" % i)
            for c in range(4):
                ps = proj_chunk(wa, c, nt, xn)
                yq = yqs[c % ND_]
                act(yq[:, 0:nt], ps[:, 0:nt], AF.Tanh, scale=0.5)
                stt(m1s[c][:, 0:nt], yq[:, 0:nt], 1.0, oaT[:, i * 4 + c, 0:nt], ALU.add, ALU.mult)
            yield
            wb_ = wget(l, "gb%d" % i)
            for c in range(4):
                ps = proj_chunk(wb_, c, nt, xn)
                yq = yqs[c % ND_]
                act(yq[:, 0:nt], ps[:, 0:nt], AF.Tanh, scale=0.5)
                m2 = m2s[c % 3]
                stt(m2[:, 0:nt], yq[:, 0:nt], 1.0, obT[:, i * 4 + c, 0:nt], ALU.add, ALU.mult)
                tt("dve", oaT[:, i * 4 + c, 0:nt], m1s[c][:, 0:nt], m2[:, 0:nt], ALU.add)
            yield
        R.cur_tag = "6wo"
        for i in range(2):
            w = wget(l, "wo%d" % i)
            for c in range(4):
                ps = proj_chunk(w, c, nt, oaT)
                stt(h[:, i * 4 + c, 0:nt], ps[:, 0:nt], 0.5, h[:, i * 4 + c, 0:nt], ALU.mult, ALU.add)
            yield

    def gen_F(tile, l):
        nt = tile["nt"]
        ST = cfg.stages
        h = hbuf[tile["idx"] % 2]
        R.cur_tag = "7ffn"
        if "ffn" in ST:
            rmsnorm_fm(h, xnF, l, SP_NF, nt, sqbF, rstdF, pool="F")
        hid = arenaF
        for hh in range(2):
            for gi in range(4):
                w = wget(l, "up%d_%d" % (hh, gi))
                if "ffn" in ST:
                    for c in range(4):
                        ps = proj_chunk(w, c, nt, xnF, pool="F")
                        act(yqF[:, 0:nt], ps[:, 0:nt], AF.Relu)
                        tt("pool", hid[:, gi * 4 + c, 0:nt], yqF[:, 0:nt], yqF[:, 0:nt], ALU.mult)
                yield
            for cb in range(4):
                w = wget(l, "dn%d_%d" % (hh, cb))
                if "ffn" in ST:
                    for c in range(2):
                        ps = proj_chunk(w, c, nt, hid, pool="F")
                        oc = cb * 2 + c
                        tt("dve", h[:, oc, 0:nt], ps[:, 0:nt], h[:, oc, 0:nt], ALU.add)
                yield
        R.cur_tag = "8ple"
        if "ple" in ST:
            rmsnorm_fm(h, xnF, l, SP_NP, nt, sqbF, rstdF, pool="F")
            dma("sp", pTf[:, :, 0:nt], pT[l, :, :, tile["t0"]:tile["t0"] + nt])
            cp("pool", pTb[:, :, 0:nt], pTf[:, :, 0:nt])
        for i in range(2):
            w = wget(l, "pg%d" % i)
            if "ple" in ST:
                for c in range(4):
                    ps = proj_chunk(w, c, nt, xnF, pool="F")
                    act(gF[:, i * 4 + c, 0:nt], ps[:, 0:nt], AF.Tanh, scale=0.5)
            yield
        wp = wget(l, "pp")
        if "ple" in ST:
            for oc in range(8):
                ps2 = proj_chunk(wp, oc, nt, pTb, pool="F")
                stt(yqF[:, 0:nt], gF[:, oc, 0:nt], 1.0, ps2[:, 0:nt], ALU.add, ALU.mult)
                stt(h[:, oc, 0:nt], yqF[:, 0:nt], 0.5, h[:, oc, 0:nt], ALU.mult, ALU.add)
        if l == DEPTH - 1:
            dma("sp", yT[:, :, tile["t0"]:tile["t0"] + nt], h[:, :, 0:nt])
        yield

    def gdn_small(tile, l, b):
        samp = tile["kind"] == "s"
        G = lambda a: gsm[:, b, a:a + 8]
        bl = tmsm[:, b, 0:8]
        al = tmsm[:, b, 8:16]
        act(G(80), bl, AF.Tanh, scale=0.5)
        ts("dve", G(0), G(80), 0.5, ALU.mult, 0.5, ALU.add)
        tt("dve", G(80), al, sp_t[:, l, SP_DT:SP_DT + 8], ALU.add)
        act(G(80), G(80), AF.Exp)
        act(G(80), G(80), AF.Ln, bias=1.0)
        tt("dve", G(8), G(80), spd[:, l, 10:18], ALU.mult)
        ps = bank("G")
        mm(ps[:, 0:8], C("U_S") if samp else C("U"), G(8))
        mm(ps[:, 8:16], C("ONESSEQ_S") if samp else C("ONES"), G(8))
        cp("dve", gsm[:, b, 16:32], ps[:, 0:16])
        act(G(32), G(16), AF.Exp)
        tt("dve", G(40), G(0), G(32), ALU.mult)
        tt("dve", G(80), G(24), G(16), ALU.subtract)
        act(G(48), G(80), AF.Exp)
        act(G(56), G(24), AF.Exp)
        ts("dve", G(64), G(0), -1.0, ALU.mult)
        ts("dve", G(72), G(0), 0.5, ALU.mult)
        if samp:
            tt("dve", gsq[:, :, :], G(8).unsqueeze(1).broadcast_to([128, 16, 8]),
               C("SM", 16).unsqueeze(2).broadcast_to([128, 16, 8]), ALU.mult)
            ps2 = bank("G")
            mm(ps2[:, 0:128], C("ONES"), gsq[:, :, :].rearrange("p s h -> p (s h)"))
            act(egls[:, :, :].rearrange("p s h -> p (s h)"), ps2[:, 0:128], AF.Exp)

    def bc4(ap):
        return ap.unsqueeze(2).broadcast_to([128, 4, 128])

    def hb(ap):
        return ap.unsqueeze(1).broadcast_to([128, 4, 128])

    def f4(ap):
        return ap.rearrange("p h n -> p (h n)")

    def v4(ap):
        return ap.rearrange("p (h n) -> p h n", h=4)

    def gdn_unit(tile, l, b, hg):
        samp = tile["kind"] == "s"
        H0 = hg * 4
        G = lambda a: gsm[:, b, a + H0:a + H0 + 4]
        blk = slice(b * 128, (b + 1) * 128)
        ab = hg * 16
        qT = lambda hh: arena[:, ab + 0 + hh, blk]
        kT = lambda hh: arena[:, ab + 4 + hh, blk]
        vT = lambda hh: arena[:, ab + 8 + hh, blk]
        Um = C("U_S") if samp else C("U")
        MSm = CB("MS_S") if samp else CB("MS")
        MIm = CB("MI_S") if samp else CB("MI")
        tt("dve", GL[:, :, :], bc4(G(8)), hb(CB("L")), ALU.mult)
        pd = bank("G")
        mm(pd[:, :], Um, f4(GL[:, :, :]))
        act(f4(Eraw[:, :, :]), pd[:, :], AF.Exp)
        tt("pool", Ems[:, :, :], Eraw[:, :, :], hb(MSm), ALU.mult)
        tt("pool", Emi[:, :, :], Eraw[:, :, :], hb(MIm), ALU.mult)
        pkk = bank("G")
        pqk = bank("G")
        for hh in range(4):
            mm(pkk[:, hh * 128:(hh + 1) * 128], kT(hh), kT(hh))
        for hh in range(4):
            mm(pqk[:, hh * 128:(hh + 1) * 128], qT(hh), kT(hh))
        for hh in range(4):
            stt(Nb[:, hh, :], pkk[:, hh * 128:(hh + 1) * 128], gsm[:, b, 64 + H0 + hh:64 + H0 + hh + 1], Ems[:, hh, :],
                ALU.mult, ALU.mult)
        tt("dve", f4(qkb[:, :, :]), pqk[:, :], f4(Emi[:, :, :]), ALU.mult)
        pt1 = bank("G")
        pt1b = pt1[:, :].bitcast(BF16)
        for hh in range(4):
            tr(pt1b[:, hh * 128:(hh + 1) * 128], Nb[:, hh, :], IDb)
        NTb = PTm[0]
        cp("act", f4(NTb[:, :, :]), pt1b[:, 0:512])
        pt2 = bank("G")
        pt2b = pt2[:, :].bitcast(BF16)
        for hh in range(4):
            tr(pt2b[:, hh * 128:(hh + 1) * 128], qkb[:, hh, :], IDb)
        cp("act", f4(qkT[:, :, :]), pt2b[:, 0:512])

        def mm4(lhs, rhs, add=None):
            p_ = bank("G")
            for hh in range(4):
                mm(p_[:, hh * 128:(hh + 1) * 128], lhs[:, hh, :], rhs[:, hh, :], start=True,
                   stop=(add is None or not PEADD))
                if add is not None and PEADD:
                    a_ = add if add is IDb else add[:, hh, :]
                    mm(p_[:, hh * 128:(hh + 1) * 128], IDb, a_, start=False, stop=True)
            return p_

        def evac_add(dst, p_, add, eng):
            if PEADD:
                cp(eng, f4(dst[:, :, :]), p_[:, :])
            elif add is IDb:
                tt("dve", dst[:, :, :], v4(p_[:, :]), hb(IDb), ALU.add)
            else:
                tt("dve", f4(dst[:, :, :]), p_[:, :], f4(add[:, :, :]), ALU.add)
        Nd, NdT = Pm[0], Pm[1]
        tt("dve", Nd[:, :, :], Nb[:, :, :], hb(CB("MD4")), ALU.mult)
        tt("dve", NdT[:, :, :], NTb[:, :, :], hb(CB("MD4")), ALU.mult)
        p1 = mm4(NdT, Nd, add=IDb)
        p2 = mm4(Nd, NdT, add=IDb)
        Q_, QT_, R_, RT_ = X1b, X2b, No_b, NoT_b
        evac_add(Q_, p1, IDb, "act")
        evac_add(QT_, p2, IDb, "act")
        tt("pool", R_[:, :, :], Nd[:, :, :], hb(CB("IDENT")), ALU.add)
        tt("pool", RT_[:, :, :], NdT[:, :, :], hb(CB("IDENT")), ALU.add)
        p3 = mm4(QT_, R_)
        p4 = mm4(R_, QT_)
        cp("act", f4(Tdn[:, :, :]), p3[:, :])
        cp("act", f4(TTb[:, :, :]), p4[:, :])
        levels = [4] if samp else [4, 8, 16, 32, 64]
        for li, m_ in enumerate(levels):
            last = (li == len(levels) - 1)
            tt("dve", No_b[:, :, :], Nb[:, :, :], hb(CB("ML%d" % m_)), ALU.mult)
            if not last:
                tt("dve", NoT_b[:, :, :], NTb[:, :, :], hb(CB("MLT%d" % m_)), ALU.mult)
                px1 = mm4(NoT_b, Tdn)
                cp("act", f4(X1b[:, :, :]), px1[:, :])
            px2 = mm4(No_b, TTb)
            cp("act", f4(X2b[:, :, :]), px2[:, :])
            if not last:
                py1 = mm4(TTb, X1b, add=Tdn)
            py2 = mm4(Tdn, X2b, add=TTb)
            if not last:
                evac_add(Tdn, py1, Tdn, "act")
            evac_add(TTb, py2, TTb, "dve")
        pk = bank("G")
        pkb = pk[:, :].bitcast(BF16)
        for hh in range(4):
            tr(pkb[:, hh * 128:(hh + 1) * 128], kT(hh), IDb)
        tt("dve", kbg[:, :, :], v4(pkb[:, 0:512]), bc4(G(40)), ALU.mult)
        tt("dve", kdec[:, :, :], v4(pkb[:, 0:512]), bc4(G(48)), ALU.mult)
        pv = bank("G")
        pvb = pv[:, :].bitcast(BF16)
        for hh in range(4):
            tr(pvb[:, hh * 128:(hh + 1) * 128], vT(hh), IDb)
        tt("dve", vbt[:, :, :], v4(pvb[:, 0:512]), bc4(G(72)), ALU.mult)
        pw = bank("G")
        for hh in range(4):
            mm(pw[:, hh * 128:(hh + 1) * 128], kbg[:, hh, :], TTb[:, hh, :])
        cp("act", f4(wTb[:, :, :]), pw[:, :])
        if not samp:
            pu = bank("G")
            for hh in range(4):
                mm(pu[:, hh * 128:(hh + 1) * 128], TTb[:, hh, :], vbt[:, hh, :])
            cp("act", f4(u_t[:, :, :]), pu[:, :])
            pws = bank("G")
            for hh in range(4):
                mm(pws[:, hh * 128:(hh + 1) * 128], wTb[:, hh, :], Sb[:, H0 + hh, :])
            tt("dve", f4(vnew[:, :, :]), f4(u_t[:, :, :]), pws[:, :], ALU.subtract)
            pqs = bank("G")
            for hh in range(4):
                mm(pqs[:, hh * 128:(hh + 1) * 128], qT(hh), Sb[:, H0 + hh, :])
            pin = bank("G")
            for hh in range(4):
                mm(pin[:, hh * 128:(hh + 1) * 128], qkT[:, hh, :], vnew[:, hh, :])
            tt("dve", tq[:, :, :], v4(pqs[:, :]), bc4(G(32)), ALU.mult)
            tt("dve", f4(o_t[:, :, :]), pin[:, :], f4(tq[:, :, :]), ALU.add)
            psu = bank("G")
            for hh in range(4):
                mm(psu[:, hh * 128:(hh + 1) * 128], kdec[:, hh, :], vnew[:, hh, :])
            Sv = S_all[:, l, H0:H0 + 4, :]
            tt("pool", Sv, Sv, bc4(G(56)), ALU.mult)
            tt("dve", Sv, v4(psu[:, :]), Sv, ALU.add)
            cp("act", Sb[:, H0:H0 + 4, :], Sv)
        else:
            pu = bank("G")
            for hh in range(4):
                mm(pu[:, hh * 128:(hh + 1) * 128], vbt[:, hh, :], TTb[:, hh, :])
            cp("act", f4(uT[:, :, :]), pu[:, :])
            pws = bank("G")
            pqs = bank("G")
            for sg in range(8):
                Sf_, Sb_ = S_all[:, sg % 2, :, :], Ssb[sg % 2]
                for si_ in range(2):
                    dma("sp", Sf_[:, si_ * 4:si_ * 4 + 4, :],
                        i_sgdn[l, sg * 2 + si_, H0:H0 + 4, :, :].rearrange("h d v -> d h v"))
                cp("pool", Sb_[:, :, :], Sf_[:, :, :])
                for si in range(2):
                    s = sg * 2 + si
                    cols = slice(s * 8, s * 8 + 8)
                    for hh in range(4):
                        mm(pws[:, hh * 128 + s * 8:hh * 128 + s * 8 + 8], Sb_[:, si * 4 + hh, :], wTb[:, hh, cols],
                           start=True, stop=True, skip=True)
                        mm(pqs[:, hh * 128 + s * 8:hh * 128 + s * 8 + 8], Sb_[:, si * 4 + hh, :],
                           arena[:, ab + hh, b * 128 + s * 8:b * 128 + s * 8 + 8], start=True, stop=True, skip=True)
            tt("dve", f4(vnT[:, :, :]), f4(uT[:, :, :]), pws[:, :], ALU.subtract)
            cp("act", f4(qsT[:, :, :]), pqs[:, :])
            pt3 = bank("G")
            pt3b = pt3[:, :].bitcast(BF16)
            for hh in range(4):
                tr(pt3b[:, hh * 128:(hh + 1) * 128], vnT[:, hh, :], IDb)
            cp("act", f4(vnew[:, :, :]), pt3b[:, 0:512])
            pt4 = bank("G")
            pt4b = pt4[:, :].bitcast(BF16)
            for hh in range(4):
                tr(pt4b[:, hh * 128:(hh + 1) * 128], qsT[:, hh, :], IDb)
            tt("dve", tq[:, :, :], v4(pt4b[:, 0:512]), bc4(G(32)), ALU.mult)
            pin = bank("G")
            for hh in range(4):
                mm(pin[:, hh * 128:(hh + 1) * 128], qkT[:, hh, :], vnew[:, hh, :])
            tt("dve", f4(o_t[:, :, :]), pin[:, :], f4(tq[:, :, :]), ALU.add)
            for sg in range(8):
                Sf_ = S_all[:, sg % 2, :, :]
                for si_ in range(2):
                    dma("sp", Sf_[:, si_ * 4:si_ * 4 + 4, :],
                        i_sgdn[l, sg * 2 + si_, H0:H0 + 4, :, :].rearrange("h d v -> d h v"))
                for si in range(2):
                    s = sg * 2 + si
                    km = kdm[s % 2]
                    ts("dve", km[:, :, :], kdec[:, :, :], C("SM", 16)[:, s:s + 1], ALU.mult)
                    psu = bank("G")
                    for hh in range(4):
                        mm(psu[:, hh * 128:(hh + 1) * 128], km[:, hh, :], vnew[:, hh, :])
                    sn = S_all[:, 2, (s % 2) * 4:(s % 2) * 4 + 4, :]
                    tt("pool", sn[:, :, :], Sf_[:, si * 4:si * 4 + 4, :], bc4(egls[:, s, H0:H0 + 4]), ALU.mult)
                    tt("dve", sn[:, :, :], v4(psu[:, :]), sn[:, :, :], ALU.add)
                    dst = o_gdns[l, s, H0:H0 + 4, :, :].rearrange("h d v -> d h v")
                    dma("sp", dst, sn[:, :, :])
        tt("pool", osq[:, :, :], o_t[:, :, :], o_t[:, :, :], ALU.mult)
        red(osm[:, 0:4], osq[:, :, :], ALU.add)
        rsqrt_ln(osm[:, 4:8], osm[:, 0:4], 1.0 / 128.0, EPS)
        tt("dve", on_b[:, :, :], o_t[:, :, :], bc4(osm[:, 4:8]), ALU.mult)
        po = bank("G")
        pob = po[:, :].bitcast(BF16)
        for hh in range(4):
            tr(pob[:, hh * 128:(hh + 1) * 128], on_b[:, hh, :], IDb)
        stt(oaT[:, H0:H0 + 4, blk], v4(pob[:, 0:512]), spd[:, l, 0:1], arena[:, ab + 12:ab + 16, blk], ALU.mult, ALU.mult)

    def swa_unit(tile, l, qb, kvh):
        samp = tile["kind"] == "s"
        qa = qaT
        qs = slice(qb * 128, (qb + 1) * 128)
        heads = [4 * kvh + 0, 4 * kvh + 2, 4 * kvh + 1, 4 * kvh + 3]
        kbs = []
        if not samp and not (tile["first"] and qb == 0):
            kbs.append((qb, C("DO")))
        kbs.append((qb + 1, C("DN_S") if samp else C("DD")))
        pts = []
        for i, (kb, Dm) in enumerate(kbs):
            pse, pso = bank(), bank()
            ks = slice(kb * 128, (kb + 1) * 128)
            mm(pse[:, 0:256], kdup[0:64, kvh, ks], qa[0:64, 2 * kvh:2 * kvh + 2, qs])
            mm(pso[:, 0:256], kdup[64:128, kvh, ks], qa[64:128, 2 * kvh:2 * kvh + 2, qs])
            s_ = sc[i]
            for j in range(4):
                src = (pse if j < 2 else pso)[:, (j % 2) * 128:(j % 2 + 1) * 128]
                stt(s_[:, j, :], Dm, -SLOPES[heads[j]], src, ALU.mult, ALU.add)
            act(PTa[i][:, :, :], s_[:, :, :], AF.Exp)
            pts.append((PTa[i], vext[:, kb, kvh * 64:(kvh + 1) * 64]))
        if samp:
            dma("pool", kcb[:, :, :], i_kcT[l, kvh])
            dma("pool", vcb[:, :, :], i_vc[l, :, :, kvh * 64:(kvh + 1) * 64].rearrange("s k d -> k s d"))
            pse, pso = bank(), bank()
            psve = pse[:, 0:256].rearrange("p (s j t) -> p s j t", s=16, j=2)
            psvo = pso[:, 0:256].rearrange("p (s j t) -> p s j t", s=16, j=2)
            for s in range(16):
                cols = slice(s * 8, s * 8 + 8)
                mm(psve[:, s, :, :], kcb[0:64, s, :], qa[0:64, 2 * kvh:2 * kvh + 2, cols])
                mm(psvo[:, s, :, :], kcb[64:128, s, :], qa[64:128, 2 * kvh:2 * kvh + 2, cols])
            for j in range(4):
                src = (psve if j < 2 else psvo)[:, :, j % 2, :]
                stt(scs[:, :, j, :], C("DC", 8).unsqueeze(1).broadcast_to([128, 16, 8]), -SLOPES[heads[j]],
                    src, ALU.mult, ALU.add)
            act(PTc[:, :, :, 0:8], scs[:, :, :, :], AF.Exp)
        nd = bank()
        ndv = nd[:, :].rearrange("p (a c q) -> p a c q", a=2, c=2)
        for a in range(2):
            for par in range(2):
                prt = slice(par * 64, par * 64 + 64)
                for i, (P_, v_) in enumerate(pts):
                    lhs = v_ if a == 0 else ONESb[:, 0:64]
                    mm(ndv[prt, a, :, :], lhs, P_[:, 2 * par:2 * par + 2, :], start=(i == 0),
                       stop=(i == len(pts) - 1 and not samp), skip=samp)
                if samp:
                    for s in range(16):
                        lhs = vcb[:, s, :] if a == 0 else ONESb[:, 0:64]
                        for c in range(2):
                            mm(ndv[prt, a, c, s * 8:s * 8 + 8], lhs, PTc[:, s, 2 * par + c, 0:8], start=False,
                               stop=(s == 15 and c == 1), skip=True)
        for c in range(2):
            ts("dve", rden[:, c, :], ndv[:, 1, c, :], spd[:, l, 2 + 2 * kvh + c:3 + 2 * kvh + c], ALU.add)
        recip(rden[:, :, :], rden[:, :, :])
        tt("dve", obT[:, 2 * kvh:2 * kvh + 2, qs], ndv[:, 0, :, :], rden[:, :, :], ALU.mult)

    def interleave(g1, g2):
        a_, b_ = g1, g2
        while a_ is not None or b_ is not None:
            if a_ is not None:
                try:
                    next(a_)
                except StopIteration:
                    a_ = None
            if b_ is not None:
                try:
                    next(b_)
                except StopIteration:
                    b_ = None

    def drive():
        units = []
        i = 0
        while i < len(tiles):
            if i + 1 < len(tiles) and tiles[i]["kind"] == "p" and tiles[i + 1]["kind"] == "p" and not cfg.nopair:
                for l in range(DEPTH):
                    units.append((tiles[i], l))
                    units.append((tiles[i + 1], l))
                i += 2
            else:
                for l in range(DEPTH):
                    units.append((tiles[i], l))
                i += 1
        prevF, prev_tile = None, None
        for (tile, l) in units:
            if prevF is not None and prev_tile is tile:
                interleave(prevF, None)
                prevF = None
            if l == 0:
                dma("sp", hbuf[tile["idx"] % 2][:, :, 0:tile["nt"]], xT[:, :, tile["t0"]:tile["t0"] + tile["nt"]])
            interleave(gen_M(tile, l), prevF)
            prevF, prev_tile = gen_F(tile, l), tile
        interleave(prevF, None)

    n_setup = len(R.ops)
    R.dry = True
    drive()
    R.dry = False
    assert len(R.ops) == n_setup
    bank_i[0] = 0
    bank_f[0] = 0
    bank_g[0] = 0
    wstate["next_load"] = 0
    wstate["next_use"] = 0
    drive()
    assert wstate["next_use"] == len(plan)
    R.emit(st)
    return nc, st, R, dbg_out


def pack_weights(inp, depth):
    offs, TOT = group_offsets()
    out = np.zeros((depth, 128, TOT), np.float32)
    for l in range(depth):
        for (name, src, rows, cols) in weight_groups():
            W = inp[src][l]
            if isinstance(rows, tuple):
                W = W[rows[0]:rows[0] + rows[1]]
            M = W[:, cols]
            kc = M.shape[0] // 128
            o, _, ncol = offs[name]
            out[l, :, o:o + kc * ncol] = M.reshape(kc, 128, ncol).transpose(1, 0, 2).reshape(128, kc * ncol)
    return out


def pack_small(inp, depth):
    sp = np.zeros((128, depth, NSP), np.float32)
    for l in range(depth):
        sp[:, l, SP_NM:SP_NM + 8] = inp["norm_mix"][l].reshape(8, 128).T
        sp[:, l, SP_NF:SP_NF + 8] = inp["norm_ffn"][l].reshape(8, 128).T
        sp[:, l, SP_NP:SP_NP + 8] = inp["norm_ple"][l].reshape(8, 128).T
        cw = inp["conv_w"][l]
        sp[:, l, SP_CW:SP_CW + 96] = cw.reshape(4, 24, 128).transpose(2, 1, 0).reshape(128, 96)
        sp[:, l, SP_GN] = inp["gdn_norm"][l]
        sp[:, l, SP_QN] = np.tile(inp["q_norm"][l], 2)
        sp[:, l, SP_KN] = np.tile(inp["k_norm"][l], 2)
        sk = inp["attn_sinks"][l]
        for c in range(8):
            sp[0:64, l, SP_SK + c] = sk[2 * c]
            sp[64:128, l, SP_SK + c] = sk[2 * c + 1]
        sp[:, l, SP_AL:SP_AL + 8] = inp["a_log"][l][None, :]
        sp[:, l, SP_DT:SP_DT + 8] = inp["dt_bias"][l][None, :]
    return sp


def fm(x):
    T, Dm = x.shape
    return np.ascontiguousarray(x.reshape(T, Dm // 128, 128).transpose(2, 1, 0))


def unfm(y):
    p, k, T = y.shape
    return np.ascontiguousarray(y.transpose(2, 1, 0).reshape(T, k * 128))


def make_in_maps(inp, cfg, ncores):
    depth = cfg.depth
    wsrc = pack_weights(inp, depth)
    spar = pack_small(inp, depth)
    cst = make_consts()
    maps = []
    for c in range(ncores):
        xs = [inp["x_prompt"][c, :cfg.seq]]
        ps = [inp["p_prompt"][:depth, c, :cfg.seq]]
        if cfg.sample:
            xs.append(inp["x_sample"][16 * c:16 * c + 16].reshape(128, D_MODEL))
            ps.append(inp["p_sample"][:depth, 16 * c:16 * c + 16].reshape(depth, 128, 256))
        x = np.concatenate(xs, 0)
        p = np.concatenate(ps, 1)
        m = {"xT": fm(x), "pT": np.stack([fm(p[l]) for l in range(depth)]), "wsrc": wsrc, "spar": spar,
             "cst": cst[0], "cstm": cst[1]}
        if cfg.sample:
            sl = slice(16 * c, 16 * c + 16)
            cc = inp["cache_conv"][:depth, sl]
            m["i_cconv"] = np.ascontiguousarray(cc.reshape(depth, 16, 3, 24, 128).transpose(0, 4, 3, 1, 2))
            m["i_sgdn"] = np.ascontiguousarray(inp["state_gdn"][:depth, sl])
            kc = inp["cache_swa_k"][:depth, sl]
            kt = kc.transpose(0, 3, 4, 1, 2)
            m["i_kcT"] = np.ascontiguousarray(np.concatenate([kt, kt], axis=2))
            m["i_vc"] = np.ascontiguousarray(inp["cache_swa_v"][:depth, sl].reshape(depth, 16, 128, 256))
            m["i_kc"] = np.ascontiguousarray(kc.reshape(depth, 16, 128, 256))
        maps.append(m)
    return maps


def assemble(results, cfg, ncores):
    depth, seq = cfg.depth, cfg.seq
    yp = np.zeros((ncores, seq, D_MODEL), np.float32)
    convp = np.zeros((depth, ncores, 3, 3072), np.float32)
    gdnp = np.zeros((depth, ncores, 8, 128, 128), np.float32)
    kp = np.zeros((depth, ncores, 128, 4, 64), np.float32)
    vp = np.zeros((depth, ncores, 128, 4, 64), np.float32)
    if cfg.sample:
        ys = np.zeros((ncores * 16, 8, D_MODEL), np.float32)
        convs = np.zeros((depth, ncores * 16, 3, 3072), np.float32)
        gdns = np.zeros((depth, ncores * 16, 8, 128, 128), np.float32)
        ks = np.zeros((depth, ncores * 16, 128, 4, 64), np.float32)
        vs = np.zeros((depth, ncores * 16, 128, 4, 64), np.float32)
    for c in range(ncores):
        r = results[c]
        y = unfm(r["yT"])
        yp[c] = y[:seq]
        convp[:, c] = r["o_convp"].transpose(0, 3, 2, 1).reshape(depth, 3, 3072)
        gdnp[:, c] = r["o_gdnp"].transpose(0, 2, 1, 3)
        kp[:, c] = r["o_kp"][:, 0:64].transpose(0, 3, 2, 1)
        vp[:, c] = r["o_vp"].reshape(depth, 128, 4, 64)
        if cfg.sample:
            sl = slice(16 * c, 16 * c + 16)
            ys[sl] = y[seq:].reshape(16, 8, D_MODEL)
            convs[:, sl] = r["o_convs"].transpose(0, 3, 4, 2, 1).reshape(depth, 16, 3, 3072)
            gdns[:, sl] = r["o_gdns"]
            ks[:, sl, 0:120] = r["o_kcopy"].reshape(depth, 16, 120, 4, 64)
            vs[:, sl, 0:120] = r["o_vcopy"].reshape(depth, 16, 120, 4, 64)
            kn = r["o_ksn"][:, 0:64].transpose(0, 3, 2, 1)
            ks[:, sl, 120:128] = kn.reshape(depth, 16, 8, 4, 64)
            vs[:, sl, 120:128] = r["o_vsn"].reshape(depth, 16, 8, 4, 64)
    if cfg.sample:
        return (yp, ys, convp, gdnp, kp, vp, convs, gdns, ks, vs)
    return (yp, convp, gdnp, kp, vp)


_CACHE = {}


def kernel(**inputs):
    inp = {k: np.asarray(v) for k, v in inputs.items()}
    cfg = Cfg()
    if "prog" not in _CACHE:
        _CACHE["prog"] = build(cfg)
    nc, st, R, _ = _CACHE["prog"]
    maps = make_in_maps(inp, cfg, 8)
    res = run_bass_kernel_spmd(nc, maps, core_ids=list(range(8)))
    return assemble(res.results, cfg, 8)
```

```python
import contextlib
import numpy as np
import concourse.bass as bass
import concourse.mybir as mybir
from concourse.bass_utils import run_bass_kernel_spmd

F32 = mybir.dt.float32
BF16 = mybir.dt.bfloat16
AF = mybir.ActivationFunctionType
ALU = mybir.AluOpType
AX = mybir.AxisListType

D_MODEL = 1024
KC = 8
EPS = 1e-6
BIG = 1.0e6
COMPUTE = ("pe", "act", "dve", "pool")
SLOPES = [float(2.0 ** (-8.0 * (h + 1) / 16.0)) for h in range(16)]


def _ap_box(ap):
    t = ap.tensor
    name = t.name
    dims = [(s, n) for (s, n) in ap.ap]
    off = ap.offset
    tn = type(t).__name__
    if "PSum" in tn:
        return (name, 0, 128, 0, 1 << 30, True)
    if "DRam" in tn:
        nz = [(abs(s), n) for (s, n) in dims if n > 1 and s != 0]
        if nz:
            smax, nmax = max(nz)
            rest = [(s, n) for (s, n) in dims if not (abs(s) == smax and n == nmax)]
            ext = sum(abs(s) * (n - 1) for (s, n) in rest if n > 1) + 1
            f0 = off % smax
            if len(rest) == len(dims) - 1 and f0 + ext <= smax and all(s >= 0 for s, n in dims):
                p_lo = off // smax
                return (name, p_lo, p_lo + nmax, f0, f0 + ext, False)
        lo = hi = off
        for s, n in dims:
            if n > 1:
                if s >= 0:
                    hi += s * (n - 1)
                else:
                    lo += s * (n - 1)
        return (name, 0, 1 << 30, lo, hi + 1, False)
    pst, pn = dims[0]
    if pst == 0:
        p_lo, f0 = 0, off
    else:
        p_lo = off // pst
        f0 = off - p_lo * pst
    lo = hi = f0
    for s, n in dims[1:]:
        if n > 1:
            if s >= 0:
                hi += s * (n - 1)
            else:
                lo += s * (n - 1)
    return (name, p_lo, p_lo + pn, lo, hi + 1, False)


class Op:
    __slots__ = ("eng", "fn", "reads", "writes", "deps", "signal", "count", "is_dma", "slot", "dval", "idx",
                 "dur", "succ", "fin", "st", "why", "tag", "aset", "lat")

    def __init__(self, eng, fn, reads, writes, is_dma, dur):
        self.eng = eng
        self.fn = fn
        self.reads = reads
        self.writes = writes
        self.deps = ()
        self.signal = False
        self.count = 0
        self.is_dma = is_dma
        self.slot = None
        self.dval = 0
        self.idx = -1
        self.dur = dur
        self.succ = []
        self.fin = 0.0
        self.st = 0.0
        self.why = None
        self.tag = ""
        self.aset = None


def _ov(a, b):
    return a[1] < b[2] and b[1] < a[2] and a[3] < b[4] and b[3] < a[4]


def _cov(b, e):
    return b[1] <= e[1] and e[2] <= b[2] and b[3] <= e[3] and e[4] <= b[4]


class Rec:
    SEM_LAT = 250.0

    def __init__(self, nc, slots):
        self.nc = nc
        self.ops = []
        self.wr = {}
        self.rd = {}
        self.slots = slots
        self.const_names = set()
        self.cur_tag = ""
        self.dry = False
        self.dma_hist = {}
        self.makespan = 0.0
        import os
        self.maxops = int(os.environ["K_MAXOPS"]) if "K_MAXOPS" in os.environ else None
        self.sched = os.environ.get("K_NOSCHED") is None

    def add(self, eng, fn, reads, writes, is_dma=False, dur=100.0, extra=()):
        if self.dry or (self.maxops is not None and len(self.ops) >= self.maxops):
            return None
        op = Op(eng, fn, [_ap_box(a) for a in reads], [_ap_box(a) for a in writes], is_dma, dur)
        op.idx = len(self.ops)
        op.tag = self.cur_tag
        self.ops.append(op)
        deps = set()
        for b in op.reads:
            for (ob, oop) in self.wr.get(b[0], ()):
                if _ov(ob, b):
                    deps.add(oop)
            if b[5]:
                for (ob, oop) in self.rd.get(b[0], ()):
                    if oop.eng != eng and _ov(ob, b):
                        deps.add(oop)
        for b in op.writes:
            for (ob, oop) in self.wr.get(b[0], ()):
                if _ov(ob, b):
                    deps.add(oop)
            for (ob, oop) in self.rd.get(b[0], ()):
                if _ov(ob, b):
                    deps.add(oop)
        for x in extra:
            if x is not None:
                deps.add(x)
        if is_dma:
            hist = self.dma_hist.setdefault(eng, [])
            ns = self.slots[eng]
            if len(hist) >= ns:
                deps.add(hist[-ns])
            hist.append(op)
        deps.discard(op)
        op.deps = tuple(deps)
        for d in deps:
            d.succ.append(op)
        for b in op.writes:
            for dct in (self.wr, self.rd):
                lst = dct.get(b[0])
                if lst:
                    lst[:] = [e for e in lst if not _cov(b, e[0])]
            self.wr.setdefault(b[0], []).append((b, op))
        for b in op.reads:
            if b[0] in self.const_names:
                continue
            self.rd.setdefault(b[0], []).append((b, op))
        return op

    def schedule(self):
        import heapq
        ops = self.ops
        indeg = [len(o.deps) for o in ops]
        heaps = {}
        free = {}
        L = self.SEM_LAT

        import os
        fbonus = float(os.environ.get("K_FBONUS", "0"))

        tru = {}

        def push(o, t):
            heaps.setdefault(o.eng, [])
            tru[o.idx] = t
            if fbonus and o.tag in ("7ffn", "8ple"):
                t = t - fbonus
            heapq.heappush(heaps[o.eng], (t, o.idx))

        for o in ops:
            if indeg[o.idx] == 0:
                push(o, 0.0)
        order = []
        n = len(ops)
        hp_t = {}
        last_on = {}
        cur_set = ["A"]
        while len(order) < n:
            best = None
            for e, hp in heaps.items():
                if not hp:
                    continue
                t, i = hp[0]
                stt_ = max(t, free.get(e, 0.0))
                if best is None or (stt_, i) < best[0]:
                    best = ((stt_, i), e)
            (stt_, i), e = best
            stt_ = max(tru[i], free.get(e, 0.0))
            if e == "act" and ops[i].aset is not None and ops[i].aset != cur_set[0] and len(heaps[e]) > 1:
                cands = heapq.nsmallest(6, heaps[e])
                pick = None
                for (t_, j_) in cands:
                    if ops[j_].aset in (None, cur_set[0]) and max(t_, free.get(e, 0.0)) <= stt_ + 1300.0:
                        pick = (t_, j_)
                        break
                if pick is not None:
                    heaps[e].remove(pick)
                    heapq.heapify(heaps[e])
                    i = pick[1]
                    stt_ = max(tru[i], free.get(e, 0.0))
                else:
                    heapq.heappop(heaps[e])
            else:
                heapq.heappop(heaps[e])
            o = ops[i]
            if e == "act" and o.aset is not None:
                if o.aset != cur_set[0]:
                    stt_ += 1300.0
                cur_set[0] = o.aset
            order.append(o)
            o.st = stt_
            if stt_ > hp_t.get(o.idx, 0.0) + 1e-9:
                o.why = last_on.get(e)
            last_on[e] = o
            if o.is_dma:
                free[e] = stt_ + 60.0
            else:
                free[e] = stt_ + o.dur
            o.fin = stt_ + o.dur
            for sc_ in o.succ:
                indeg[sc_.idx] -= 1
                if indeg[sc_.idx] == 0:
                    rt = 0.0
                    for d in sc_.deps:
                        lat = 0.0 if (d.eng == "pe" and sc_.eng == "pe" and not d.is_dma and not sc_.is_dma) else L
                        if d.fin + lat > rt:
                            rt = d.fin + lat
                            sc_.why = d
                    hp_t[sc_.idx] = rt
                    push(sc_, rt)
        self.makespan = max(o.fin for o in ops)
        return order

    def schedule_hlfet(self):
        ops = self.ops
        L = self.SEM_LAT
        n = len(ops)
        bl = [0.0] * n
        for o in reversed(ops):
            m = 0.0
            for sc_ in o.succ:
                lat = 0.0 if (o.eng == "pe" and sc_.eng == "pe" and not o.is_dma and not sc_.is_dma) else L
                v = lat + bl[sc_.idx]
                if v > m:
                    m = v
            bl[o.idx] = o.dur + m
        indeg = [len(o.deps) for o in ops]
        ready = {}
        rt = {}
        free = {}
        for o in ops:
            if indeg[o.idx] == 0:
                ready.setdefault(o.eng, []).append(o.idx)
                rt[o.idx] = 0.0
        order = []
        cur_set = "A"
        while len(order) < n:
            best = None
            for e, lst in ready.items():
                if not lst:
                    continue
                f = free.get(e, 0.0)
                avail = [i for i in lst if rt[i] <= f]
                if avail:
                    if e == "act":
                        same = [i for i in avail if ops[i].aset in (None, cur_set)]
                        if same:
                            avail = same
                    c = max(avail, key=lambda i: (bl[i], -i))
                    stt_ = f
                else:
                    c = min(lst, key=lambda i: (rt[i], -bl[i]))
                    stt_ = rt[c]
                key = (stt_, -bl[c])
                if best is None or key < best[0]:
                    best = (key, e, c, stt_)
            _, e, c, stt_ = best
            ready[e].remove(c)
            o = ops[c]
            if e == "act" and o.aset is not None:
                if o.aset != cur_set:
                    stt_ += 1300.0
                cur_set = o.aset
            order.append(o)
            o.st = stt_
            free[e] = stt_ + (60.0 if o.is_dma else o.dur)
            o.fin = stt_ + o.dur
            for sc_ in o.succ:
                indeg[sc_.idx] -= 1
                if indeg[sc_.idx] == 0:
                    r = 0.0
                    for d in sc_.deps:
                        lat = 0.0 if (d.eng == "pe" and sc_.eng == "pe" and not d.is_dma and not sc_.is_dma) else L
                        if d.fin + lat > r:
                            r = d.fin + lat
                    rt[sc_.idx] = r
                    ready.setdefault(sc_.eng, []).append(sc_.idx)
        self.makespan = max(o.fin for o in ops)
        return order

    def emit(self, stack):
        nc = self.nc
        engs = {"pe": nc.tensor, "act": nc.scalar, "dve": nc.vector, "pool": nc.gpsimd, "sp": nc.sync}
        import os
        if not self.sched:
            order = list(self.ops)
        elif os.environ.get("K_SCHED", "hlfet") == "hlfet":
            order = self.schedule_hlfet()
        else:
            order = self.schedule()
        pos = {}
        for k, op in enumerate(order):
            pos[op.idx] = k
        for op in order:
            latest = {}
            for d in op.deps:
                if d.is_dma:
                    continue
                if d.eng == op.eng and not op.is_dma and d.eng == "pe":
                    continue
                cur = latest.get(d.eng)
                if cur is None or pos[d.idx] > pos[cur.idx]:
                    latest[d.eng] = d
            op.lat = latest
            for d in latest.values():
                d.signal = True
        cnt = {e: 0 for e in COMPUTE}
        dq = {}
        for op in order:
            if op.is_dma:
                k = dq.get(op.eng, 0)
                dq[op.eng] = k + 1
                ns = self.slots[op.eng]
                op.slot = (op.eng, k % ns)
                op.dval = 16 * (k // ns + 1)
            elif op.signal:
                cnt[op.eng] += 1
                op.count = cnt[op.eng]
        sems = {}
        for e in COMPUTE:
            sems[e] = stack.enter_context(nc.semaphore("s_" + e))
        for q in dq:
            for k in range(min(self.slots[q], dq[q])):
                sems[(q, k)] = stack.enter_context(nc.semaphore("d_%s_%d" % (q, k)))
        waited = {}
        nwaits = 0
        for op in order:
            e = engs[op.eng]
            need = {}
            for d in op.deps:
                if d.is_dma:
                    key, val = d.slot, d.dval
                    if need.get(key, 0) < val:
                        need[key] = val
            for eng_, d in op.lat.items():
                need[eng_] = d.count
            if op.is_dma and op.dval > 16:
                if need.get(op.slot, 0) < op.dval - 16:
                    need[op.slot] = op.dval - 16
            for key, val in need.items():
                wk = (op.eng, key)
                if waited.get(wk, 0) >= val:
                    continue
                waited[wk] = val
                e.wait_ge(sems[key], val)
                nwaits += 1
            ins = op.fn(e)
            if op.is_dma:
                ins.then_inc(sems[op.slot], 16)
            elif op.signal:
                ins.then_inc(sems[op.eng], 1)
        for q, n in dq.items():
            e = engs[q]
            ns = self.slots[q]
            for k in range(min(n, ns)):
                e.wait_ge(sems[(q, k)], 16 * ((n - 1 - k) // ns + 1))
        for ce in COMPUTE:
            if cnt[ce] > 0:
                nc.sync.wait_ge(sems[ce], cnt[ce])
        self.stats = dict(nops=len(self.ops), nwaits=nwaits, cnt=cnt, dq=dq, makespan_us=self.makespan / 1000.0)


C_QKV, C_Z, C_B, C_A, C_SQ, C_SK, C_SV, C_G = 0, 3072, 4096, 4104, 4112, 5136, 5392, 5648


def weight_groups():
    g = []
    r = lambda a, n: list(range(a, a + n))
    g.append(("tm", "w_in", 1024, r(C_B, 16) + r(C_SV, 256)))
    for hg in range(2):
        g.append(("q%d" % hg, "w_in", 1024, r(C_QKV + hg * 512, 512)))
        g.append(("k%d" % hg, "w_in", 1024, r(C_QKV + 1024 + hg * 512, 512)))
        g.append(("v%d" % hg, "w_in", 1024, r(C_QKV + 2048 + hg * 512, 512)))
        g.append(("z%d" % hg, "w_in", 1024, r(C_Z + hg * 512, 512)))
    for i in range(2):
        g.append(("sq%d" % i, "w_in", 1024, r(C_SQ + i * 512, 512)))
    cols = []
    for j in range(4):
        cols += r(C_SK + j * 64, 64) + r(C_SK + j * 64, 64)
    g.append(("skd", "w_in", 1024, cols))
    for i in range(2):
        g.append(("ga%d" % i, "w_in", 1024, r(C_G + i * 512, 512)))
        g.append(("gb%d" % i, "w_in", 1024, r(C_G + 1024 + i * 512, 512)))
    for i in range(2):
        g.append(("wo%d" % i, "w_out", 1024, r(i * 512, 512)))
    for hh in range(2):
        for gi in range(4):
            g.append(("up%d_%d" % (hh, gi), "w_up", 1024, r((hh * 16 + gi * 4) * 128, 512)))
        for cb in range(4):
            g.append(("dn%d_%d" % (hh, cb), "w_down", (hh * 2048, 2048), r(cb * 256, 256)))
    for i in range(2):
        g.append(("pg%d" % i, "w_ple_gate", 1024, r(i * 512, 512)))
    g.append(("pp", "w_ple_proj", 256, r(0, 1024)))
    return g


def group_offsets():
    offs = {}
    o = 0
    for (name, src, rows, cols) in weight_groups():
        k = rows[1] if isinstance(rows, tuple) else rows
        n = (k // 128) * len(cols)
        offs[name] = (o, k // 128, len(cols))
        o += n
    return offs, o


SP_NM, SP_NF, SP_NP, SP_CW, SP_GN, SP_QN, SP_KN, SP_SK, SP_AL, SP_DT = 0, 8, 16, 24, 120, 121, 122, 123, 131, 139
NSP = 147

CSTF_NAMES = ["ONES", "U", "U_S", "ONESSEQ_S", "DD", "DO", "DN_S"]
CSTF = {n: i * 128 for i, n in enumerate(CSTF_NAMES)}
CSTF["DC"] = len(CSTF_NAMES) * 128
CSTF["SM"] = CSTF["DC"] + 8
NCSTF = CSTF["SM"] + 16
CSTB_NAMES = ["IDENT", "ONES", "ONESB64", "L", "MS", "MI", "MS_S", "MI_S", "MD4", "ML4", "MLT4", "ML8", "MLT8",
              "ML16", "MLT16", "ML32", "MLT32", "ML64", "MLT64"]
CSTB = {n: i * 128 for i, n in enumerate(CSTB_NAMES)}
NCSTB = len(CSTB_NAMES) * 128


def make_consts():
    cf = np.zeros((128, NCSTF), np.float32)
    cb = np.zeros((128, NCSTB), np.float32)
    i = np.arange(128)
    P, Fq = np.meshgrid(i, i, indexing="ij")
    same = (P // 8) == (Fq // 8)

    def put(n, m):
        m = np.asarray(m, np.float32)
        if n in CSTF:
            cf[:, CSTF[n]:CSTF[n] + m.shape[1]] = m
        if n in CSTB:
            cb[:, CSTB[n]:CSTB[n] + m.shape[1]] = m
    put("IDENT", P == Fq)
    put("ONES", np.ones((128, 128)))
    put("ONESB64", (P // 64) == (Fq // 64))
    put("U", P <= Fq)
    put("L", P > Fq)
    put("MS", P > Fq)
    put("MI", P >= Fq)
    put("U_S", same & (P <= Fq))
    put("MS_S", same & (P > Fq))
    put("MI_S", same & (P >= Fq))
    put("ONESSEQ_S", same)
    put("DD", np.where(Fq >= P, Fq - P, BIG))
    put("DO", np.where(Fq <= P, Fq + 128 - P, BIG))
    put("DN_S", np.where(same & (Fq >= P), Fq - P, BIG))
    put("MD4", (P // 4) == (Fq // 4))
    for m in (4, 8, 16, 32, 64):
        ml = ((P // (2 * m)) == (Fq // (2 * m))) & ((P // m) == (Fq // m) + 1)
        put("ML%d" % m, ml)
        put("MLT%d" % m, ml.T)
    j = np.arange(128)[:, None]
    t = np.arange(8)[None, :]
    put("DC", np.where(j >= t, 128 + t - j, BIG))
    put("SM", (np.arange(128)[:, None] // 8) == np.arange(16)[None, :])
    return cf, cb


class Cfg:
    def __init__(self, depth=4, seq=2048, tt=256, sample=True, stages=("gdn", "swa", "ffn", "ple"), dbg=()):
        self.depth = depth
        self.seq = seq
        self.tt = tt
        self.sample = sample
        self.ntok = seq + (128 if sample else 0)
        self.stages = stages
        self.dbg = dbg
        import os
        self.nopair = os.environ.get("K_NOPAIR") is not None


def build(cfg):
    nc = bass.Bass("TRN2", target_bir_lowering=False)
    st = contextlib.ExitStack()
    DEPTH, SEQ, TT, NTOK = cfg.depth, cfg.seq, cfg.tt, cfg.ntok
    offs, TOT = group_offsets()
    import os as _os
    PEADD = _os.environ.get("K_PEADD", "0") == "1"
    SQPOOL = _os.environ.get("K_SQPOOL", "0") == "1"
    POOLMIX = _os.environ.get("K_POOLMIX", "s")
    R = Rec(nc, {"sp": 12, "pool": 3, "act": 4})

    def din(name, shape, dt=F32):
        return nc.dram_tensor(name, list(shape), dt, kind="ExternalInput").ap()

    def dout(name, shape, dt=F32):
        return nc.dram_tensor(name, list(shape), dt, kind="ExternalOutput").ap()

    def sb(name, shape, dt=F32):
        return st.enter_context(nc.sbuf_tensor(name, list(shape), dt))

    xT = din("xT", [128, KC, NTOK])
    pT = din("pT", [DEPTH, 128, 2, NTOK])
    wsrc = din("wsrc", [DEPTH, 128, TOT])
    spar = din("spar", [128, DEPTH, NSP])
    cstd = din("cst", [128, NCSTF])
    cstmd = din("cstm", [128, NCSTB])
    wbf = nc.dram_tensor("wbf", [DEPTH, 128, TOT], BF16, kind="Internal").ap()
    yT = dout("yT", [128, KC, NTOK])
    o_convp = dout("o_convp", [DEPTH, 128, 24, 3])
    o_gdnp = dout("o_gdnp", [DEPTH, 128, 8, 128])
    o_kp = dout("o_kp", [DEPTH, 128, 4, 128])
    o_vp = dout("o_vp", [DEPTH, 128, 256])
    if cfg.sample:
        i_cconv = din("i_cconv", [DEPTH, 128, 24, 16, 3])
        i_sgdn = din("i_sgdn", [DEPTH, 16, 8, 128, 128])
        i_kcT = din("i_kcT", [DEPTH, 4, 128, 16, 128])
        i_vc = din("i_vc", [DEPTH, 16, 128, 256])
        i_kc = din("i_kc", [DEPTH, 16, 128, 256])
        o_convs = dout("o_convs", [DEPTH, 128, 24, 16, 3])
        o_gdns = dout("o_gdns", [DEPTH, 16, 8, 128, 128])
        o_ksn = dout("o_ksn", [DEPTH, 128, 4, 128])
        o_vsn = dout("o_vsn", [DEPTH, 128, 256])
        o_kcopy = dout("o_kcopy", [DEPTH, 16, 120, 256])
        o_vcopy = dout("o_vcopy", [DEPTH, 16, 120, 256])
    dbg_out = {}

    cst = sb("cst_f", [128, NCSTF])
    cstb = sb("cst_b", [128, NCSTB], BF16)
    sp_t = sb("sp_t", [128, DEPTH, NSP])
    spd = sb("spd", [128, DEPTH, 32])
    hbuf = [sb("h%d" % i, [128, KC, TT]) for i in range(2)]
    xn = sb("xn", [128, KC, TT], BF16)
    xnF = sb("xnF", [128, KC, TT], BF16)
    arena = sb("arena", [128, 32, TT], BF16)
    arenaF = sb("arenaF", [128, 16, TT], BF16)
    gF = arenaF[:, 0:KC, :]
    qaT = sb("qaT", [128, KC, TT], BF16)
    yqF = sb("yqF", [128, TT])
    sqbF = [sb("sqbF%d" % i, [128, TT], BF16) for i in range(2)]
    rstdF = sb("rstdF", [128, TT])
    oaT = sb("oaT", [128, KC, TT], BF16)
    obT = sb("obT", [128, KC, TT], BF16)
    ND_ = 3
    rawx = [sb("rawx%d" % i, [128, TT + 48]) for i in range(ND_)]
    caccs = [sb("cacc%d" % i, [128, TT]) for i in range(ND_)]
    ctan = [sb("ctan%d" % i, [128, TT]) for i in range(ND_)]
    yqs = [sb("yq%d" % i, [128, TT]) for i in range(ND_)]
    sqb = [sb("sqb%d" % i, [128, TT], BF16) for i in range(2)]
    rstd = [sb("rstd%d" % i, [128, TT]) for i in range(2)]
    rot = [0]
    arena_f = arena[:, :, :].rearrange("p a t -> p (a t)").bitcast(F32)
    m1s = [arena_f[:, i * TT:(i + 1) * TT] for i in range(4)]
    m2s = [arena_f[:, (4 + i) * TT:(5 + i) * TT] for i in range(3)]
    NSLOT = int(_os.environ.get("K_NSLOT", "4"))
    wring = [sb("wring%d" % i, [128, 4096], BF16) for i in range(NSLOT)]
    pTf = sb("pTf", [128, 2, TT])
    pTb = sb("pTb", [128, 2, TT], BF16)
    NBMAX = TT // 128
    tmsm = sb("tmsm", [128, NBMAX, 16])
    gsm = sb("gsm", [128, NBMAX, 96])
    vext = sb("vext", [128, NBMAX + 1, 256], BF16)
    kdup = sb("kdup", [128, 4, (NBMAX + 1) * 128], BF16)
    S_all = sb("S_all", [128, max(DEPTH, 3), 8, 128])
    Sb = sb("Sb", [128, 8, 128], BF16)
    tails = sb("tails", [128, DEPTH, 24, 3])
    kprev = sb("kprev", [128, DEPTH, 4, 128], BF16)
    vprev = sb("vprev", [128, DEPTH, 256], BF16)
    GL = sb("GL", [128, 4, 128])
    Eraw = sb("Eraw", [128, 4, 128])
    Ems = sb("Ems", [128, 4, 128], BF16)
    Emi = sb("Emi", [128, 4, 128], BF16)
    Nb = sb("Nb", [128, 4, 128], BF16)
    Pm = [sb("Pm%d" % i, [128, 4, 128], BF16) for i in range(2)]
    PTm = [sb("PTm0", [128, 4, 128], BF16)]
    X1b = sb("X1b", [128, 4, 128], BF16)
    X2b = sb("X2b", [128, 4, 128], BF16)
    No_b = sb("No_b", [128, 4, 128], BF16)
    NoT_b = sb("NoT_b", [128, 4, 128], BF16)
    Tdn = sb("Tdn", [128, 4, 128], BF16)
    TTb = sb("TTb", [128, 4, 128], BF16)
    qkb = sb("qkb", [128, 4, 128], BF16)
    qkT = sb("qkT", [128, 4, 128], BF16)
    kbg, kdec, vbt, wTb = Ems, Emi, Pm[0], Pm[1]
    u_t = GL
    vnew = X1b
    tq = sb("tq", [128, 4, 128])
    o_t = Eraw
    osq = tq
    on_b = X2b
    osm = sb("osm", [128, 16])
    sc = [sb("sc%d" % i, [128, 4, 128]) for i in range(2)]
    PTa = [sb("PTa%d" % i, [128, 4, 128], BF16) for i in range(2)]
    rden = sb("rden", [128, 2, 128])
    kfin = sc[0]
    vfin = rden[:, :, :].rearrange("p a b -> p (a b)")
    if cfg.sample:
        ccin = sb("ccin", [128, 24, 16, 3])
        ccout = sb("ccout", [128, 24, 16, 3])
        Ssf = None
        Ssb = [sb("Ssb%d" % i, [128, 8, 128], BF16) for i in range(2)]
        kcb = sb("kcb", [128, 16, 128], BF16)
        vcb = sb("vcb", [128, 16, 64], BF16)
        kdm = [sb("kdm%d" % i, [128, 4, 128], BF16) for i in range(2)]
        scs = sb("scs", [128, 16, 4, 8])
        PTc = sb("PTc", [128, 16, 4, 9], BF16)
        qsT = No_b
        uT = GL
        vnT = NoT_b
        egls = sb("egls", [128, 16, 8])
        gsq = sb("gsq", [128, 16, 8])
        snew = None
    import sys as _sys
    print("[kernel] SBUF bytes/partition remaining:", nc.sbuf_bytes_remaining, file=_sys.stderr)
    banks = [st.enter_context(nc.psum_tensor("ps%d" % i, [128, 512], F32)) for i in range(8)]
    bank_i = [0]

    bank_f = [0]
    bank_g = [0]
    NG_, NM_, NF_ = [int(x) for x in _os.environ.get("K_BANKS", "2,4,2").split(",")]

    def bank(pool="M"):
        if pool == "F":
            b = banks[NG_ + NM_ + bank_f[0] % NF_]
            bank_f[0] += 1
        elif pool == "G":
            b = banks[bank_g[0] % NG_]
            bank_g[0] += 1
        else:
            b = banks[NG_ + bank_i[0] % NM_]
            bank_i[0] += 1
        return b

    def C(name, n=128):
        return cst[:, CSTF[name]:CSTF[name] + n]

    def CB(name):
        return cstb[:, CSTB[name]:CSTB[name] + 128]

    IDb = CB("IDENT")
    ONESb = CB("ONES")
    ONES64b = CB("ONESB64")

    isap = lambda x: not isinstance(x, (int, float))

    def vdur(eng, ap):
        n = fsz(ap)
        return (n + 70) / (0.96 if eng == "dve" else 0.5) + 60.0

    def fsz(ap):
        n = 1
        for d in ap.shape[1:]:
            n *= d
        return n

    def mm(out, lhsT, rhs, start=True, stop=True, skip=False):
        d = max(64, fsz(rhs)) / 2.4 * (4.0 if rhs.dtype == F32 else 1.0) + 8.0
        R.add("pe", lambda e: e.matmul(out, lhsT=lhsT, rhs=rhs, start=start, stop=stop, skip_group_check=skip),
              [lhsT, rhs], [out], dur=d)

    def tr(out, in_, ident):
        R.add("pe", lambda e: e.transpose(out, in_, ident), [in_, ident], [out], dur=128 / 2.4 + 8.0)

    def act(out, in_, func, scale=1.0, bias=0.0):
        rd = [in_] + [x for x in (scale, bias) if isap(x)]
        o_ = R.add("act", lambda e: e.activation(out=out, in_=in_, func=func, bias=bias, scale=scale), rd, [out],
                   dur=(fsz(in_) + 200) / 1.2)
        if o_ is not None:
            o_.aset = "B" if func == AF.Ln else ("A" if func == AF.Tanh else None)

    def tt(eng, out, a, b, op):
        R.add(eng, lambda e: e.tensor_tensor(out=out, in0=a, in1=b, op=op), [a, b], [out], dur=vdur(eng, a))

    def ts(eng, out, a, s1, op0, s2=None, op1=None):
        rd = [a] + [x for x in (s1, s2) if x is not None and isap(x)]
        if op1 is None:
            R.add(eng, lambda e: e.tensor_scalar(out=out, in0=a, scalar1=s1, scalar2=None, op0=op0), rd, [out],
                  dur=vdur(eng, a))
        else:
            R.add(eng, lambda e: e.tensor_scalar(out=out, in0=a, scalar1=s1, scalar2=s2, op0=op0, op1=op1), rd, [out],
                  dur=vdur(eng, a))

    def stt(out, a, s, b, op0, op1):
        rd = [a, b] + ([s] if isap(s) else [])
        R.add("dve", lambda e: e.scalar_tensor_tensor(out=out, in0=a, scalar=s, in1=b, op0=op0, op1=op1), rd, [out],
              dur=vdur("dve", a))

    def cp(eng, out, in_):
        if eng == "act":
            act(out, in_, AF.Copy)
        else:
            R.add(eng, lambda e: e.tensor_copy(out=out, in_=in_), [in_], [out], dur=vdur(eng, in_))

    def red(out, in_, op):
        R.add("dve", lambda e: e.tensor_reduce(out=out, in_=in_, axis=AX.X, op=op), [in_], [out], dur=vdur("dve", in_))

    def recip(out, in_):
        R.add("dve", lambda e: e.reciprocal(out=out, in_=in_), [in_], [out], dur=vdur("dve", in_))

    def memset(eng, out, v):
        R.add(eng, lambda e: e.memset(out, v), [], [out], dur=vdur(eng, out))

    def dma(q, out, in_, extra=(), **kw):
        nbytes = 1
        for d in out.shape:
            nbytes *= d
        nbytes *= (4 if out.dtype == F32 else 2) + (4 if in_.dtype == F32 else 2)
        return R.add(q, lambda e: e.dma_start(out=out, in_=in_, **kw), [in_], [out], is_dma=True,
                     dur=2000.0 + nbytes / 2 / 400.0, extra=extra)

    def dbg(name, ap, shape):
        if name in cfg.dbg:
            if name not in dbg_out:
                dbg_out[name] = dout("dbg_" + name, shape)
            dma("sp", dbg_out[name], ap)

    def rsqrt_ln(out, in_, scale, eps, mult=1.0):
        act(out, in_, AF.Ln, scale=scale, bias=eps)
        act(out, out, AF.Exp, scale=-0.5, bias=float(np.log(mult)))

    dma("sp", cst[:, :], cstd)
    dma("sp", sp_t[:, :, :], spar)
    step = 2 * 4096
    o = 0
    while o < TOT:
        n = min(step, TOT - o)
        dma("pool", wbf[0, :, o:o + n], wsrc[0, :, o:o + n], max_dma_last_dim=4096)
        o += n
    cast_done = set()
    stage = S_all[:, :, :, :].rearrange("p a b c -> p (a b c)")
    dma("sp", stage[:, 0:NCSTB], cstmd)
    cp("dve", cstb[:, :], stage[:, 0:NCSTB])
    for l in range(DEPTH):
        ts("dve", spd[:, l, 0:1], sp_t[:, l, SP_GN:SP_GN + 1], 0.5, ALU.mult)
        ts("dve", spd[:, l, 1:2], sp_t[:, l, SP_QN:SP_QN + 1], 0.125, ALU.mult)
        act(spd[:, l, 2:10], sp_t[:, l, SP_SK:SP_SK + 8], AF.Exp)
        act(spd[:, l, 10:18], sp_t[:, l, SP_AL:SP_AL + 8], AF.Exp)
        ts("dve", spd[:, l, 10:18], spd[:, l, 10:18], -1.0, ALU.mult)

    R.const_names.update(["cst_f", "cst_b", "sp_t", "spd", "xT", "pT", "wsrc", "spar", "cst", "cstm", "i_cconv", "i_sgdn",
                          "i_kcT", "i_vc", "i_kc"])

    plan = []
    wstate = dict(next_load=0, next_use=0)

    def wload(i):
        l, g = plan[i]
        o, kc, ncol = offs[g]
        n = kc * ncol
        op_ = dma("sp", wring[i % NSLOT][:, 0:n], wbf[l, :, o:o + n])
        if (l, g) not in cast_done and not R.dry:
            cast_done.add((l, g))
            if l + 1 < DEPTH:
                dma("pool", wbf[l + 1, :, o:o + n], wsrc[l + 1, :, o:o + n], max_dma_last_dim=4096, extra=(op_,))

    def wget(l, g):
        i = wstate["next_use"]
        wstate["next_use"] += 1
        o, kc, ncol = offs[g]
        if R.dry:
            plan.append((l, g))
        else:
            assert plan[i] == (l, g), (plan[i], l, g)
            while wstate["next_load"] < min(len(plan), i + NSLOT):
                wload(wstate["next_load"])
                wstate["next_load"] += 1
        return wring[i % NSLOT][:, 0:kc * ncol].rearrange("p (k n) -> p k n", k=kc)

    tiles = []
    t0 = 0
    while t0 < SEQ:
        tiles.append(dict(kind="p", t0=t0, nt=TT, first=(t0 == 0), last=(t0 + TT >= SEQ), idx=len(tiles)))
        t0 += TT
    if cfg.sample:
        tiles.append(dict(kind="s", t0=SEQ, nt=128, first=True, last=True, idx=len(tiles)))

    def rmsnorm_fm(hs, xd, l, col, nt, sq2, r, pool="M"):
        ps = bank(pool)
        for kc in range(KC):
            s = sq2[kc % 2]
            act(s[:, 0:nt], hs[:, kc, 0:nt], AF.Square)
            mm(ps[:, 0:nt], ONESb, s[:, 0:nt], start=(kc == 0), stop=(kc == KC - 1))
        rsqrt_ln(r[:, 0:nt], ps[:, 0:nt], 1.0 / D_MODEL, EPS)
        for kc in range(KC):
            stt(xd[:, kc, 0:nt], hs[:, kc, 0:nt], sp_t[:, l, col + kc:col + kc + 1], r[:, 0:nt], ALU.mult, ALU.mult)

    def proj_chunk(w, c, nt, rhs_t, pool="M"):
        ps = bank(pool)
        nk = w.shape[1]
        for kc in range(nk):
            mm(ps[:, 0:nt], w[:, kc, c * 128:(c + 1) * 128], rhs_t[:, kc, 0:nt], start=(kc == 0), stop=(kc == nk - 1))
        return ps

    def gen_M(tile, l):
        kind, nt = tile["kind"], tile["nt"]
        nb = nt // 128
        samp = (kind == "s")
        ST = cfg.stages
        h = hbuf[tile["idx"] % 2]
        if not samp:
            if tile["first"]:
                memset("pool", S_all[:, l, :, :], 0.0)
                memset("pool", tails[:, l, :, :], 0.0)
            cp("act", Sb[:, :, :], S_all[:, l, :, :])
            if not tile["first"]:
                cp("pool", vext[:, 0, :], vprev[:, l, :])
                cp("pool", kdup[:, :, 0:128], kprev[:, l, :, :])
        else:
            dma("sp", ccin[:, :, :, :], i_cconv[l])
        R.cur_tag = "1norm"
        rmsnorm_fm(h, xn, l, SP_NM, nt, sqb[0:2], rstd[0])
        R.cur_tag = "2tm"
        w = wget(l, "tm")
        for b in range(nb):
            ps = bank()
            for kc in range(KC):
                mm(ps[:, 0:272], xn[:, kc, b * 128:(b + 1) * 128], w[:, kc, 0:272], start=(kc == 0), stop=(kc == KC - 1))
            cp("act", tmsm[:, b, :], ps[:, 0:16])
            cp("dve", vext[:, b + 1, :], ps[:, 16:272])
            if tile["last"] and b == nb - 1:
                cp("act", vfin[:, :], ps[:, 16:272])
                dma("sp", (o_vsn if samp else o_vp)[l], vfin[:, :])
        yield
        R.cur_tag = "3conv"
        L_ = 8 if samp else nt
        nseq = 16 if samp else 1
        for hg in range(2):
            for typ in ("q", "k", "v"):
                w = wget(l, "%s%d" % (typ, hg))
                for c in range(4):
                    ch = {"q": 0, "k": 8, "v": 16}[typ] + hg * 4 + c
                    ps = proj_chunk(w, c, nt, xn)
                    if "gdn" not in ST:
                        continue
                    rot[0] += 1
                    ri = rot[0] % ND_
                    cacc, yq = caccs[ri], yqs[ri]
                    rx = rawx[ri]
                    rxv = rx[:, 0:nseq * (L_ + 3)].rearrange("p (s t) -> p s t", s=nseq)
                    psv = ps[:, 0:nt].rearrange("p (s t) -> p s t", s=nseq)
                    cp("act", rxv[:, :, 3:3 + L_], psv)
                    if samp:
                        cp("pool", rxv[:, :, 0:3], ccin[:, ch, :, :])
                    else:
                        cp("pool", rxv[:, :, 0:3], tails[:, l, ch:ch + 1, :])
                    av = cacc[:, 0:nt].rearrange("p (s t) -> p s t", s=nseq)
                    cw = lambda j: sp_t[:, l, SP_CW + ch * 4 + j:SP_CW + ch * 4 + j + 1]
                    act(av, psv, AF.Copy, scale=cw(3))
                    for j in range(0, 3):
                        stt(av, rxv[:, :, j:j + L_], cw(j), av, ALU.mult, ALU.add)
                    if samp:
                        cp("pool", ccout[:, ch, :, :], rxv[:, :, L_:L_ + 3])
                    else:
                        cp("pool", tails[:, l, ch:ch + 1, :], rxv[:, :, L_:L_ + 3])
                    tn = ctan[ri]
                    act(tn[:, 0:nt], cacc[:, 0:nt], AF.Tanh, scale=0.5)
                    if typ == "v":
                        stt(arena[:, hg * 16 + 8 + c, 0:nt], tn[:, 0:nt], 1.0, cacc[:, 0:nt], ALU.add, ALU.mult)
                    else:
                        stt(yq[:, 0:nt], tn[:, 0:nt], 1.0, cacc[:, 0:nt], ALU.add, ALU.mult)
                        s_ = sqb[ri % 2]
                        if SQPOOL:
                            tt("pool", s_[:, 0:nt], yq[:, 0:nt], yq[:, 0:nt], ALU.mult)
                        else:
                            act(s_[:, 0:nt], yq[:, 0:nt], AF.Square)
                        pn = bank()
                        mm(pn[:, 0:nt], ONESb, s_[:, 0:nt])
                        r = rstd[ri % 2]
                        rsqrt_ln(r[:, 0:nt], pn[:, 0:nt], 1.0, 4.0 * EPS, mult=(128.0 ** -0.5 if typ == "q" else 1.0))
                        dst = arena[:, hg * 16 + (0 if typ == "q" else 4) + c, 0:nt]
                        tt("pool" if "a" in POOLMIX else "dve", dst, yq[:, 0:nt], r[:, 0:nt], ALU.mult)
                yield
            w = wget(l, "z%d" % hg)
            for c in range(4):
                ps = proj_chunk(w, c, nt, xn)
                if "gdn" not in ST:
                    continue
                tn = ctan[c % ND_]
                act(tn[:, 0:nt], ps[:, 0:nt], AF.Tanh, scale=0.5)
                stt(arena[:, hg * 16 + 12 + c, 0:nt], tn[:, 0:nt], 1.0, ps[:, 0:nt], ALU.add, ALU.mult)
            if "gdn" in ST:
                for b in range(nb):
                    R.cur_tag = "3gdn"
                    if hg == 0:
                        gdn_small(tile, l, b)
                    gdn_unit(tile, l, b, hg)
                R.cur_tag = "3conv"
            else:
                if hg == 0:
                    memset("pool", oaT[:, :, 0:nt], 0.0)
            yield
        if "gdn" in ST:
            if samp:
                dma("sp", o_convs[l], ccout[:, :, :, :])
            elif tile["last"]:
                dma("sp", o_convp[l], tails[:, l, :, :])
                dma("sp", o_gdnp[l], S_all[:, l, :, :])
        R.cur_tag = "4swa"
        qa = qaT
        for i in range(2):
            w = wget(l, "sq%d" % i)
            for c in range(4):
                ps = proj_chunk(w, c, nt, xn)
                if "swa" not in ST:
                    continue
                s_ = sqb[c % 2]
                act(s_[:, 0:nt], ps[:, 0:nt], AF.Square)
                pn = bank()
                mm(pn[:, 0:nt], ONES64b, s_[:, 0:nt])
                r = rstd[c % 2]
                rsqrt_ln(r[:, 0:nt], pn[:, 0:nt], 1.0 / 64.0, EPS)
                stt(qa[:, i * 4 + c, 0:nt], ps[:, 0:nt], spd[:, l, 1:2], r[:, 0:nt], ALU.mult, ALU.mult)
            yield
        w = wget(l, "skd")
        for c in range(4):
            ps = proj_chunk(w, c, nt, xn)
            if "swa" not in ST:
                continue
            s_ = sqb[c % 2]
            act(s_[:, 0:nt], ps[:, 0:nt], AF.Square)
            pn = bank()
            mm(pn[:, 0:nt], ONES64b, s_[:, 0:nt])
            r = rstd[c % 2]
            rsqrt_ln(r[:, 0:nt], pn[:, 0:nt], 1.0 / 64.0, EPS)
            stt(kdup[:, c, 128:128 + nt], ps[:, 0:nt], sp_t[:, l, SP_KN:SP_KN + 1], r[:, 0:nt], ALU.mult, ALU.mult)
            if tile["last"]:
                stt(kfin[:, c, :], ps[:, nt - 128:nt], sp_t[:, l, SP_KN:SP_KN + 1], r[:, nt - 128:nt], ALU.mult, ALU.mult)
        if "swa" in ST:
            if tile["last"]:
                dma("sp", (o_ksn if samp else o_kp)[l], kfin[:, :, :])
            for qb in range(nb):
                for kvh in range(4):
                    swa_unit(tile, l, qb, kvh)
            if not samp and not tile["last"]:
                cp("pool", vprev[:, l, :], vext[:, nb, :])
                cp("pool", kprev[:, l, :, :], kdup[:, :, nb * 128:(nb + 1) * 128])
            if samp:
                dma("act", o_kcopy[l], i_kc[l, :, 8:128, :])
                dma("act", o_vcopy[l], i_vc[l, :, 8:128, :])
        else:
            memset("pool", obT[:, :, 0:nt], 0.0)
        yield
        R.cur_tag = "5mix"
        for i in range(2):
            wa = wget(l, "ga%d" % i)
            for c in range(4):
                ps = proj_chunk(wa, c, nt, xn)
                yq = yqs[c % ND_]
                act(yq[:, 0:nt], ps[:, 0:nt], AF.Tanh, scale=0.5)
                stt(m1s[c][:, 0:nt], yq[:, 0:nt], 1.0, oaT[:, i * 4 + c, 0:nt], ALU.add, ALU.mult)
            yield
            wb_ = wget(l, "gb%d" % i)
            for c in range(4):
                ps = proj_chunk(wb_, c, nt, xn)
                yq = yqs[c % ND_]
                act(yq[:, 0:nt], ps[:, 0:nt], AF.Tanh, scale=0.5)
                m2 = m2s[c % 3]
                stt(m2[:, 0:nt], yq[:, 0:nt], 1.0, obT[:, i * 4 + c, 0:nt], ALU.add, ALU.mult)
                tt("pool" if "b" in POOLMIX else "dve", oaT[:, i * 4 + c, 0:nt], m1s[c][:, 0:nt], m2[:, 0:nt], ALU.add)
            yield
        R.cur_tag = "6wo"
        for i in range(2):
            w = wget(l, "wo%d" % i)
            for c in range(4):
                ps = proj_chunk(w, c, nt, oaT)
                stt(h[:, i * 4 + c, 0:nt], ps[:, 0:nt], 0.5, h[:, i * 4 + c, 0:nt], ALU.mult, ALU.add)
            yield

    def gen_F(tile, l):
        nt = tile["nt"]
        ST = cfg.stages
        h = hbuf[tile["idx"] % 2]
        R.cur_tag = "7ffn"
        if "ffn" in ST:
            rmsnorm_fm(h, xnF, l, SP_NF, nt, sqbF, rstdF, pool="F")
        hid = arenaF
        for hh in range(2):
            for gi in range(4):
                w = wget(l, "up%d_%d" % (hh, gi))
                if "ffn" in ST:
                    for c in range(4):
                        ps = proj_chunk(w, c, nt, xnF, pool="F")
                        act(yqF[:, 0:nt], ps[:, 0:nt], AF.Relu)
                        tt("pool", hid[:, gi * 4 + c, 0:nt], yqF[:, 0:nt], yqF[:, 0:nt], ALU.mult)
                yield
            for cb in range(4):
                w = wget(l, "dn%d_%d" % (hh, cb))
                if "ffn" in ST:
                    for c in range(2):
                        ps = proj_chunk(w, c, nt, hid, pool="F")
                        oc = cb * 2 + c
                        tt("dve", h[:, oc, 0:nt], ps[:, 0:nt], h[:, oc, 0:nt], ALU.add)
                yield
        R.cur_tag = "8ple"
        if "ple" in ST:
            rmsnorm_fm(h, xnF, l, SP_NP, nt, sqbF, rstdF, pool="F")
            dma("sp", pTf[:, :, 0:nt], pT[l, :, :, tile["t0"]:tile["t0"] + nt])
            cp("pool", pTb[:, :, 0:nt], pTf[:, :, 0:nt])
        for i in range(2):
            w = wget(l, "pg%d" % i)
            if "ple" in ST:
                for c in range(4):
                    ps = proj_chunk(w, c, nt, xnF, pool="F")
                    act(gF[:, i * 4 + c, 0:nt], ps[:, 0:nt], AF.Tanh, scale=0.5)
            yield
        wp = wget(l, "pp")
        if "ple" in ST:
            for oc in range(8):
                ps2 = proj_chunk(wp, oc, nt, pTb, pool="F")
                stt(yqF[:, 0:nt], gF[:, oc, 0:nt], 1.0, ps2[:, 0:nt], ALU.add, ALU.mult)
                stt(h[:, oc, 0:nt], yqF[:, 0:nt], 0.5, h[:, oc, 0:nt], ALU.mult, ALU.add)
        if l == DEPTH - 1:
            dma("sp", yT[:, :, tile["t0"]:tile["t0"] + nt], h[:, :, 0:nt])
        yield

    def gdn_small(tile, l, b):
        samp = tile["kind"] == "s"
        G = lambda a: gsm[:, b, a:a + 8]
        bl = tmsm[:, b, 0:8]
        al = tmsm[:, b, 8:16]
        act(G(80), bl, AF.Tanh, scale=0.5)
        ts("dve", G(0), G(80), 0.5, ALU.mult, 0.5, ALU.add)
        tt("dve", G(80), al, sp_t[:, l, SP_DT:SP_DT + 8], ALU.add)
        act(G(80), G(80), AF.Exp)
        act(G(80), G(80), AF.Ln, bias=1.0)
        tt("dve", G(8), G(80), spd[:, l, 10:18], ALU.mult)
        ps = bank("G")
        mm(ps[:, 0:8], C("U_S") if samp else C("U"), G(8))
        mm(ps[:, 8:16], C("ONESSEQ_S") if samp else C("ONES"), G(8))
        cp("dve", gsm[:, b, 16:32], ps[:, 0:16])
        act(G(32), G(16), AF.Exp)
        tt("dve", G(40), G(0), G(32), ALU.mult)
        tt("dve", G(80), G(24), G(16), ALU.subtract)
        act(G(48), G(80), AF.Exp)
        act(G(56), G(24), AF.Exp)
        ts("dve", G(64), G(0), -1.0, ALU.mult)
        ts("dve", G(72), G(0), 0.5, ALU.mult)
        if samp:
            tt("dve", gsq[:, :, :], G(8).unsqueeze(1).broadcast_to([128, 16, 8]),
               C("SM", 16).unsqueeze(2).broadcast_to([128, 16, 8]), ALU.mult)
            ps2 = bank("G")
            mm(ps2[:, 0:128], C("ONES"), gsq[:, :, :].rearrange("p s h -> p (s h)"))
            act(egls[:, :, :].rearrange("p s h -> p (s h)"), ps2[:, 0:128], AF.Exp)

    def bc4(ap):
        return ap.unsqueeze(2).broadcast_to([128, 4, 128])

    def hb(ap):
        return ap.unsqueeze(1).broadcast_to([128, 4, 128])

    def f4(ap):
        return ap.rearrange("p h n -> p (h n)")

    def v4(ap):
        return ap.rearrange("p (h n) -> p h n", h=4)

    def gdn_unit(tile, l, b, hg):
        samp = tile["kind"] == "s"
        H0 = hg * 4
        G = lambda a: gsm[:, b, a + H0:a + H0 + 4]
        blk = slice(b * 128, (b + 1) * 128)
        ab = hg * 16
        qT = lambda hh: arena[:, ab + 0 + hh, blk]
        kT = lambda hh: arena[:, ab + 4 + hh, blk]
        vT = lambda hh: arena[:, ab + 8 + hh, blk]
        Um = C("U_S") if samp else C("U")
        MSm = CB("MS_S") if samp else CB("MS")
        MIm = CB("MI_S") if samp else CB("MI")
        tt("dve", GL[:, :, :], bc4(G(8)), hb(CB("L")), ALU.mult)
        pd = bank("G")
        mm(pd[:, :], Um, f4(GL[:, :, :]))
        act(f4(Eraw[:, :, :]), pd[:, :], AF.Exp)
        tt("pool", Ems[:, :, :], Eraw[:, :, :], hb(MSm), ALU.mult)
        tt("pool", Emi[:, :, :], Eraw[:, :, :], hb(MIm), ALU.mult)
        pkk = bank("G")
        pqk = bank("G")
        for hh in range(4):
            mm(pkk[:, hh * 128:(hh + 1) * 128], kT(hh), kT(hh))
        for hh in range(4):
            mm(pqk[:, hh * 128:(hh + 1) * 128], qT(hh), kT(hh))
        for hh in range(4):
            stt(Nb[:, hh, :], pkk[:, hh * 128:(hh + 1) * 128], gsm[:, b, 64 + H0 + hh:64 + H0 + hh + 1], Ems[:, hh, :],
                ALU.mult, ALU.mult)
        tt("dve", f4(qkb[:, :, :]), pqk[:, :], f4(Emi[:, :, :]), ALU.mult)
        pt1 = bank("G")
        pt1b = pt1[:, :].bitcast(BF16)
        for hh in range(4):
            tr(pt1b[:, hh * 128:(hh + 1) * 128], Nb[:, hh, :], IDb)
        NTb = PTm[0]
        cp("act", f4(NTb[:, :, :]), pt1b[:, 0:512])
        pt2 = bank("G")
        pt2b = pt2[:, :].bitcast(BF16)
        for hh in range(4):
            tr(pt2b[:, hh * 128:(hh + 1) * 128], qkb[:, hh, :], IDb)
        cp("act", f4(qkT[:, :, :]), pt2b[:, 0:512])

        def mm4(lhs, rhs, add=None):
            p_ = bank("G")
            for hh in range(4):
                mm(p_[:, hh * 128:(hh + 1) * 128], lhs[:, hh, :], rhs[:, hh, :], start=True,
                   stop=(add is None or not PEADD))
                if add is not None and PEADD:
                    a_ = add if add is IDb else add[:, hh, :]
                    mm(p_[:, hh * 128:(hh + 1) * 128], IDb, a_, start=False, stop=True)
            return p_

        def evac_add(dst, p_, add, eng):
            if PEADD:
                cp(eng, f4(dst[:, :, :]), p_[:, :])
            elif add is IDb:
                tt("dve", dst[:, :, :], v4(p_[:, :]), hb(IDb), ALU.add)
            else:
                tt("dve", f4(dst[:, :, :]), p_[:, :], f4(add[:, :, :]), ALU.add)
        Nd, NdT = Pm[0], Pm[1]
        tt("dve", Nd[:, :, :], Nb[:, :, :], hb(CB("MD4")), ALU.mult)
        tt("dve", NdT[:, :, :], NTb[:, :, :], hb(CB("MD4")), ALU.mult)
        p1 = mm4(NdT, Nd, add=IDb)
        p2 = mm4(Nd, NdT, add=IDb)
        Q_, QT_, R_, RT_ = X1b, X2b, No_b, NoT_b
        evac_add(Q_, p1, IDb, "act")
        evac_add(QT_, p2, IDb, "act")
        tt("pool", R_[:, :, :], Nd[:, :, :], hb(CB("IDENT")), ALU.add)
        tt("pool", RT_[:, :, :], NdT[:, :, :], hb(CB("IDENT")), ALU.add)
        p3 = mm4(QT_, R_)
        p4 = mm4(R_, QT_)
        cp("act", f4(Tdn[:, :, :]), p3[:, :])
        cp("act", f4(TTb[:, :, :]), p4[:, :])
        levels = [4] if samp else [4, 8, 16, 32, 64]
        for li, m_ in enumerate(levels):
            last = (li == len(levels) - 1)
            tt("dve", No_b[:, :, :], Nb[:, :, :], hb(CB("ML%d" % m_)), ALU.mult)
            if not last:
                tt("pool" if "c" in POOLMIX else "dve", NoT_b[:, :, :], NTb[:, :, :], hb(CB("MLT%d" % m_)), ALU.mult)
                px1 = mm4(NoT_b, Tdn)
                cp("act", f4(X1b[:, :, :]), px1[:, :])
            px2 = mm4(No_b, TTb)
            cp("act", f4(X2b[:, :, :]), px2[:, :])
            if not last:
                py1 = mm4(TTb, X1b, add=Tdn)
            py2 = mm4(Tdn, X2b, add=TTb)
            if not last:
                evac_add(Tdn, py1, Tdn, "act")
            evac_add(TTb, py2, TTb, "dve")
        pk = bank("G")
        pkb = pk[:, :].bitcast(BF16)
        for hh in range(4):
            tr(pkb[:, hh * 128:(hh + 1) * 128], kT(hh), IDb)
        tt("dve", kbg[:, :, :], v4(pkb[:, 0:512]), bc4(G(40)), ALU.mult)
        tt("dve", kdec[:, :, :], v4(pkb[:, 0:512]), bc4(G(48)), ALU.mult)
        pv = bank("G")
        pvb = pv[:, :].bitcast(BF16)
        for hh in range(4):
            tr(pvb[:, hh * 128:(hh + 1) * 128], vT(hh), IDb)
        tt("dve", vbt[:, :, :], v4(pvb[:, 0:512]), bc4(G(72)), ALU.mult)
        pw = bank("G")
        for hh in range(4):
            mm(pw[:, hh * 128:(hh + 1) * 128], kbg[:, hh, :], TTb[:, hh, :])
        cp("act", f4(wTb[:, :, :]), pw[:, :])
        if not samp:
            pu = bank("G")
            for hh in range(4):
                mm(pu[:, hh * 128:(hh + 1) * 128], TTb[:, hh, :], vbt[:, hh, :])
            cp("act", f4(u_t[:, :, :]), pu[:, :])
            pws = bank("G")
            for hh in range(4):
                mm(pws[:, hh * 128:(hh + 1) * 128], wTb[:, hh, :], Sb[:, H0 + hh, :])
            tt("dve", f4(vnew[:, :, :]), f4(u_t[:, :, :]), pws[:, :], ALU.subtract)
            pqs = bank("G")
            for hh in range(4):
                mm(pqs[:, hh * 128:(hh + 1) * 128], qT(hh), Sb[:, H0 + hh, :])
            pin = bank("G")
            for hh in range(4):
                mm(pin[:, hh * 128:(hh + 1) * 128], qkT[:, hh, :], vnew[:, hh, :])
            tt("dve", tq[:, :, :], v4(pqs[:, :]), bc4(G(32)), ALU.mult)
            tt("dve", f4(o_t[:, :, :]), pin[:, :], f4(tq[:, :, :]), ALU.add)
            psu = bank("G")
            for hh in range(4):
                mm(psu[:, hh * 128:(hh + 1) * 128], kdec[:, hh, :], vnew[:, hh, :])
            Sv = S_all[:, l, H0:H0 + 4, :]
            tt("pool", Sv, Sv, bc4(G(56)), ALU.mult)
            tt("dve", Sv, v4(psu[:, :]), Sv, ALU.add)
            cp("act", Sb[:, H0:H0 + 4, :], Sv)
        else:
            pu = bank("G")
            for hh in range(4):
                mm(pu[:, hh * 128:(hh + 1) * 128], vbt[:, hh, :], TTb[:, hh, :])
            cp("act", f4(uT[:, :, :]), pu[:, :])
            pws = bank("G")
            pqs = bank("G")
            for sg in range(8):
                Sf_, Sb_ = S_all[:, sg % 2, :, :], Ssb[sg % 2]
                for si_ in range(2):
                    dma("sp", Sf_[:, si_ * 4:si_ * 4 + 4, :],
                        i_sgdn[l, sg * 2 + si_, H0:H0 + 4, :, :].rearrange("h d v -> d h v"))
                cp("act" if "s" in POOLMIX else "pool", Sb_[:, :, :], Sf_[:, :, :])
                for si in range(2):
                    s = sg * 2 + si
                    cols = slice(s * 8, s * 8 + 8)
                    for hh in range(4):
                        mm(pws[:, hh * 128 + s * 8:hh * 128 + s * 8 + 8], Sb_[:, si * 4 + hh, :], wTb[:, hh, cols],
                           start=True, stop=True, skip=True)
                        mm(pqs[:, hh * 128 + s * 8:hh * 128 + s * 8 + 8], Sb_[:, si * 4 + hh, :],
                           arena[:, ab + hh, b * 128 + s * 8:b * 128 + s * 8 + 8], start=True, stop=True, skip=True)
            tt("dve", f4(vnT[:, :, :]), f4(uT[:, :, :]), pws[:, :], ALU.subtract)
            cp("act", f4(qsT[:, :, :]), pqs[:, :])
            pt3 = bank("G")
            pt3b = pt3[:, :].bitcast(BF16)
            for hh in range(4):
                tr(pt3b[:, hh * 128:(hh + 1) * 128], vnT[:, hh, :], IDb)
            cp("act", f4(vnew[:, :, :]), pt3b[:, 0:512])
            pt4 = bank("G")
            pt4b = pt4[:, :].bitcast(BF16)
            for hh in range(4):
                tr(pt4b[:, hh * 128:(hh + 1) * 128], qsT[:, hh, :], IDb)
            tt("dve", tq[:, :, :], v4(pt4b[:, 0:512]), bc4(G(32)), ALU.mult)
            pin = bank("G")
            for hh in range(4):
                mm(pin[:, hh * 128:(hh + 1) * 128], qkT[:, hh, :], vnew[:, hh, :])
            tt("dve", f4(o_t[:, :, :]), pin[:, :], f4(tq[:, :, :]), ALU.add)
            for sg in range(8):
                Sf_ = S_all[:, sg % 2, :, :]
                for si_ in range(2):
                    dma("sp", Sf_[:, si_ * 4:si_ * 4 + 4, :],
                        i_sgdn[l, sg * 2 + si_, H0:H0 + 4, :, :].rearrange("h d v -> d h v"))
                for si in range(2):
                    s = sg * 2 + si
                    km = kdm[s % 2]
                    ts("dve", km[:, :, :], kdec[:, :, :], C("SM", 16)[:, s:s + 1], ALU.mult)
                    psu = bank("G")
                    for hh in range(4):
                        mm(psu[:, hh * 128:(hh + 1) * 128], km[:, hh, :], vnew[:, hh, :])
                    sn = S_all[:, 2, (s % 2) * 4:(s % 2) * 4 + 4, :]
                    tt("dve" if "t" in POOLMIX else "pool", sn[:, :, :], Sf_[:, si * 4:si * 4 + 4, :], bc4(egls[:, s, H0:H0 + 4]), ALU.mult)
                    tt("dve", sn[:, :, :], v4(psu[:, :]), sn[:, :, :], ALU.add)
                    dst = o_gdns[l, s, H0:H0 + 4, :, :].rearrange("h d v -> d h v")
                    dma("sp", dst, sn[:, :, :])
        tt("pool", osq[:, :, :], o_t[:, :, :], o_t[:, :, :], ALU.mult)
        red(osm[:, 0:4], osq[:, :, :], ALU.add)
        rsqrt_ln(osm[:, 4:8], osm[:, 0:4], 1.0 / 128.0, EPS)
        tt("dve", on_b[:, :, :], o_t[:, :, :], bc4(osm[:, 4:8]), ALU.mult)
        po = bank("G")
        pob = po[:, :].bitcast(BF16)
        for hh in range(4):
            tr(pob[:, hh * 128:(hh + 1) * 128], on_b[:, hh, :], IDb)
        stt(oaT[:, H0:H0 + 4, blk], v4(pob[:, 0:512]), spd[:, l, 0:1], arena[:, ab + 12:ab + 16, blk], ALU.mult, ALU.mult)

    def swa_unit(tile, l, qb, kvh):
        samp = tile["kind"] == "s"
        qa = qaT
        qs = slice(qb * 128, (qb + 1) * 128)
        heads = [4 * kvh + 0, 4 * kvh + 2, 4 * kvh + 1, 4 * kvh + 3]
        kbs = []
        if not samp and not (tile["first"] and qb == 0):
            kbs.append((qb, C("DO")))
        kbs.append((qb + 1, C("DN_S") if samp else C("DD")))
        pts = []
        for i, (kb, Dm) in enumerate(kbs):
            pse, pso = bank(), bank()
            ks = slice(kb * 128, (kb + 1) * 128)
            mm(pse[:, 0:256], kdup[0:64, kvh, ks], qa[0:64, 2 * kvh:2 * kvh + 2, qs])
            mm(pso[:, 0:256], kdup[64:128, kvh, ks], qa[64:128, 2 * kvh:2 * kvh + 2, qs])
            s_ = sc[i]
            for j in range(4):
                src = (pse if j < 2 else pso)[:, (j % 2) * 128:(j % 2 + 1) * 128]
                stt(s_[:, j, :], Dm, -SLOPES[heads[j]], src, ALU.mult, ALU.add)
            act(PTa[i][:, :, :], s_[:, :, :], AF.Exp)
            pts.append((PTa[i], vext[:, kb, kvh * 64:(kvh + 1) * 64]))
        if samp:
            dma("pool", kcb[:, :, :], i_kcT[l, kvh])
            dma("pool", vcb[:, :, :], i_vc[l, :, :, kvh * 64:(kvh + 1) * 64].rearrange("s k d -> k s d"))
            pse, pso = bank(), bank()
            psve = pse[:, 0:256].rearrange("p (s j t) -> p s j t", s=16, j=2)
            psvo = pso[:, 0:256].rearrange("p (s j t) -> p s j t", s=16, j=2)
            for s in range(16):
                cols = slice(s * 8, s * 8 + 8)
                mm(psve[:, s, :, :], kcb[0:64, s, :], qa[0:64, 2 * kvh:2 * kvh + 2, cols])
                mm(psvo[:, s, :, :], kcb[64:128, s, :], qa[64:128, 2 * kvh:2 * kvh + 2, cols])
            for j in range(4):
                src = (psve if j < 2 else psvo)[:, :, j % 2, :]
                stt(scs[:, :, j, :], C("DC", 8).unsqueeze(1).broadcast_to([128, 16, 8]), -SLOPES[heads[j]],
                    src, ALU.mult, ALU.add)
            act(PTc[:, :, :, 0:8], scs[:, :, :, :], AF.Exp)
        nd = bank()
        ndv = nd[:, :].rearrange("p (a c q) -> p a c q", a=2, c=2)
        for a in range(2):
            for par in range(2):
                prt = slice(par * 64, par * 64 + 64)
                for i, (P_, v_) in enumerate(pts):
                    lhs = v_ if a == 0 else ONESb[:, 0:64]
                    mm(ndv[prt, a, :, :], lhs, P_[:, 2 * par:2 * par + 2, :], start=(i == 0),
                       stop=(i == len(pts) - 1 and not samp), skip=samp)
                if samp:
                    for s in range(16):
                        lhs = vcb[:, s, :] if a == 0 else ONESb[:, 0:64]
                        for c in range(2):
                            mm(ndv[prt, a, c, s * 8:s * 8 + 8], lhs, PTc[:, s, 2 * par + c, 0:8], start=False,
                               stop=(s == 15 and c == 1), skip=True)
        for c in range(2):
            ts("dve", rden[:, c, :], ndv[:, 1, c, :], spd[:, l, 2 + 2 * kvh + c:3 + 2 * kvh + c], ALU.add)
        recip(rden[:, :, :], rden[:, :, :])
        tt("dve", obT[:, 2 * kvh:2 * kvh + 2, qs], ndv[:, 0, :, :], rden[:, :, :], ALU.mult)

    def interleave(g1, g2):
        a_, b_ = g1, g2
        while a_ is not None or b_ is not None:
            if a_ is not None:
                try:
                    next(a_)
                except StopIteration:
                    a_ = None
            if b_ is not None:
                try:
                    next(b_)
                except StopIteration:
                    b_ = None

    def drive():
        units = []
        i = 0
        while i < len(tiles):
            if i + 1 < len(tiles) and tiles[i]["kind"] == "p" and tiles[i + 1]["kind"] == "p" and not cfg.nopair:
                for l in range(DEPTH):
                    units.append((tiles[i], l))
                    units.append((tiles[i + 1], l))
                i += 2
            else:
                for l in range(DEPTH):
                    units.append((tiles[i], l))
                i += 1
        prevF, prev_tile = None, None
        for (tile, l) in units:
            if prevF is not None and prev_tile is tile:
                interleave(prevF, None)
                prevF = None
            if l == 0:
                dma("sp", hbuf[tile["idx"] % 2][:, :, 0:tile["nt"]], xT[:, :, tile["t0"]:tile["t0"] + tile["nt"]])
            interleave(gen_M(tile, l), prevF)
            prevF, prev_tile = gen_F(tile, l), tile
        interleave(prevF, None)

    n_setup = len(R.ops)
    R.dry = True
    drive()
    R.dry = False
    assert len(R.ops) == n_setup
    bank_i[0] = 0
    bank_f[0] = 0
    bank_g[0] = 0
    wstate["next_load"] = 0
    wstate["next_use"] = 0
    drive()
    assert wstate["next_use"] == len(plan)
    R.emit(st)
    return nc, st, R, dbg_out


def pack_weights(inp, depth):
    offs, TOT = group_offsets()
    out = np.zeros((depth, 128, TOT), np.float32)
    for l in range(depth):
        for (name, src, rows, cols) in weight_groups():
            W = inp[src][l]
            if isinstance(rows, tuple):
                W = W[rows[0]:rows[0] + rows[1]]
            M = W[:, cols]
            kc = M.shape[0] // 128
            o, _, ncol = offs[name]
            out[l, :, o:o + kc * ncol] = M.reshape(kc, 128, ncol).transpose(1, 0, 2).reshape(128, kc * ncol)
    return out


def pack_small(inp, depth):
    sp = np.zeros((128, depth, NSP), np.float32)
    for l in range(depth):
        sp[:, l, SP_NM:SP_NM + 8] = inp["norm_mix"][l].reshape(8, 128).T
        sp[:, l, SP_NF:SP_NF + 8] = inp["norm_ffn"][l].reshape(8, 128).T
        sp[:, l, SP_NP:SP_NP + 8] = inp["norm_ple"][l].reshape(8, 128).T
        cw = inp["conv_w"][l]
        sp[:, l, SP_CW:SP_CW + 96] = cw.reshape(4, 24, 128).transpose(2, 1, 0).reshape(128, 96)
        sp[:, l, SP_GN] = inp["gdn_norm"][l]
        sp[:, l, SP_QN] = np.tile(inp["q_norm"][l], 2)
        sp[:, l, SP_KN] = np.tile(inp["k_norm"][l], 2)
        sk = inp["attn_sinks"][l]
        for c in range(8):
            sp[0:64, l, SP_SK + c] = sk[2 * c]
            sp[64:128, l, SP_SK + c] = sk[2 * c + 1]
        sp[:, l, SP_AL:SP_AL + 8] = inp["a_log"][l][None, :]
        sp[:, l, SP_DT:SP_DT + 8] = inp["dt_bias"][l][None, :]
    return sp


def fm(x):
    T, Dm = x.shape
    return np.ascontiguousarray(x.reshape(T, Dm // 128, 128).transpose(2, 1, 0))


def unfm(y):
    p, k, T = y.shape
    return np.ascontiguousarray(y.transpose(2, 1, 0).reshape(T, k * 128))


def make_in_maps(inp, cfg, ncores):
    depth = cfg.depth
    wsrc = pack_weights(inp, depth)
    spar = pack_small(inp, depth)
    cst = make_consts()
    maps = []
    for c in range(ncores):
        xs = [inp["x_prompt"][c, :cfg.seq]]
        ps = [inp["p_prompt"][:depth, c, :cfg.seq]]
        if cfg.sample:
            xs.append(inp["x_sample"][16 * c:16 * c + 16].reshape(128, D_MODEL))
            ps.append(inp["p_sample"][:depth, 16 * c:16 * c + 16].reshape(depth, 128, 256))
        x = np.concatenate(xs, 0)
        p = np.concatenate(ps, 1)
        m = {"xT": fm(x), "pT": np.stack([fm(p[l]) for l in range(depth)]), "wsrc": wsrc, "spar": spar,
             "cst": cst[0], "cstm": cst[1]}
        if cfg.sample:
            sl = slice(16 * c, 16 * c + 16)
            cc = inp["cache_conv"][:depth, sl]
            m["i_cconv"] = np.ascontiguousarray(cc.reshape(depth, 16, 3, 24, 128).transpose(0, 4, 3, 1, 2))
            m["i_sgdn"] = np.ascontiguousarray(inp["state_gdn"][:depth, sl])
            kc = inp["cache_swa_k"][:depth, sl]
            kt = kc.transpose(0, 3, 4, 1, 2)
            m["i_kcT"] = np.ascontiguousarray(np.concatenate([kt, kt], axis=2))
            m["i_vc"] = np.ascontiguousarray(inp["cache_swa_v"][:depth, sl].reshape(depth, 16, 128, 256))
            m["i_kc"] = np.ascontiguousarray(kc.reshape(depth, 16, 128, 256))
        maps.append(m)
    return maps


def assemble(results, cfg, ncores):
    depth, seq = cfg.depth, cfg.seq
    yp = np.zeros((ncores, seq, D_MODEL), np.float32)
    convp = np.zeros((depth, ncores, 3, 3072), np.float32)
    gdnp = np.zeros((depth, ncores, 8, 128, 128), np.float32)
    kp = np.zeros((depth, ncores, 128, 4, 64), np.float32)
    vp = np.zeros((depth, ncores, 128, 4, 64), np.float32)
    if cfg.sample:
        ys = np.zeros((ncores * 16, 8, D_MODEL), np.float32)
        convs = np.zeros((depth, ncores * 16, 3, 3072), np.float32)
        gdns = np.zeros((depth, ncores * 16, 8, 128, 128), np.float32)
        ks = np.zeros((depth, ncores * 16, 128, 4, 64), np.float32)
        vs = np.zeros((depth, ncores * 16, 128, 4, 64), np.float32)
    for c in range(ncores):
        r = results[c]
        y = unfm(r["yT"])
        yp[c] = y[:seq]
        convp[:, c] = r["o_convp"].transpose(0, 3, 2, 1).reshape(depth, 3, 3072)
        gdnp[:, c] = r["o_gdnp"].transpose(0, 2, 1, 3)
        kp[:, c] = r["o_kp"][:, 0:64].transpose(0, 3, 2, 1)
        vp[:, c] = r["o_vp"].reshape(depth, 128, 4, 64)
        if cfg.sample:
            sl = slice(16 * c, 16 * c + 16)
            ys[sl] = y[seq:].reshape(16, 8, D_MODEL)
            convs[:, sl] = r["o_convs"].transpose(0, 3, 4, 2, 1).reshape(depth, 16, 3, 3072)
            gdns[:, sl] = r["o_gdns"]
            ks[:, sl, 0:120] = r["o_kcopy"].reshape(depth, 16, 120, 4, 64)
            vs[:, sl, 0:120] = r["o_vcopy"].reshape(depth, 16, 120, 4, 64)
            kn = r["o_ksn"][:, 0:64].transpose(0, 3, 2, 1)
            ks[:, sl, 120:128] = kn.reshape(depth, 16, 8, 4, 64)
            vs[:, sl, 120:128] = r["o_vsn"].reshape(depth, 16, 8, 4, 64)
    if cfg.sample:
        return (yp, ys, convp, gdnp, kp, vp, convs, gdns, ks, vs)
    return (yp, convp, gdnp, kp, vp)


_CACHE = {}


def kernel(**inputs):
    inp = {k: np.asarray(v) for k, v in inputs.items()}
    cfg = Cfg()
    if "prog" not in _CACHE:
        _CACHE["prog"] = build(cfg)
    nc, st, R, _ = _CACHE["prog"]
    maps = make_in_maps(inp, cfg, 8)
    res = run_bass_kernel_spmd(nc, maps, core_ids=list(range(8)))
    return assemble(res.results, cfg, 8)
```

```python
import contextlib
import numpy as np
import concourse.bass as bass
import concourse.mybir as mybir
from concourse.bass_utils import run_bass_kernel_spmd

F32 = mybir.dt.float32
BF16 = mybir.dt.bfloat16
AF = mybir.ActivationFunctionType
ALU = mybir.AluOpType
AX = mybir.AxisListType

D_MODEL = 1024
KC = 8
EPS = 1e-6
BIG = 1.0e6
COMPUTE = ("pe", "act", "dve", "pool")
SLOPES = [float(2.0 ** (-8.0 * (h + 1) / 16.0)) for h in range(16)]


def _ap_box(ap):
    t = ap.tensor
    name = t.name
    dims = [(s, n) for (s, n) in ap.ap]
    off = ap.offset
    tn = type(t).__name__
    if "PSum" in tn:
        return (name, 0, 128, 0, 1 << 30, True)
    if "DRam" in tn:
        nz = [(abs(s), n) for (s, n) in dims if n > 1 and s != 0]
        if nz:
            smax, nmax = max(nz)
            rest = [(s, n) for (s, n) in dims if not (abs(s) == smax and n == nmax)]
            ext = sum(abs(s) * (n - 1) for (s, n) in rest if n > 1) + 1
            f0 = off % smax
            if len(rest) == len(dims) - 1 and f0 + ext <= smax and all(s >= 0 for s, n in dims):
                p_lo = off // smax
                return (name, p_lo, p_lo + nmax, f0, f0 + ext, False)
        lo = hi = off
        for s, n in dims:
            if n > 1:
                if s >= 0:
                    hi += s * (n - 1)
                else:
                    lo += s * (n - 1)
        return (name, 0, 1 << 30, lo, hi + 1, False)
    pst, pn = dims[0]
    if pst == 0:
        p_lo, f0 = 0, off
    else:
        p_lo = off // pst
        f0 = off - p_lo * pst
    lo = hi = f0
    for s, n in dims[1:]:
        if n > 1:
            if s >= 0:
                hi += s * (n - 1)
            else:
                lo += s * (n - 1)
    return (name, p_lo, p_lo + pn, lo, hi + 1, False)


class Op:
    __slots__ = ("eng", "fn", "reads", "writes", "deps", "signal", "count", "is_dma", "slot", "dval", "idx",
                 "dur", "succ", "fin", "st", "why", "tag", "aset", "lat")

    def __init__(self, eng, fn, reads, writes, is_dma, dur):
        self.eng = eng
        self.fn = fn
        self.reads = reads
        self.writes = writes
        self.deps = ()
        self.signal = False
        self.count = 0
        self.is_dma = is_dma
        self.slot = None
        self.dval = 0
        self.idx = -1
        self.dur = dur
        self.succ = []
        self.fin = 0.0
        self.st = 0.0
        self.why = None
        self.tag = ""
        self.aset = None


def _ov(a, b):
    return a[1] < b[2] and b[1] < a[2] and a[3] < b[4] and b[3] < a[4]


def _cov(b, e):
    return b[1] <= e[1] and e[2] <= b[2] and b[3] <= e[3] and e[4] <= b[4]


class Rec:
    SEM_LAT = 250.0

    def __init__(self, nc, slots):
        self.nc = nc
        self.ops = []
        self.wr = {}
        self.rd = {}
        self.slots = slots
        self.const_names = set()
        self.cur_tag = ""
        self.dry = False
        self.dma_hist = {}
        self.makespan = 0.0
        import os
        self.maxops = int(os.environ["K_MAXOPS"]) if "K_MAXOPS" in os.environ else None
        self.sched = os.environ.get("K_NOSCHED") is None

    def add(self, eng, fn, reads, writes, is_dma=False, dur=100.0, extra=()):
        if self.dry or (self.maxops is not None and len(self.ops) >= self.maxops):
            return None
        op = Op(eng, fn, [_ap_box(a) for a in reads], [_ap_box(a) for a in writes], is_dma, dur)
        op.idx = len(self.ops)
        op.tag = self.cur_tag
        self.ops.append(op)
        deps = set()
        for b in op.reads:
            for (ob, oop) in self.wr.get(b[0], ()):
                if _ov(ob, b):
                    deps.add(oop)
            if b[5]:
                for (ob, oop) in self.rd.get(b[0], ()):
                    if oop.eng != eng and _ov(ob, b):
                        deps.add(oop)
        for b in op.writes:
            for (ob, oop) in self.wr.get(b[0], ()):
                if _ov(ob, b):
                    deps.add(oop)
            for (ob, oop) in self.rd.get(b[0], ()):
                if _ov(ob, b):
                    deps.add(oop)
        for x in extra:
            if x is not None:
                deps.add(x)
        if is_dma:
            hist = self.dma_hist.setdefault(eng, [])
            ns = self.slots[eng]
            if len(hist) >= ns:
                deps.add(hist[-ns])
            hist.append(op)
        deps.discard(op)
        op.deps = tuple(deps)
        for d in deps:
            d.succ.append(op)
        for b in op.writes:
            for dct in (self.wr, self.rd):
                lst = dct.get(b[0])
                if lst:
                    lst[:] = [e for e in lst if not _cov(b, e[0])]
            self.wr.setdefault(b[0], []).append((b, op))
        for b in op.reads:
            if b[0] in self.const_names:
                continue
            self.rd.setdefault(b[0], []).append((b, op))
        return op

    def schedule(self):
        import heapq
        ops = self.ops
        indeg = [len(o.deps) for o in ops]
        heaps = {}
        free = {}
        L = self.SEM_LAT

        import os
        fbonus = float(os.environ.get("K_FBONUS", "0"))

        tru = {}

        def push(o, t):
            heaps.setdefault(o.eng, [])
            tru[o.idx] = t
            if fbonus and o.tag in ("7ffn", "8ple"):
                t = t - fbonus
            heapq.heappush(heaps[o.eng], (t, o.idx))

        for o in ops:
            if indeg[o.idx] == 0:
                push(o, 0.0)
        order = []
        n = len(ops)
        hp_t = {}
        last_on = {}
        cur_set = ["A"]
        while len(order) < n:
            best = None
            for e, hp in heaps.items():
                if not hp:
                    continue
                t, i = hp[0]
                stt_ = max(t, free.get(e, 0.0))
                if best is None or (stt_, i) < best[0]:
                    best = ((stt_, i), e)
            (stt_, i), e = best
            stt_ = max(tru[i], free.get(e, 0.0))
            if e == "act" and ops[i].aset is not None and ops[i].aset != cur_set[0] and len(heaps[e]) > 1:
                cands = heapq.nsmallest(6, heaps[e])
                pick = None
                for (t_, j_) in cands:
                    if ops[j_].aset in (None, cur_set[0]) and max(t_, free.get(e, 0.0)) <= stt_ + 1300.0:
                        pick = (t_, j_)
                        break
                if pick is not None:
                    heaps[e].remove(pick)
                    heapq.heapify(heaps[e])
                    i = pick[1]
                    stt_ = max(tru[i], free.get(e, 0.0))
                else:
                    heapq.heappop(heaps[e])
            else:
                heapq.heappop(heaps[e])
            o = ops[i]
            if e == "act" and o.aset is not None:
                if o.aset != cur_set[0]:
                    stt_ += 1300.0
                cur_set[0] = o.aset
            order.append(o)
            o.st = stt_
            if stt_ > hp_t.get(o.idx, 0.0) + 1e-9:
                o.why = last_on.get(e)
            last_on[e] = o
            if o.is_dma:
                free[e] = stt_ + 60.0
            else:
                free[e] = stt_ + o.dur
            o.fin = stt_ + o.dur
            for sc_ in o.succ:
                indeg[sc_.idx] -= 1
                if indeg[sc_.idx] == 0:
                    rt = 0.0
                    for d in sc_.deps:
                        lat = 0.0 if (d.eng == "pe" and sc_.eng == "pe" and not d.is_dma and not sc_.is_dma) else L
                        if d.fin + lat > rt:
                            rt = d.fin + lat
                            sc_.why = d
                    hp_t[sc_.idx] = rt
                    push(sc_, rt)
        self.makespan = max(o.fin for o in ops)
        return order

    def schedule_hlfet(self):
        ops = self.ops
        L = self.SEM_LAT
        n = len(ops)
        bl = [0.0] * n
        for o in reversed(ops):
            m = 0.0
            for sc_ in o.succ:
                lat = 0.0 if (o.eng == "pe" and sc_.eng == "pe" and not o.is_dma and not sc_.is_dma) else L
                v = lat + bl[sc_.idx]
                if v > m:
                    m = v
            bl[o.idx] = o.dur + m
        indeg = [len(o.deps) for o in ops]
        ready = {}
        rt = {}
        free = {}
        for o in ops:
            if indeg[o.idx] == 0:
                ready.setdefault(o.eng, []).append(o.idx)
                rt[o.idx] = 0.0
        order = []
        cur_set = "A"
        while len(order) < n:
            best = None
            for e, lst in ready.items():
                if not lst:
                    continue
                f = free.get(e, 0.0)
                avail = [i for i in lst if rt[i] <= f]
                if avail:
                    if e == "act":
                        same = [i for i in avail if ops[i].aset in (None, cur_set)]
                        if same:
                            avail = same
                    c = max(avail, key=lambda i: (bl[i], -i))
                    stt_ = f
                else:
                    c = min(lst, key=lambda i: (rt[i], -bl[i]))
                    stt_ = rt[c]
                key = (stt_, -bl[c])
                if best is None or key < best[0]:
                    best = (key, e, c, stt_)
            _, e, c, stt_ = best
            ready[e].remove(c)
            o = ops[c]
            if e == "act" and o.aset is not None:
                if o.aset != cur_set:
                    stt_ += 1300.0
                cur_set = o.aset
            order.append(o)
            o.st = stt_
            free[e] = stt_ + (60.0 if o.is_dma else o.dur)
            o.fin = stt_ + o.dur
            for sc_ in o.succ:
                indeg[sc_.idx] -= 1
                if indeg[sc_.idx] == 0:
                    r = 0.0
                    for d in sc_.deps:
                        lat = 0.0 if (d.eng == "pe" and sc_.eng == "pe" and not d.is_dma and not sc_.is_dma) else L
                        if d.fin + lat > r:
                            r = d.fin + lat
                    rt[sc_.idx] = r
                    ready.setdefault(sc_.eng, []).append(sc_.idx)
        self.makespan = max(o.fin for o in ops)
        return order

    def emit(self, stack):
        nc = self.nc
        engs = {"pe": nc.tensor, "act": nc.scalar, "dve": nc.vector, "pool": nc.gpsimd, "sp": nc.sync}
        import os
        if not self.sched:
            order = list(self.ops)
        elif os.environ.get("K_SCHED", "hlfet") == "hlfet":
            order = self.schedule_hlfet()
        else:
            order = self.schedule()
        pos = {}
        for k, op in enumerate(order):
            pos[op.idx] = k
        for op in order:
            latest = {}
            for d in op.deps:
                if d.is_dma:
                    continue
                if d.eng == op.eng and not op.is_dma and d.eng == "pe":
                    continue
                cur = latest.get(d.eng)
                if cur is None or pos[d.idx] > pos[cur.idx]:
                    latest[d.eng] = d
            op.lat = latest
            for d in latest.values():
                d.signal = True
        cnt = {e: 0 for e in COMPUTE}
        dq = {}
        for op in order:
            if op.is_dma:
                k = dq.get(op.eng, 0)
                dq[op.eng] = k + 1
                ns = self.slots[op.eng]
                op.slot = (op.eng, k % ns)
                op.dval = 16 * (k // ns + 1)
            elif op.signal:
                cnt[op.eng] += 1
                op.count = cnt[op.eng]
        sems = {}
        for e in COMPUTE:
            sems[e] = stack.enter_context(nc.semaphore("s_" + e))
        for q in dq:
            for k in range(min(self.slots[q], dq[q])):
                sems[(q, k)] = stack.enter_context(nc.semaphore("d_%s_%d" % (q, k)))
        waited = {}
        nwaits = 0
        for op in order:
            e = engs[op.eng]
            need = {}
            for d in op.deps:
                if d.is_dma:
                    key, val = d.slot, d.dval
                    if need.get(key, 0) < val:
                        need[key] = val
            for eng_, d in op.lat.items():
                need[eng_] = d.count
            if op.is_dma and op.dval > 16:
                if need.get(op.slot, 0) < op.dval - 16:
                    need[op.slot] = op.dval - 16
            for key, val in need.items():
                wk = (op.eng, key)
                if waited.get(wk, 0) >= val:
                    continue
                waited[wk] = val
                e.wait_ge(sems[key], val)
                nwaits += 1
            ins = op.fn(e)
            if op.is_dma:
                ins.then_inc(sems[op.slot], 16)
            elif op.signal:
                ins.then_inc(sems[op.eng], 1)
        for q, n in dq.items():
            e = engs[q]
            ns = self.slots[q]
            for k in range(min(n, ns)):
                e.wait_ge(sems[(q, k)], 16 * ((n - 1 - k) // ns + 1))
        for ce in COMPUTE:
            if cnt[ce] > 0:
                nc.sync.wait_ge(sems[ce], cnt[ce])
        self.stats = dict(nops=len(self.ops), nwaits=nwaits, cnt=cnt, dq=dq, makespan_us=self.makespan / 1000.0)


C_QKV, C_Z, C_B, C_A, C_SQ, C_SK, C_SV, C_G = 0, 3072, 4096, 4104, 4112, 5136, 5392, 5648


def weight_groups():
    g = []
    r = lambda a, n: list(range(a, a + n))
    g.append(("tm", "w_in", 1024, r(C_B, 16) + r(C_SV, 256)))
    for hg in range(2):
        g.append(("q%d" % hg, "w_in", 1024, r(C_QKV + hg * 512, 512)))
        g.append(("k%d" % hg, "w_in", 1024, r(C_QKV + 1024 + hg * 512, 512)))
        g.append(("v%d" % hg, "w_in", 1024, r(C_QKV + 2048 + hg * 512, 512)))
        g.append(("z%d" % hg, "w_in", 1024, r(C_Z + hg * 512, 512)))
    for i in range(2):
        g.append(("sq%d" % i, "w_in", 1024, r(C_SQ + i * 512, 512)))
    cols = []
    for j in range(4):
        cols += r(C_SK + j * 64, 64) + r(C_SK + j * 64, 64)
    g.append(("skd", "w_in", 1024, cols))
    for i in range(2):
        g.append(("ga%d" % i, "w_in", 1024, r(C_G + i * 512, 512)))
        g.append(("gb%d" % i, "w_in", 1024, r(C_G + 1024 + i * 512, 512)))
    for i in range(2):
        g.append(("wo%d" % i, "w_out", 1024, r(i * 512, 512)))
    for hh in range(2):
        for gi in range(4):
            g.append(("up%d_%d" % (hh, gi), "w_up", 1024, r((hh * 16 + gi * 4) * 128, 512)))
        for cb in range(4):
            g.append(("dn%d_%d" % (hh, cb), "w_down", (hh * 2048, 2048), r(cb * 256, 256)))
    for i in range(2):
        g.append(("pg%d" % i, "w_ple_gate", 1024, r(i * 512, 512)))
    g.append(("pp", "w_ple_proj", 256, r(0, 1024)))
    return g


def group_offsets():
    offs = {}
    o = 0
    for (name, src, rows, cols) in weight_groups():
        k = rows[1] if isinstance(rows, tuple) else rows
        n = (k // 128) * len(cols)
        offs[name] = (o, k // 128, len(cols))
        o += n
    return offs, o


SP_NM, SP_NF, SP_NP, SP_CW, SP_GN, SP_QN, SP_KN, SP_SK, SP_AL, SP_DT = 0, 8, 16, 24, 120, 121, 122, 123, 131, 139
NSP = 147

CSTF_NAMES = ["ONES", "U", "U_S", "ONESSEQ_S", "DD", "DO", "DN_S"]
CSTF = {n: i * 128 for i, n in enumerate(CSTF_NAMES)}
CSTF["DC"] = len(CSTF_NAMES) * 128
CSTF["SM"] = CSTF["DC"] + 8
NCSTF = CSTF["SM"] + 16
CSTB_NAMES = ["IDENT", "ONES", "ONESB64", "L", "MS", "MI", "MS_S", "MI_S", "MD4", "ML4", "MLT4", "ML8", "MLT8",
              "ML16", "MLT16", "ML32", "MLT32", "ML64", "MLT64"]
CSTB = {n: i * 128 for i, n in enumerate(CSTB_NAMES)}
NCSTB = len(CSTB_NAMES) * 128


def make_consts():
    cf = np.zeros((128, NCSTF), np.float32)
    cb = np.zeros((128, NCSTB), np.float32)
    i = np.arange(128)
    P, Fq = np.meshgrid(i, i, indexing="ij")
    same = (P // 8) == (Fq // 8)

    def put(n, m):
        m = np.asarray(m, np.float32)
        if n in CSTF:
            cf[:, CSTF[n]:CSTF[n] + m.shape[1]] = m
        if n in CSTB:
            cb[:, CSTB[n]:CSTB[n] + m.shape[1]] = m
    put("IDENT", P == Fq)
    put("ONES", np.ones((128, 128)))
    put("ONESB64", (P // 64) == (Fq // 64))
    put("U", P <= Fq)
    put("L", P > Fq)
    put("MS", P > Fq)
    put("MI", P >= Fq)
    put("U_S", same & (P <= Fq))
    put("MS_S", same & (P > Fq))
    put("MI_S", same & (P >= Fq))
    put("ONESSEQ_S", same)
    put("DD", np.where(Fq >= P, Fq - P, BIG))
    put("DO", np.where(Fq <= P, Fq + 128 - P, BIG))
    put("DN_S", np.where(same & (Fq >= P), Fq - P, BIG))
    put("MD4", (P // 4) == (Fq // 4))
    for m in (4, 8, 16, 32, 64):
        ml = ((P // (2 * m)) == (Fq // (2 * m))) & ((P // m) == (Fq // m) + 1)
        put("ML%d" % m, ml)
        put("MLT%d" % m, ml.T)
    j = np.arange(128)[:, None]
    t = np.arange(8)[None, :]
    put("DC", np.where(j >= t, 128 + t - j, BIG))
    put("SM", (np.arange(128)[:, None] // 8) == np.arange(16)[None, :])
    return cf, cb


class Cfg:
    def __init__(self, depth=4, seq=2048, tt=256, sample=True, stages=("gdn", "swa", "ffn", "ple"), dbg=()):
        self.depth = depth
        self.seq = seq
        self.tt = tt
        self.sample = sample
        self.ntok = seq + (128 if sample else 0)
        self.stages = stages
        self.dbg = dbg
        import os
        self.nopair = os.environ.get("K_NOPAIR") is not None


def build(cfg):
    nc = bass.Bass("TRN2", target_bir_lowering=False)
    st = contextlib.ExitStack()
    DEPTH, SEQ, TT, NTOK = cfg.depth, cfg.seq, cfg.tt, cfg.ntok
    offs, TOT = group_offsets()
    import os as _os
    PEADD = _os.environ.get("K_PEADD", "0") == "1"
    SQPOOL = _os.environ.get("K_SQPOOL", "0") == "1"
    POOLMIX = _os.environ.get("K_POOLMIX", "abcs")
    R = Rec(nc, {"sp": 12, "pool": 3, "act": 4})

    def din(name, shape, dt=F32):
        return nc.dram_tensor(name, list(shape), dt, kind="ExternalInput").ap()

    def dout(name, shape, dt=F32):
        return nc.dram_tensor(name, list(shape), dt, kind="ExternalOutput").ap()

    def sb(name, shape, dt=F32):
        return st.enter_context(nc.sbuf_tensor(name, list(shape), dt))

    xT = din("xT", [128, KC, NTOK])
    pT = din("pT", [DEPTH, 128, 2, NTOK])
    wsrc = din("wsrc", [DEPTH, 128, TOT])
    spar = din("spar", [128, DEPTH, NSP])
    cstd = din("cst", [128, NCSTF])
    cstmd = din("cstm", [128, NCSTB])
    wbf = nc.dram_tensor("wbf", [DEPTH, 128, TOT], BF16, kind="Internal").ap()
    yT = dout("yT", [128, KC, NTOK])
    o_convp = dout("o_convp", [DEPTH, 128, 24, 3])
    o_gdnp = dout("o_gdnp", [DEPTH, 128, 8, 128])
    o_kp = dout("o_kp", [DEPTH, 128, 4, 128])
    o_vp = dout("o_vp", [DEPTH, 128, 256])
    if cfg.sample:
        i_cconv = din("i_cconv", [DEPTH, 128, 24, 16, 3])
        i_sgdn = din("i_sgdn", [DEPTH, 16, 8, 128, 128])
        i_kcT = din("i_kcT", [DEPTH, 4, 128, 16, 128])
        i_vc = din("i_vc", [DEPTH, 16, 128, 256])
        i_kc = din("i_kc", [DEPTH, 16, 128, 256])
        o_convs = dout("o_convs", [DEPTH, 128, 24, 16, 3])
        o_gdns = dout("o_gdns", [DEPTH, 16, 8, 128, 128])
        o_ksn = dout("o_ksn", [DEPTH, 128, 4, 128])
        o_vsn = dout("o_vsn", [DEPTH, 128, 256])
        o_kcopy = dout("o_kcopy", [DEPTH, 16, 120, 256])
        o_vcopy = dout("o_vcopy", [DEPTH, 16, 120, 256])
    dbg_out = {}

    cst = sb("cst_f", [128, NCSTF])
    cstb = sb("cst_b", [128, NCSTB], BF16)
    sp_t = sb("sp_t", [128, DEPTH, NSP])
    spd = sb("spd", [128, DEPTH, 32])
    hbuf = [sb("h%d" % i, [128, KC, TT]) for i in range(2)]
    xn = sb("xn", [128, KC, TT], BF16)
    xnF = sb("xnF", [128, KC, TT], BF16)
    arena = sb("arena", [128, 32, TT], BF16)
    arenaF = sb("arenaF", [128, 16, TT], BF16)
    gF = arenaF[:, 0:KC, :]
    qaT = sb("qaT", [128, KC, TT], BF16)
    yqF = sb("yqF", [128, TT])
    sqbF = [sb("sqbF%d" % i, [128, TT], BF16) for i in range(2)]
    rstdF = sb("rstdF", [128, TT])
    oaT = sb("oaT", [128, KC, TT], BF16)
    obT = sb("obT", [128, KC, TT], BF16)
    ND_ = 3
    rawx = [sb("rawx%d" % i, [128, TT + 48]) for i in range(ND_)]
    caccs = [sb("cacc%d" % i, [128, TT]) for i in range(ND_)]
    ctan = [sb("ctan%d" % i, [128, TT]) for i in range(ND_)]
    yqs = [sb("yq%d" % i, [128, TT]) for i in range(ND_)]
    sqb = [sb("sqb%d" % i, [128, TT], BF16) for i in range(2)]
    rstd = [sb("rstd%d" % i, [128, TT]) for i in range(2)]
    rot = [0]
    arena_f = arena[:, :, :].rearrange("p a t -> p (a t)").bitcast(F32)
    m1s = [arena_f[:, i * TT:(i + 1) * TT] for i in range(4)]
    m2s = [arena_f[:, (4 + i) * TT:(5 + i) * TT] for i in range(3)]
    NSLOT = int(_os.environ.get("K_NSLOT", "4"))
    wring = [sb("wring%d" % i, [128, 4096], BF16) for i in range(NSLOT)]
    pTf = sb("pTf", [128, 2, TT])
    pTb = sb("pTb", [128, 2, TT], BF16)
    NBMAX = TT // 128
    tmsm = sb("tmsm", [128, NBMAX, 16])
    gsm = sb("gsm", [128, NBMAX, 96])
    vext = sb("vext", [128, NBMAX + 1, 256], BF16)
    kdup = sb("kdup", [128, 4, (NBMAX + 1) * 128], BF16)
    S_all = sb("S_all", [128, max(DEPTH, 3), 8, 128])
    Sb = sb("Sb", [128, 8, 128], BF16)
    tails = sb("tails", [128, DEPTH, 24, 3])
    kprev = sb("kprev", [128, DEPTH, 4, 128], BF16)
    vprev = sb("vprev", [128, DEPTH, 256], BF16)
    GL = sb("GL", [128, 4, 128])
    Eraw = sb("Eraw", [128, 4, 128])
    Ems = sb("Ems", [128, 4, 128], BF16)
    Emi = sb("Emi", [128, 4, 128], BF16)
    Nb = sb("Nb", [128, 4, 128], BF16)
    Pm = [sb("Pm%d" % i, [128, 4, 128], BF16) for i in range(2)]
    PTm = [sb("PTm0", [128, 4, 128], BF16)]
    X1b = sb("X1b", [128, 4, 128], BF16)
    X2b = sb("X2b", [128, 4, 128], BF16)
    No_b = sb("No_b", [128, 4, 128], BF16)
    NoT_b = sb("NoT_b", [128, 4, 128], BF16)
    Tdn = sb("Tdn", [128, 4, 128], BF16)
    TTb = sb("TTb", [128, 4, 128], BF16)
    qkb = sb("qkb", [128, 4, 128], BF16)
    qkT = sb("qkT", [128, 4, 128], BF16)
    kbg, kdec, vbt, wTb = Ems, Emi, Pm[0], Pm[1]
    u_t = GL
    vnew = X1b
    tq = sb("tq", [128, 4, 128])
    o_t = Eraw
    osq = tq
    on_b = X2b
    osm = sb("osm", [128, 16])
    sc = [sb("sc%d" % i, [128, 4, 128]) for i in range(2)]
    PTa = [sb("PTa%d" % i, [128, 4, 128], BF16) for i in range(2)]
    rden = sb("rden", [128, 2, 128])
    kfin = sc[0]
    vfin = rden[:, :, :].rearrange("p a b -> p (a b)")
    if cfg.sample:
        ccin = sb("ccin", [128, 24, 16, 3])
        ccout = sb("ccout", [128, 24, 16, 3])
        Ssf = None
        Ssb = [sb("Ssb%d" % i, [128, 8, 128], BF16) for i in range(2)]
        kcb = sb("kcb", [128, 16, 128], BF16)
        vcb = sb("vcb", [128, 16, 64], BF16)
        kdm = [sb("kdm%d" % i, [128, 4, 128], BF16) for i in range(2)]
        scs = sb("scs", [128, 16, 4, 8])
        PTc = sb("PTc", [128, 16, 4, 9], BF16)
        qsT = No_b
        uT = GL
        vnT = NoT_b
        egls = sb("egls", [128, 16, 8])
        gsq = sb("gsq", [128, 16, 8])
        snew = None
    import sys as _sys
    print("[kernel] SBUF bytes/partition remaining:", nc.sbuf_bytes_remaining, file=_sys.stderr)
    banks = [st.enter_context(nc.psum_tensor("ps%d" % i, [128, 512], F32)) for i in range(8)]
    bank_i = [0]

    bank_f = [0]
    bank_g = [0]
    NG_, NM_, NF_ = [int(x) for x in _os.environ.get("K_BANKS", "2,4,2").split(",")]

    def bank(pool="M"):
        if pool == "F":
            b = banks[NG_ + NM_ + bank_f[0] % NF_]
            bank_f[0] += 1
        elif pool == "G":
            b = banks[bank_g[0] % NG_]
            bank_g[0] += 1
        else:
            b = banks[NG_ + bank_i[0] % NM_]
            bank_i[0] += 1
        return b

    def C(name, n=128):
        return cst[:, CSTF[name]:CSTF[name] + n]

    def CB(name):
        return cstb[:, CSTB[name]:CSTB[name] + 128]

    IDb = CB("IDENT")
    ONESb = CB("ONES")
    ONES64b = CB("ONESB64")

    isap = lambda x: not isinstance(x, (int, float))

    def vdur(eng, ap):
        n = fsz(ap)
        return (n + 70) / (0.96 if eng == "dve" else 0.5) + 60.0

    def fsz(ap):
        n = 1
        for d in ap.shape[1:]:
            n *= d
        return n

    def mm(out, lhsT, rhs, start=True, stop=True, skip=False):
        d = max(64, fsz(rhs)) / 1.95 * (4.0 if rhs.dtype == F32 else 1.0) + 8.0
        R.add("pe", lambda e: e.matmul(out, lhsT=lhsT, rhs=rhs, start=start, stop=stop, skip_group_check=skip),
              [lhsT, rhs], [out], dur=d)

    def tr(out, in_, ident):
        R.add("pe", lambda e: e.transpose(out, in_, ident), [in_, ident], [out], dur=128 / 2.4 + 8.0)

    def act(out, in_, func, scale=1.0, bias=0.0):
        rd = [in_] + [x for x in (scale, bias) if isap(x)]
        o_ = R.add("act", lambda e: e.activation(out=out, in_=in_, func=func, bias=bias, scale=scale), rd, [out],
                   dur=(fsz(in_) + 200) / 1.2)
        if o_ is not None:
            o_.aset = "B" if func == AF.Ln else ("A" if func == AF.Tanh else None)

    def tt(eng, out, a, b, op):
        R.add(eng, lambda e: e.tensor_tensor(out=out, in0=a, in1=b, op=op), [a, b], [out], dur=vdur(eng, a))

    def ts(eng, out, a, s1, op0, s2=None, op1=None):
        rd = [a] + [x for x in (s1, s2) if x is not None and isap(x)]
        if op1 is None:
            R.add(eng, lambda e: e.tensor_scalar(out=out, in0=a, scalar1=s1, scalar2=None, op0=op0), rd, [out],
                  dur=vdur(eng, a))
        else:
            R.add(eng, lambda e: e.tensor_scalar(out=out, in0=a, scalar1=s1, scalar2=s2, op0=op0, op1=op1), rd, [out],
                  dur=vdur(eng, a))

    def stt(out, a, s, b, op0, op1):
        rd = [a, b] + ([s] if isap(s) else [])
        R.add("dve", lambda e: e.scalar_tensor_tensor(out=out, in0=a, scalar=s, in1=b, op0=op0, op1=op1), rd, [out],
              dur=vdur("dve", a))

    def cp(eng, out, in_):
        if eng == "act":
            act(out, in_, AF.Copy)
        else:
            R.add(eng, lambda e: e.tensor_copy(out=out, in_=in_), [in_], [out], dur=vdur(eng, in_))

    def red(out, in_, op):
        R.add("dve", lambda e: e.tensor_reduce(out=out, in_=in_, axis=AX.X, op=op), [in_], [out], dur=vdur("dve", in_))

    def recip(out, in_):
        R.add("dve", lambda e: e.reciprocal(out=out, in_=in_), [in_], [out], dur=vdur("dve", in_))

    def memset(eng, out, v):
        R.add(eng, lambda e: e.memset(out, v), [], [out], dur=vdur(eng, out))

    def dma(q, out, in_, extra=(), **kw):
        nbytes = 1
        for d in out.shape:
            nbytes *= d
        nbytes *= (4 if out.dtype == F32 else 2) + (4 if in_.dtype == F32 else 2)
        return R.add(q, lambda e: e.dma_start(out=out, in_=in_, **kw), [in_], [out], is_dma=True,
                     dur=2000.0 + nbytes / 2 / 400.0, extra=extra)

    def dbg(name, ap, shape):
        if name in cfg.dbg:
            if name not in dbg_out:
                dbg_out[name] = dout("dbg_" + name, shape)
            dma("sp", dbg_out[name], ap)

    def rsqrt_ln(out, in_, scale, eps, mult=1.0):
        act(out, in_, AF.Ln, scale=scale, bias=eps)
        act(out, out, AF.Exp, scale=-0.5, bias=float(np.log(mult)))

    dma("sp", cst[:, :], cstd)
    dma("sp", sp_t[:, :, :], spar)
    step = 2 * 4096
    o = 0
    while o < TOT:
        n = min(step, TOT - o)
        dma("pool", wbf[0, :, o:o + n], wsrc[0, :, o:o + n], max_dma_last_dim=4096)
        o += n
    cast_done = set()
    stage = S_all[:, :, :, :].rearrange("p a b c -> p (a b c)")
    dma("sp", stage[:, 0:NCSTB], cstmd)
    cp("dve", cstb[:, :], stage[:, 0:NCSTB])
    for l in range(DEPTH):
        ts("dve", spd[:, l, 0:1], sp_t[:, l, SP_GN:SP_GN + 1], 0.5, ALU.mult)
        ts("dve", spd[:, l, 1:2], sp_t[:, l, SP_QN:SP_QN + 1], 0.125, ALU.mult)
        act(spd[:, l, 2:10], sp_t[:, l, SP_SK:SP_SK + 8], AF.Exp)
        act(spd[:, l, 10:18], sp_t[:, l, SP_AL:SP_AL + 8], AF.Exp)
        ts("dve", spd[:, l, 10:18], spd[:, l, 10:18], -1.0, ALU.mult)

    R.const_names.update(["cst_f", "cst_b", "sp_t", "spd", "xT", "pT", "wsrc", "spar", "cst", "cstm", "i_cconv", "i_sgdn",
                          "i_kcT", "i_vc", "i_kc"])

    plan = []
    wstate = dict(next_load=0, next_use=0)

    def wload(i):
        l, g = plan[i]
        o, kc, ncol = offs[g]
        n = kc * ncol
        op_ = dma("sp", wring[i % NSLOT][:, 0:n], wbf[l, :, o:o + n])
        if (l, g) not in cast_done and not R.dry:
            cast_done.add((l, g))
            if l + 1 < DEPTH:
                dma("pool", wbf[l + 1, :, o:o + n], wsrc[l + 1, :, o:o + n], max_dma_last_dim=4096, extra=(op_,))

    def wget(l, g):
        i = wstate["next_use"]
        wstate["next_use"] += 1
        o, kc, ncol = offs[g]
        if R.dry:
            plan.append((l, g))
        else:
            assert plan[i] == (l, g), (plan[i], l, g)
            while wstate["next_load"] < min(len(plan), i + NSLOT):
                wload(wstate["next_load"])
                wstate["next_load"] += 1
        return wring[i % NSLOT][:, 0:kc * ncol].rearrange("p (k n) -> p k n", k=kc)

    tiles = []
    t0 = 0
    while t0 < SEQ:
        tiles.append(dict(kind="p", t0=t0, nt=TT, first=(t0 == 0), last=(t0 + TT >= SEQ), idx=len(tiles)))
        t0 += TT
    if cfg.sample:
        tiles.append(dict(kind="s", t0=SEQ, nt=128, first=True, last=True, idx=len(tiles)))

    def rmsnorm_fm(hs, xd, l, col, nt, sq2, r, pool="M"):
        ps = bank(pool)
        for kc in range(KC):
            s = sq2[kc % 2]
            act(s[:, 0:nt], hs[:, kc, 0:nt], AF.Square)
            mm(ps[:, 0:nt], ONESb, s[:, 0:nt], start=(kc == 0), stop=(kc == KC - 1))
        rsqrt_ln(r[:, 0:nt], ps[:, 0:nt], 1.0 / D_MODEL, EPS)
        for kc in range(KC):
            stt(xd[:, kc, 0:nt], hs[:, kc, 0:nt], sp_t[:, l, col + kc:col + kc + 1], r[:, 0:nt], ALU.mult, ALU.mult)

    def proj_chunk(w, c, nt, rhs_t, pool="M"):
        ps = bank(pool)
        nk = w.shape[1]
        for kc in range(nk):
            mm(ps[:, 0:nt], w[:, kc, c * 128:(c + 1) * 128], rhs_t[:, kc, 0:nt], start=(kc == 0), stop=(kc == nk - 1))
        return ps

    def gen_M(tile, l):
        kind, nt = tile["kind"], tile["nt"]
        nb = nt // 128
        samp = (kind == "s")
        ST = cfg.stages
        h = hbuf[tile["idx"] % 2]
        if not samp:
            if tile["first"]:
                memset("pool", S_all[:, l, :, :], 0.0)
                memset("pool", tails[:, l, :, :], 0.0)
            cp("act", Sb[:, :, :], S_all[:, l, :, :])
            if not tile["first"]:
                cp("pool", vext[:, 0, :], vprev[:, l, :])
                cp("pool", kdup[:, :, 0:128], kprev[:, l, :, :])
        else:
            dma("sp", ccin[:, :, :, :], i_cconv[l])
        R.cur_tag = "1norm"
        rmsnorm_fm(h, xn, l, SP_NM, nt, sqb[0:2], rstd[0])
        R.cur_tag = "2tm"
        w = wget(l, "tm")
        for b in range(nb):
            ps = bank()
            for kc in range(KC):
                mm(ps[:, 0:272], xn[:, kc, b * 128:(b + 1) * 128], w[:, kc, 0:272], start=(kc == 0), stop=(kc == KC - 1))
            cp("act", tmsm[:, b, :], ps[:, 0:16])
            cp("dve", vext[:, b + 1, :], ps[:, 16:272])
            if tile["last"] and b == nb - 1:
                cp("act", vfin[:, :], ps[:, 16:272])
                dma("sp", (o_vsn if samp else o_vp)[l], vfin[:, :])
        yield
        R.cur_tag = "3conv"
        L_ = 8 if samp else nt
        nseq = 16 if samp else 1
        for hg in range(2):
            for typ in ("q", "k", "v"):
                w = wget(l, "%s%d" % (typ, hg))
                for c in range(4):
                    ch = {"q": 0, "k": 8, "v": 16}[typ] + hg * 4 + c
                    ps = proj_chunk(w, c, nt, xn)
                    if "gdn" not in ST:
                        continue
                    rot[0] += 1
                    ri = rot[0] % ND_
                    cacc, yq = caccs[ri], yqs[ri]
                    rx = rawx[ri]
                    rxv = rx[:, 0:nseq * (L_ + 3)].rearrange("p (s t) -> p s t", s=nseq)
                    psv = ps[:, 0:nt].rearrange("p (s t) -> p s t", s=nseq)
                    cp("act", rxv[:, :, 3:3 + L_], psv)
                    if samp:
                        cp("pool", rxv[:, :, 0:3], ccin[:, ch, :, :])
                    else:
                        cp("pool", rxv[:, :, 0:3], tails[:, l, ch:ch + 1, :])
                    av = cacc[:, 0:nt].rearrange("p (s t) -> p s t", s=nseq)
                    cw = lambda j: sp_t[:, l, SP_CW + ch * 4 + j:SP_CW + ch * 4 + j + 1]
                    act(av, psv, AF.Copy, scale=cw(3))
                    for j in range(0, 3):
                        stt(av, rxv[:, :, j:j + L_], cw(j), av, ALU.mult, ALU.add)
                    if samp:
                        cp("pool", ccout[:, ch, :, :], rxv[:, :, L_:L_ + 3])
                    else:
                        cp("pool", tails[:, l, ch:ch + 1, :], rxv[:, :, L_:L_ + 3])
                    tn = ctan[ri]
                    act(tn[:, 0:nt], cacc[:, 0:nt], AF.Tanh, scale=0.5)
                    if typ == "v":
                        stt(arena[:, hg * 16 + 8 + c, 0:nt], tn[:, 0:nt], 1.0, cacc[:, 0:nt], ALU.add, ALU.mult)
                    else:
                        stt(yq[:, 0:nt], tn[:, 0:nt], 1.0, cacc[:, 0:nt], ALU.add, ALU.mult)
                        s_ = sqb[ri % 2]
                        if SQPOOL:
                            tt("pool", s_[:, 0:nt], yq[:, 0:nt], yq[:, 0:nt], ALU.mult)
                        else:
                            act(s_[:, 0:nt], yq[:, 0:nt], AF.Square)
                        pn = bank()
                        mm(pn[:, 0:nt], ONESb, s_[:, 0:nt])
                        r = rstd[ri % 2]
                        rsqrt_ln(r[:, 0:nt], pn[:, 0:nt], 1.0, 4.0 * EPS, mult=(128.0 ** -0.5 if typ == "q" else 1.0))
                        dst = arena[:, hg * 16 + (0 if typ == "q" else 4) + c, 0:nt]
                        tt("pool" if "a" in POOLMIX else "dve", dst, yq[:, 0:nt], r[:, 0:nt], ALU.mult)
                yield
            w = wget(l, "z%d" % hg)
            for c in range(4):
                ps = proj_chunk(w, c, nt, xn)
                if "gdn" not in ST:
                    continue
                tn = ctan[c % ND_]
                act(tn[:, 0:nt], ps[:, 0:nt], AF.Tanh, scale=0.5)
                stt(arena[:, hg * 16 + 12 + c, 0:nt], tn[:, 0:nt], 1.0, ps[:, 0:nt], ALU.add, ALU.mult)
            if "gdn" in ST:
                for b in range(nb):
                    R.cur_tag = "3gdn"
                    if hg == 0:
                        gdn_small(tile, l, b)
                    gdn_unit(tile, l, b, hg)
                R.cur_tag = "3conv"
            else:
                if hg == 0:
                    memset("pool", oaT[:, :, 0:nt], 0.0)
            yield
        if "gdn" in ST:
            if samp:
                dma("sp", o_convs[l], ccout[:, :, :, :])
            elif tile["last"]:
                dma("sp", o_convp[l], tails[:, l, :, :])
                dma("sp", o_gdnp[l], S_all[:, l, :, :])
        R.cur_tag = "4swa"
        qa = qaT
        for i in range(2):
            w = wget(l, "sq%d" % i)
            for c in range(4):
                ps = proj_chunk(w, c, nt, xn)
                if "swa" not in ST:
                    continue
                s_ = sqb[c % 2]
                act(s_[:, 0:nt], ps[:, 0:nt], AF.Square)
                pn = bank()
                mm(pn[:, 0:nt], ONES64b, s_[:, 0:nt])
                r = rstd[c % 2]
                rsqrt_ln(r[:, 0:nt], pn[:, 0:nt], 1.0 / 64.0, EPS)
                stt(qa[:, i * 4 + c, 0:nt], ps[:, 0:nt], spd[:, l, 1:2], r[:, 0:nt], ALU.mult, ALU.mult)
            yield
        w = wget(l, "skd")
        for c in range(4):
            ps = proj_chunk(w, c, nt, xn)
            if "swa" not in ST:
                continue
            s_ = sqb[c % 2]
            act(s_[:, 0:nt], ps[:, 0:nt], AF.Square)
            pn = bank()
            mm(pn[:, 0:nt], ONES64b, s_[:, 0:nt])
            r = rstd[c % 2]
            rsqrt_ln(r[:, 0:nt], pn[:, 0:nt], 1.0 / 64.0, EPS)
            stt(kdup[:, c, 128:128 + nt], ps[:, 0:nt], sp_t[:, l, SP_KN:SP_KN + 1], r[:, 0:nt], ALU.mult, ALU.mult)
            if tile["last"]:
                stt(kfin[:, c, :], ps[:, nt - 128:nt], sp_t[:, l, SP_KN:SP_KN + 1], r[:, nt - 128:nt], ALU.mult, ALU.mult)
        if "swa" in ST:
            if tile["last"]:
                dma("sp", (o_ksn if samp else o_kp)[l], kfin[:, :, :])
            for qb in range(nb):
                for kvh in range(4):
                    swa_unit(tile, l, qb, kvh)
            if not samp and not tile["last"]:
                cp("pool", vprev[:, l, :], vext[:, nb, :])
                cp("pool", kprev[:, l, :, :], kdup[:, :, nb * 128:(nb + 1) * 128])
            if samp:
                dma("act", o_kcopy[l], i_kc[l, :, 8:128, :])
                dma("act", o_vcopy[l], i_vc[l, :, 8:128, :])
        else:
            memset("pool", obT[:, :, 0:nt], 0.0)
        yield
        R.cur_tag = "5mix"
        for i in range(2):
            wa = wget(l, "ga%d" % i)
            for c in range(4):
                ps = proj_chunk(wa, c, nt, xn)
                yq = yqs[c % ND_]
                act(yq[:, 0:nt], ps[:, 0:nt], AF.Tanh, scale=0.5)
                stt(m1s[c][:, 0:nt], yq[:, 0:nt], 1.0, oaT[:, i * 4 + c, 0:nt], ALU.add, ALU.mult)
            yield
            wb_ = wget(l, "gb%d" % i)
            for c in range(4):
                ps = proj_chunk(wb_, c, nt, xn)
                yq = yqs[c % ND_]
                act(yq[:, 0:nt], ps[:, 0:nt], AF.Tanh, scale=0.5)
                m2 = m2s[c % 3]
                stt(m2[:, 0:nt], yq[:, 0:nt], 1.0, obT[:, i * 4 + c, 0:nt], ALU.add, ALU.mult)
                tt("pool" if "b" in POOLMIX else "dve", oaT[:, i * 4 + c, 0:nt], m1s[c][:, 0:nt], m2[:, 0:nt], ALU.add)
            yield
        R.cur_tag = "6wo"
        for i in range(2):
            w = wget(l, "wo%d" % i)
            for c in range(4):
                ps = proj_chunk(w, c, nt, oaT)
                stt(h[:, i * 4 + c, 0:nt], ps[:, 0:nt], 0.5, h[:, i * 4 + c, 0:nt], ALU.mult, ALU.add)
            yield

    def gen_F(tile, l):
        nt = tile["nt"]
        ST = cfg.stages
        h = hbuf[tile["idx"] % 2]
        R.cur_tag = "7ffn"
        if "ffn" in ST:
            rmsnorm_fm(h, xnF, l, SP_NF, nt, sqbF, rstdF, pool="F")
        hid = arenaF
        for hh in range(2):
            for gi in range(4):
                w = wget(l, "up%d_%d" % (hh, gi))
                if "ffn" in ST:
                    for c in range(4):
                        ps = proj_chunk(w, c, nt, xnF, pool="F")
                        act(yqF[:, 0:nt], ps[:, 0:nt], AF.Relu)
                        tt("pool", hid[:, gi * 4 + c, 0:nt], yqF[:, 0:nt], yqF[:, 0:nt], ALU.mult)
                yield
            for cb in range(4):
                w = wget(l, "dn%d_%d" % (hh, cb))
                if "ffn" in ST:
                    for c in range(2):
                        ps = proj_chunk(w, c, nt, hid, pool="F")
                        oc = cb * 2 + c
                        tt("dve", h[:, oc, 0:nt], ps[:, 0:nt], h[:, oc, 0:nt], ALU.add)
                yield
        R.cur_tag = "8ple"
        if "ple" in ST:
            rmsnorm_fm(h, xnF, l, SP_NP, nt, sqbF, rstdF, pool="F")
            dma("sp", pTf[:, :, 0:nt], pT[l, :, :, tile["t0"]:tile["t0"] + nt])
            cp("pool", pTb[:, :, 0:nt], pTf[:, :, 0:nt])
        for i in range(2):
            w = wget(l, "pg%d" % i)
            if "ple" in ST:
                for c in range(4):
                    ps = proj_chunk(w, c, nt, xnF, pool="F")
                    act(gF[:, i * 4 + c, 0:nt], ps[:, 0:nt], AF.Tanh, scale=0.5)
            yield
        wp = wget(l, "pp")
        if "ple" in ST:
            for oc in range(8):
                ps2 = proj_chunk(wp, oc, nt, pTb, pool="F")
                stt(yqF[:, 0:nt], gF[:, oc, 0:nt], 1.0, ps2[:, 0:nt], ALU.add, ALU.mult)
                stt(h[:, oc, 0:nt], yqF[:, 0:nt], 0.5, h[:, oc, 0:nt], ALU.mult, ALU.add)
        if l == DEPTH - 1:
            dma("sp", yT[:, :, tile["t0"]:tile["t0"] + nt], h[:, :, 0:nt])
        yield

    def gdn_small(tile, l, b):
        samp = tile["kind"] == "s"
        G = lambda a: gsm[:, b, a:a + 8]
        bl = tmsm[:, b, 0:8]
        al = tmsm[:, b, 8:16]
        act(G(80), bl, AF.Tanh, scale=0.5)
        ts("dve", G(0), G(80), 0.5, ALU.mult, 0.5, ALU.add)
        tt("dve", G(80), al, sp_t[:, l, SP_DT:SP_DT + 8], ALU.add)
        act(G(80), G(80), AF.Exp)
        act(G(80), G(80), AF.Ln, bias=1.0)
        tt("dve", G(8), G(80), spd[:, l, 10:18], ALU.mult)
        ps = bank("G")
        mm(ps[:, 0:8], C("U_S") if samp else C("U"), G(8))
        mm(ps[:, 8:16], C("ONESSEQ_S") if samp else C("ONES"), G(8))
        cp("dve", gsm[:, b, 16:32], ps[:, 0:16])
        act(G(32), G(16), AF.Exp)
        tt("dve", G(40), G(0), G(32), ALU.mult)
        tt("dve", G(80), G(24), G(16), ALU.subtract)
        act(G(48), G(80), AF.Exp)
        act(G(56), G(24), AF.Exp)
        ts("dve", G(64), G(0), -1.0, ALU.mult)
        ts("dve", G(72), G(0), 0.5, ALU.mult)
        if samp:
            tt("dve", gsq[:, :, :], G(8).unsqueeze(1).broadcast_to([128, 16, 8]),
               C("SM", 16).unsqueeze(2).broadcast_to([128, 16, 8]), ALU.mult)
            ps2 = bank("G")
            mm(ps2[:, 0:128], C("ONES"), gsq[:, :, :].rearrange("p s h -> p (s h)"))
            act(egls[:, :, :].rearrange("p s h -> p (s h)"), ps2[:, 0:128], AF.Exp)

    def bc4(ap):
        return ap.unsqueeze(2).broadcast_to([128, 4, 128])

    def hb(ap):
        return ap.unsqueeze(1).broadcast_to([128, 4, 128])

    def f4(ap):
        return ap.rearrange("p h n -> p (h n)")

    def v4(ap):
        return ap.rearrange("p (h n) -> p h n", h=4)

    def gdn_unit(tile, l, b, hg):
        samp = tile["kind"] == "s"
        H0 = hg * 4
        G = lambda a: gsm[:, b, a + H0:a + H0 + 4]
        blk = slice(b * 128, (b + 1) * 128)
        ab = hg * 16
        qT = lambda hh: arena[:, ab + 0 + hh, blk]
        kT = lambda hh: arena[:, ab + 4 + hh, blk]
        vT = lambda hh: arena[:, ab + 8 + hh, blk]
        Um = C("U_S") if samp else C("U")
        MSm = CB("MS_S") if samp else CB("MS")
        MIm = CB("MI_S") if samp else CB("MI")
        tt("dve", GL[:, :, :], bc4(G(8)), hb(CB("L")), ALU.mult)
        pd = bank("G")
        mm(pd[:, :], Um, f4(GL[:, :, :]))
        act(f4(Eraw[:, :, :]), pd[:, :], AF.Exp)
        tt("pool", Ems[:, :, :], Eraw[:, :, :], hb(MSm), ALU.mult)
        tt("pool", Emi[:, :, :], Eraw[:, :, :], hb(MIm), ALU.mult)
        pkk = bank("G")
        pqk = bank("G")
        for hh in range(4):
            mm(pkk[:, hh * 128:(hh + 1) * 128], kT(hh), kT(hh))
        for hh in range(4):
            mm(pqk[:, hh * 128:(hh + 1) * 128], qT(hh), kT(hh))
        for hh in range(4):
            stt(Nb[:, hh, :], pkk[:, hh * 128:(hh + 1) * 128], gsm[:, b, 64 + H0 + hh:64 + H0 + hh + 1], Ems[:, hh, :],
                ALU.mult, ALU.mult)
        tt("dve", f4(qkb[:, :, :]), pqk[:, :], f4(Emi[:, :, :]), ALU.mult)
        pt1 = bank("G")
        pt1b = pt1[:, :].bitcast(BF16)
        for hh in range(4):
            tr(pt1b[:, hh * 128:(hh + 1) * 128], Nb[:, hh, :], IDb)
        NTb = PTm[0]
        cp("act", f4(NTb[:, :, :]), pt1b[:, 0:512])
        pt2 = bank("G")
        pt2b = pt2[:, :].bitcast(BF16)
        for hh in range(4):
            tr(pt2b[:, hh * 128:(hh + 1) * 128], qkb[:, hh, :], IDb)
        cp("act", f4(qkT[:, :, :]), pt2b[:, 0:512])

        def mm4(lhs, rhs, add=None):
            p_ = bank("G")
            for hh in range(4):
                mm(p_[:, hh * 128:(hh + 1) * 128], lhs[:, hh, :], rhs[:, hh, :], start=True,
                   stop=(add is None or not PEADD))
                if add is not None and PEADD:
                    a_ = add if add is IDb else add[:, hh, :]
                    mm(p_[:, hh * 128:(hh + 1) * 128], IDb, a_, start=False, stop=True)
            return p_

        def evac_add(dst, p_, add, eng):
            if PEADD:
                cp(eng, f4(dst[:, :, :]), p_[:, :])
            elif add is IDb:
                tt("dve", dst[:, :, :], v4(p_[:, :]), hb(IDb), ALU.add)
            else:
                tt("dve", f4(dst[:, :, :]), p_[:, :], f4(add[:, :, :]), ALU.add)
        Nd, NdT = Pm[0], Pm[1]
        tt("dve", Nd[:, :, :], Nb[:, :, :], hb(CB("MD4")), ALU.mult)
        tt("dve", NdT[:, :, :], NTb[:, :, :], hb(CB("MD4")), ALU.mult)
        p1 = mm4(NdT, Nd, add=IDb)
        p2 = mm4(Nd, NdT, add=IDb)
        Q_, QT_, R_, RT_ = X1b, X2b, No_b, NoT_b
        evac_add(Q_, p1, IDb, "act")
        evac_add(QT_, p2, IDb, "act")
        tt("pool", R_[:, :, :], Nd[:, :, :], hb(CB("IDENT")), ALU.add)
        tt("pool", RT_[:, :, :], NdT[:, :, :], hb(CB("IDENT")), ALU.add)
        p3 = mm4(QT_, R_)
        p4 = mm4(R_, QT_)
        cp("act", f4(Tdn[:, :, :]), p3[:, :])
        cp("act", f4(TTb[:, :, :]), p4[:, :])
        levels = [4] if samp else [4, 8, 16, 32, 64]
        for li, m_ in enumerate(levels):
            last = (li == len(levels) - 1)
            tt("dve", No_b[:, :, :], Nb[:, :, :], hb(CB("ML%d" % m_)), ALU.mult)
            if not last:
                tt("pool" if "c" in POOLMIX else "dve", NoT_b[:, :, :], NTb[:, :, :], hb(CB("MLT%d" % m_)), ALU.mult)
                px1 = mm4(NoT_b, Tdn)
                cp("act", f4(X1b[:, :, :]), px1[:, :])
            px2 = mm4(No_b, TTb)
            cp("act", f4(X2b[:, :, :]), px2[:, :])
            if not last:
                py1 = mm4(TTb, X1b, add=Tdn)
            py2 = mm4(Tdn, X2b, add=TTb)
            if not last:
                evac_add(Tdn, py1, Tdn, "act")
            evac_add(TTb, py2, TTb, "dve")
        pk = bank("G")
        pkb = pk[:, :].bitcast(BF16)
        for hh in range(4):
            tr(pkb[:, hh * 128:(hh + 1) * 128], kT(hh), IDb)
        tt("dve", kbg[:, :, :], v4(pkb[:, 0:512]), bc4(G(40)), ALU.mult)
        tt("dve", kdec[:, :, :], v4(pkb[:, 0:512]), bc4(G(48)), ALU.mult)
        pv = bank("G")
        pvb = pv[:, :].bitcast(BF16)
        for hh in range(4):
            tr(pvb[:, hh * 128:(hh + 1) * 128], vT(hh), IDb)
        tt("dve", vbt[:, :, :], v4(pvb[:, 0:512]), bc4(G(72)), ALU.mult)
        pw = bank("G")
        for hh in range(4):
            mm(pw[:, hh * 128:(hh + 1) * 128], kbg[:, hh, :], TTb[:, hh, :])
        cp("act", f4(wTb[:, :, :]), pw[:, :])
        if not samp:
            pu = bank("G")
            for hh in range(4):
                mm(pu[:, hh * 128:(hh + 1) * 128], TTb[:, hh, :], vbt[:, hh, :])
            cp("act", f4(u_t[:, :, :]), pu[:, :])
            pws = bank("G")
            for hh in range(4):
                mm(pws[:, hh * 128:(hh + 1) * 128], wTb[:, hh, :], Sb[:, H0 + hh, :])
            tt("dve", f4(vnew[:, :, :]), f4(u_t[:, :, :]), pws[:, :], ALU.subtract)
            pqs = bank("G")
            for hh in range(4):
                mm(pqs[:, hh * 128:(hh + 1) * 128], qT(hh), Sb[:, H0 + hh, :])
            pin = bank("G")
            for hh in range(4):
                mm(pin[:, hh * 128:(hh + 1) * 128], qkT[:, hh, :], vnew[:, hh, :])
            tt("dve", tq[:, :, :], v4(pqs[:, :]), bc4(G(32)), ALU.mult)
            tt("dve", f4(o_t[:, :, :]), pin[:, :], f4(tq[:, :, :]), ALU.add)
            psu = bank("G")
            for hh in range(4):
                mm(psu[:, hh * 128:(hh + 1) * 128], kdec[:, hh, :], vnew[:, hh, :])
            Sv = S_all[:, l, H0:H0 + 4, :]
            tt("pool", Sv, Sv, bc4(G(56)), ALU.mult)
            tt("dve", Sv, v4(psu[:, :]), Sv, ALU.add)
            cp("act", Sb[:, H0:H0 + 4, :], Sv)
        else:
            pu = bank("G")
            for hh in range(4):
                mm(pu[:, hh * 128:(hh + 1) * 128], vbt[:, hh, :], TTb[:, hh, :])
            cp("act", f4(uT[:, :, :]), pu[:, :])
            pws = bank("G")
            pqs = bank("G")
            for sg in range(8):
                Sf_, Sb_ = S_all[:, sg % 2, :, :], Ssb[sg % 2]
                for si_ in range(2):
                    dma("sp", Sf_[:, si_ * 4:si_ * 4 + 4, :],
                        i_sgdn[l, sg * 2 + si_, H0:H0 + 4, :, :].rearrange("h d v -> d h v"))
                cp("act" if "s" in POOLMIX else "pool", Sb_[:, :, :], Sf_[:, :, :])
                for si in range(2):
                    s = sg * 2 + si
                    cols = slice(s * 8, s * 8 + 8)
                    for hh in range(4):
                        mm(pws[:, hh * 128 + s * 8:hh * 128 + s * 8 + 8], Sb_[:, si * 4 + hh, :], wTb[:, hh, cols],
                           start=True, stop=True, skip=True)
                        mm(pqs[:, hh * 128 + s * 8:hh * 128 + s * 8 + 8], Sb_[:, si * 4 + hh, :],
                           arena[:, ab + hh, b * 128 + s * 8:b * 128 + s * 8 + 8], start=True, stop=True, skip=True)
            tt("dve", f4(vnT[:, :, :]), f4(uT[:, :, :]), pws[:, :], ALU.subtract)
            cp("act", f4(qsT[:, :, :]), pqs[:, :])
            pt3 = bank("G")
            pt3b = pt3[:, :].bitcast(BF16)
            for hh in range(4):
                tr(pt3b[:, hh * 128:(hh + 1) * 128], vnT[:, hh, :], IDb)
            cp("act", f4(vnew[:, :, :]), pt3b[:, 0:512])
            pt4 = bank("G")
            pt4b = pt4[:, :].bitcast(BF16)
            for hh in range(4):
                tr(pt4b[:, hh * 128:(hh + 1) * 128], qsT[:, hh, :], IDb)
            tt("dve", tq[:, :, :], v4(pt4b[:, 0:512]), bc4(G(32)), ALU.mult)
            pin = bank("G")
            for hh in range(4):
                mm(pin[:, hh * 128:(hh + 1) * 128], qkT[:, hh, :], vnew[:, hh, :])
            tt("dve", f4(o_t[:, :, :]), pin[:, :], f4(tq[:, :, :]), ALU.add)
            for sg in range(8):
                Sf_ = S_all[:, sg % 2, :, :]
                for si_ in range(2):
                    dma("sp", Sf_[:, si_ * 4:si_ * 4 + 4, :],
                        i_sgdn[l, sg * 2 + si_, H0:H0 + 4, :, :].rearrange("h d v -> d h v"))
                for si in range(2):
                    s = sg * 2 + si
                    km = kdm[s % 2]
                    ts("dve", km[:, :, :], kdec[:, :, :], C("SM", 16)[:, s:s + 1], ALU.mult)
                    psu = bank("G")
                    for hh in range(4):
                        mm(psu[:, hh * 128:(hh + 1) * 128], km[:, hh, :], vnew[:, hh, :])
                    sn = S_all[:, 2, (s % 2) * 4:(s % 2) * 4 + 4, :]
                    tt("dve" if "t" in POOLMIX else "pool", sn[:, :, :], Sf_[:, si * 4:si * 4 + 4, :], bc4(egls[:, s, H0:H0 + 4]), ALU.mult)
                    tt("dve", sn[:, :, :], v4(psu[:, :]), sn[:, :, :], ALU.add)
                    dst = o_gdns[l, s, H0:H0 + 4, :, :].rearrange("h d v -> d h v")
                    dma("sp", dst, sn[:, :, :])
        tt("pool", osq[:, :, :], o_t[:, :, :], o_t[:, :, :], ALU.mult)
        red(osm[:, 0:4], osq[:, :, :], ALU.add)
        rsqrt_ln(osm[:, 4:8], osm[:, 0:4], 1.0 / 128.0, EPS)
        tt("dve", on_b[:, :, :], o_t[:, :, :], bc4(osm[:, 4:8]), ALU.mult)
        po = bank("G")
        pob = po[:, :].bitcast(BF16)
        for hh in range(4):
            tr(pob[:, hh * 128:(hh + 1) * 128], on_b[:, hh, :], IDb)
        stt(oaT[:, H0:H0 + 4, blk], v4(pob[:, 0:512]), spd[:, l, 0:1], arena[:, ab + 12:ab + 16, blk], ALU.mult, ALU.mult)

    def swa_unit(tile, l, qb, kvh):
        samp = tile["kind"] == "s"
        qa = qaT
        qs = slice(qb * 128, (qb + 1) * 128)
        heads = [4 * kvh + 0, 4 * kvh + 2, 4 * kvh + 1, 4 * kvh + 3]
        kbs = []
        if not samp and not (tile["first"] and qb == 0):
            kbs.append((qb, C("DO")))
        kbs.append((qb + 1, C("DN_S") if samp else C("DD")))
        pts = []
        for i, (kb, Dm) in enumerate(kbs):
            pse, pso = bank(), bank()
            ks = slice(kb * 128, (kb + 1) * 128)
            mm(pse[:, 0:256], kdup[0:64, kvh, ks], qa[0:64, 2 * kvh:2 * kvh + 2, qs])
            mm(pso[:, 0:256], kdup[64:128, kvh, ks], qa[64:128, 2 * kvh:2 * kvh + 2, qs])
            s_ = sc[i]
            for j in range(4):
                src = (pse if j < 2 else pso)[:, (j % 2) * 128:(j % 2 + 1) * 128]
                stt(s_[:, j, :], Dm, -SLOPES[heads[j]], src, ALU.mult, ALU.add)
            act(PTa[i][:, :, :], s_[:, :, :], AF.Exp)
            pts.append((PTa[i], vext[:, kb, kvh * 64:(kvh + 1) * 64]))
        if samp:
            dma("pool", kcb[:, :, :], i_kcT[l, kvh])
            dma("pool", vcb[:, :, :], i_vc[l, :, :, kvh * 64:(kvh + 1) * 64].rearrange("s k d -> k s d"))
            pse, pso = bank(), bank()
            psve = pse[:, 0:256].rearrange("p (s j t) -> p s j t", s=16, j=2)
            psvo = pso[:, 0:256].rearrange("p (s j t) -> p s j t", s=16, j=2)
            for s in range(16):
                cols = slice(s * 8, s * 8 + 8)
                mm(psve[:, s, :, :], kcb[0:64, s, :], qa[0:64, 2 * kvh:2 * kvh + 2, cols])
                mm(psvo[:, s, :, :], kcb[64:128, s, :], qa[64:128, 2 * kvh:2 * kvh + 2, cols])
            for j in range(4):
                src = (psve if j < 2 else psvo)[:, :, j % 2, :]
                stt(scs[:, :, j, :], C("DC", 8).unsqueeze(1).broadcast_to([128, 16, 8]), -SLOPES[heads[j]],
                    src, ALU.mult, ALU.add)
            act(PTc[:, :, :, 0:8], scs[:, :, :, :], AF.Exp)
        nd = bank()
        ndv = nd[:, :].rearrange("p (a c q) -> p a c q", a=2, c=2)
        for a in range(2):
            for par in range(2):
                prt = slice(par * 64, par * 64 + 64)
                for i, (P_, v_) in enumerate(pts):
                    lhs = v_ if a == 0 else ONESb[:, 0:64]
                    mm(ndv[prt, a, :, :], lhs, P_[:, 2 * par:2 * par + 2, :], start=(i == 0),
                       stop=(i == len(pts) - 1 and not samp), skip=samp)
                if samp:
                    for s in range(16):
                        lhs = vcb[:, s, :] if a == 0 else ONESb[:, 0:64]
                        for c in range(2):
                            mm(ndv[prt, a, c, s * 8:s * 8 + 8], lhs, PTc[:, s, 2 * par + c, 0:8], start=False,
                               stop=(s == 15 and c == 1), skip=True)
        for c in range(2):
            ts("dve", rden[:, c, :], ndv[:, 1, c, :], spd[:, l, 2 + 2 * kvh + c:3 + 2 * kvh + c], ALU.add)
        recip(rden[:, :, :], rden[:, :, :])
        tt("dve", obT[:, 2 * kvh:2 * kvh + 2, qs], ndv[:, 0, :, :], rden[:, :, :], ALU.mult)

    def interleave(g1, g2):
        a_, b_ = g1, g2
        while a_ is not None or b_ is not None:
            if a_ is not None:
                try:
                    next(a_)
                except StopIteration:
                    a_ = None
            if b_ is not None:
                try:
                    next(b_)
                except StopIteration:
                    b_ = None

    def drive():
        units = []
        i = 0
        while i < len(tiles):
            if i + 1 < len(tiles) and tiles[i]["kind"] == "p" and tiles[i + 1]["kind"] == "p" and not cfg.nopair:
                for l in range(DEPTH):
                    units.append((tiles[i], l))
                    units.append((tiles[i + 1], l))
                i += 2
            else:
                for l in range(DEPTH):
                    units.append((tiles[i], l))
                i += 1
        prevF, prev_tile = None, None
        for (tile, l) in units:
            if prevF is not None and prev_tile is tile:
                interleave(prevF, None)
                prevF = None
            if l == 0:
                dma("sp", hbuf[tile["idx"] % 2][:, :, 0:tile["nt"]], xT[:, :, tile["t0"]:tile["t0"] + tile["nt"]])
            interleave(gen_M(tile, l), prevF)
            prevF, prev_tile = gen_F(tile, l), tile
        interleave(prevF, None)

    n_setup = len(R.ops)
    R.dry = True
    drive()
    R.dry = False
    assert len(R.ops) == n_setup
    bank_i[0] = 0
    bank_f[0] = 0
    bank_g[0] = 0
    wstate["next_load"] = 0
    wstate["next_use"] = 0
    drive()
    assert wstate["next_use"] == len(plan)
    R.emit(st)
    return nc, st, R, dbg_out


def pack_weights(inp, depth):
    offs, TOT = group_offsets()
    out = np.zeros((depth, 128, TOT), np.float32)
    for l in range(depth):
        for (name, src, rows, cols) in weight_groups():
            W = inp[src][l]
            if isinstance(rows, tuple):
                W = W[rows[0]:rows[0] + rows[1]]
            M = W[:, cols]
            kc = M.shape[0] // 128
            o, _, ncol = offs[name]
            out[l, :, o:o + kc * ncol] = M.reshape(kc, 128, ncol).transpose(1, 0, 2).reshape(128, kc * ncol)
    return out


def pack_small(inp, depth):
    sp = np.zeros((128, depth, NSP), np.float32)
    for l in range(depth):
        sp[:, l, SP_NM:SP_NM + 8] = inp["norm_mix"][l].reshape(8, 128).T
        sp[:, l, SP_NF:SP_NF + 8] = inp["norm_ffn"][l].reshape(8, 128).T
        sp[:, l, SP_NP:SP_NP + 8] = inp["norm_ple"][l].reshape(8, 128).T
        cw = inp["conv_w"][l]
        sp[:, l, SP_CW:SP_CW + 96] = cw.reshape(4, 24, 128).transpose(2, 1, 0).reshape(128, 96)
        sp[:, l, SP_GN] = inp["gdn_norm"][l]
        sp[:, l, SP_QN] = np.tile(inp["q_norm"][l], 2)
        sp[:, l, SP_KN] = np.tile(inp["k_norm"][l], 2)
        sk = inp["attn_sinks"][l]
        for c in range(8):
            sp[0:64, l, SP_SK + c] = sk[2 * c]
            sp[64:128, l, SP_SK + c] = sk[2 * c + 1]
        sp[:, l, SP_AL:SP_AL + 8] = inp["a_log"][l][None, :]
        sp[:, l, SP_DT:SP_DT + 8] = inp["dt_bias"][l][None, :]
    return sp


def fm(x):
    T, Dm = x.shape
    return np.ascontiguousarray(x.reshape(T, Dm // 128, 128).transpose(2, 1, 0))


def unfm(y):
    p, k, T = y.shape
    return np.ascontiguousarray(y.transpose(2, 1, 0).reshape(T, k * 128))


def make_in_maps(inp, cfg, ncores):
    depth = cfg.depth
    wsrc = pack_weights(inp, depth)
    spar = pack_small(inp, depth)
    cst = make_consts()
    maps = []
    for c in range(ncores):
        xs = [inp["x_prompt"][c, :cfg.seq]]
        ps = [inp["p_prompt"][:depth, c, :cfg.seq]]
        if cfg.sample:
            xs.append(inp["x_sample"][16 * c:16 * c + 16].reshape(128, D_MODEL))
            ps.append(inp["p_sample"][:depth, 16 * c:16 * c + 16].reshape(depth, 128, 256))
        x = np.concatenate(xs, 0)
        p = np.concatenate(ps, 1)
        m = {"xT": fm(x), "pT": np.stack([fm(p[l]) for l in range(depth)]), "wsrc": wsrc, "spar": spar,
             "cst": cst[0], "cstm": cst[1]}
        if cfg.sample:
            sl = slice(16 * c, 16 * c + 16)
            cc = inp["cache_conv"][:depth, sl]
            m["i_cconv"] = np.ascontiguousarray(cc.reshape(depth, 16, 3, 24, 128).transpose(0, 4, 3, 1, 2))
            m["i_sgdn"] = np.ascontiguousarray(inp["state_gdn"][:depth, sl])
            kc = inp["cache_swa_k"][:depth, sl]
            kt = kc.transpose(0, 3, 4, 1, 2)
            m["i_kcT"] = np.ascontiguousarray(np.concatenate([kt, kt], axis=2))
            m["i_vc"] = np.ascontiguousarray(inp["cache_swa_v"][:depth, sl].reshape(depth, 16, 128, 256))
            m["i_kc"] = np.ascontiguousarray(kc.reshape(depth, 16, 128, 256))
        maps.append(m)
    return maps


def assemble(results, cfg, ncores):
    depth, seq = cfg.depth, cfg.seq
    yp = np.zeros((ncores, seq, D_MODEL), np.float32)
    convp = np.zeros((depth, ncores, 3, 3072), np.float32)
    gdnp = np.zeros((depth, ncores, 8, 128, 128), np.float32)
    kp = np.zeros((depth, ncores, 128, 4, 64), np.float32)
    vp = np.zeros((depth, ncores, 128, 4, 64), np.float32)
    if cfg.sample:
        ys = np.zeros((ncores * 16, 8, D_MODEL), np.float32)
        convs = np.zeros((depth, ncores * 16, 3, 3072), np.float32)
        gdns = np.zeros((depth, ncores * 16, 8, 128, 128), np.float32)
        ks = np.zeros((depth, ncores * 16, 128, 4, 64), np.float32)
        vs = np.zeros((depth, ncores * 16, 128, 4, 64), np.float32)
    for c in range(ncores):
        r = results[c]
        y = unfm(r["yT"])
        yp[c] = y[:seq]
        convp[:, c] = r["o_convp"].transpose(0, 3, 2, 1).reshape(depth, 3, 3072)
        gdnp[:, c] = r["o_gdnp"].transpose(0, 2, 1, 3)
        kp[:, c] = r["o_kp"][:, 0:64].transpose(0, 3, 2, 1)
        vp[:, c] = r["o_vp"].reshape(depth, 128, 4, 64)
        if cfg.sample:
            sl = slice(16 * c, 16 * c + 16)
            ys[sl] = y[seq:].reshape(16, 8, D_MODEL)
            convs[:, sl] = r["o_convs"].transpose(0, 3, 4, 2, 1).reshape(depth, 16, 3, 3072)
            gdns[:, sl] = r["o_gdns"]
            ks[:, sl, 0:120] = r["o_kcopy"].reshape(depth, 16, 120, 4, 64)
            vs[:, sl, 0:120] = r["o_vcopy"].reshape(depth, 16, 120, 4, 64)
            kn = r["o_ksn"][:, 0:64].transpose(0, 3, 2, 1)
            ks[:, sl, 120:128] = kn.reshape(depth, 16, 8, 4, 64)
            vs[:, sl, 120:128] = r["o_vsn"].reshape(depth, 16, 8, 4, 64)
    if cfg.sample:
        return (yp, ys, convp, gdnp, kp, vp, convs, gdns, ks, vs)
    return (yp, convp, gdnp, kp, vp)


_CACHE = {}


def kernel(**inputs):
    inp = {k: np.asarray(v) for k, v in inputs.items()}
    cfg = Cfg()
    if "prog" not in _CACHE:
        _CACHE["prog"] = build(cfg)
    nc, st, R, _ = _CACHE["prog"]
    maps = make_in_maps(inp, cfg, 8)
    res = run_bass_kernel_spmd(nc, maps, core_ids=list(range(8)))
    return assemble(res.results, cfg, 8)
```
